# Optimizing a Trainium2 kernel written in Bass

```python
import math
import jax, jax.numpy as jnp
from jax import lax
import numpy as np

D_MODEL = 2048
BATCH = 2
SEQ = 8192
DEPTH = 2

PLE_DIM = 256
D_FF = 4 * D_MODEL
D_MIX = D_MODEL
GROUP_WIDTH = D_MIX // 4
NORM_EPS = 1e-6

A_HEADS = 4
A_DV = GROUP_WIDTH // A_HEADS
A_DH = A_DV // 2
A_QK = A_HEADS * 2 * A_DH
A_IN = 2 * A_QK + A_HEADS * A_DV
Q_BLOCK = 128
NUM_BUCKETS = 32
MAX_DISTANCE = 128

H_HEADS = 4
H_DK = GROUP_WIDTH // H_HEADS
H_DV = H_DK
H_CHUNK = 64
H_IN = 5 * GROUP_WIDTH

POOL_WINDOWS = (2, 4, 8, 16)
C_GROUP = GROUP_WIDTH // len(POOL_WINDOWS)
C_IN = GROUP_WIDTH

R_HEADS = 8
R_DH = GROUP_WIDTH // R_HEADS
R_DECAY_LORA = 64
R_ICLR_LORA = 64
R_GATE_LORA = 128
R_SPLITS = (GROUP_WIDTH, GROUP_WIDTH, GROUP_WIDTH, R_DECAY_LORA, R_DECAY_LORA, R_ICLR_LORA, R_ICLR_LORA, R_GATE_LORA)
R_IN = sum(R_SPLITS)
R_DECAY_SCALE = math.exp(-0.5)
R_GN_EPS = 64e-5

GROUP_SPLITS = (A_IN, H_IN, C_IN, R_IN)
N_IN = sum(GROUP_SPLITS)

kernel_name = 'hybrid_parallel_heads_bidir_encoder'


def split_last(t, sizes):
    return jnp.split(t, [int(o) for o in np.cumsum(sizes)[:-1]], axis=-1)


def rmsnorm(x, g, eps=NORM_EPS):
    xf = x.astype(jnp.float32)
    y = xf * lax.rsqrt(jnp.mean(xf * xf, axis=-1, keepdims=True) + eps)
    return (y * g.astype(jnp.float32)).astype(x.dtype)


def t5_bucket(rel):
    half = NUM_BUCKETS // 2
    max_exact = half // 2
    n = jnp.abs(rel)
    nf = jnp.maximum(n, 1).astype(jnp.float32)
    large = max_exact + (jnp.log(nf / max_exact) / math.log(MAX_DISTANCE / max_exact) * (half - max_exact)).astype(jnp.int32)
    large = jnp.minimum(large, half - 1)
    return jnp.where(rel > 0, half, 0) + jnp.where(n < max_exact, n, large)


def diff_attention(za, rel_bias, q_gain, k_gain, lam_p, subln_g, layer_idx):
    b_, s_, _ = za.shape
    nb = s_ // Q_BLOCK
    q, k, v = split_last(za, (A_QK, A_QK, A_HEADS * A_DV))
    q = rmsnorm(q.reshape(b_, s_, A_HEADS, 2, A_DH), q_gain)
    k = rmsnorm(k.reshape(b_, s_, A_HEADS, 2, A_DH), k_gain)
    v_t = v.reshape(b_, s_, A_HEADS, A_DV).transpose(0, 2, 1, 3)
    k_t = k.transpose(0, 2, 3, 1, 4)
    q_blocks = q.reshape(b_, nb, Q_BLOCK, A_HEADS, 2, A_DH).transpose(1, 0, 3, 4, 2, 5)
    lam_init = 0.8 - 0.6 * math.exp(-0.3 * layer_idx)
    lp = lam_p.astype(jnp.float32)
    lam = jnp.exp(jnp.sum(lp[0] * lp[1])) - jnp.exp(jnp.sum(lp[2] * lp[3])) + lam_init
    scale = A_DH ** -0.5
    k_pos = jnp.arange(s_, dtype=jnp.int32)
    table = rel_bias.astype(jnp.float32)

    def block(args):
        q_blk, bi = args
        q_pos = bi * Q_BLOCK + jnp.arange(Q_BLOCK, dtype=jnp.int32)
        bias = jnp.moveaxis(table[t5_bucket(k_pos[None, :] - q_pos[:, None])], -1, 0)
        logits = jnp.einsum('bhmqd,bhmkd->bhmqk', q_blk, k_t).astype(jnp.float32) * scale + bias[None, :, None]
        probs = jax.nn.softmax(logits, axis=-1).astype(v_t.dtype)
        o = jnp.einsum('bhmqk,bhkv->bhmqv', probs, v_t)
        return o[:, :, 0] - lam.astype(o.dtype) * o[:, :, 1]

    o = lax.map(block, (q_blocks, jnp.arange(nb, dtype=jnp.int32)))
    o = o.transpose(1, 0, 3, 2, 4).reshape(b_, s_, A_HEADS, A_DV)
    o = rmsnorm(o, subln_g) * (1.0 - lam_init)
    return o.reshape(b_, s_, A_HEADS * A_DV)


def hgrn2(zh, lb, o_gain):
    b_, s_, _ = zh.shape
    nc = s_ // H_CHUNK
    hq, hf_fwd, hf_bwd, hi, hg = split_last(zh, (GROUP_WIDTH,) * 5)
    lb = lb.astype(jnp.float32)
    log_lb = jnp.log(lb)
    log_ub = jnp.log1p(-lb)

    def gates(zf):
        zf = zf.astype(jnp.float32)
        log_f = jnp.logaddexp(log_lb, log_ub + jax.nn.log_sigmoid(zf))
        k_in = (1.0 - lb) * jax.nn.sigmoid(-zf)
        return log_f, k_in

    lf_f, k_f = gates(hf_fwd)
    lf_b, k_b = gates(hf_bwd)
    q = jax.nn.silu(hq.astype(jnp.float32))
    v = hi.astype(jnp.float32)

    def to_chunks(t_fwd, t_bwd, d):
        t = jnp.stack([t_fwd, jnp.flip(t_bwd, axis=1)])
        return t.reshape(2, b_, nc, H_CHUNK, H_HEADS, d).transpose(2, 0, 1, 4, 3, 5)

    xs = (to_chunks(q, q, H_DK), to_chunks(k_f, k_b, H_DK), to_chunks(v, v, H_DV), to_chunks(lf_f, lf_b, H_DK))
    causal = jnp.tril(jnp.ones((H_CHUNK, H_CHUNK), dtype=bool))

    def step(state, inp):
        qc, kc, vc, lfc = inp
        bcum = jnp.cumsum(lfc, axis=-2)
        o_inter = jnp.einsum('zbhtd,zbhdv->zbhtv', qc * jnp.exp(bcum), state)
        diff = bcum[..., :, None, :] - bcum[..., None, :, :]
        decay = jnp.exp(jnp.where(causal[:, :, None], diff, -jnp.inf))
        att = jnp.einsum('zbhtd,zbhsd,zbhtsd->zbhts', qc, kc, decay)
        o = o_inter + jnp.einsum('zbhts,zbhsv->zbhtv', att, vc)
        b_last = bcum[..., -1:, :]
        state = state * jnp.exp(b_last[..., 0, :])[..., None] + jnp.einsum('zbhsd,zbhsv->zbhdv', kc * jnp.exp(b_last - bcum), vc)
        return state, o

    s0 = jnp.zeros((2, b_, H_HEADS, H_DK, H_DV), jnp.float32)
    _, outs = lax.scan(step, s0, xs)
    outs = outs.transpose(1, 2, 0, 4, 3, 5).reshape(2, b_, s_, H_HEADS, H_DV)
    o = outs[0] + jnp.flip(outs[1], axis=1)
    o = rmsnorm(o, o_gain) * jax.nn.silu(hg.astype(jnp.float32)).reshape(b_, s_, H_HEADS, H_DV)
    return o.reshape(b_, s_, GROUP_WIDTH).astype(zh.dtype)


def pool_mixer(zc, w_grp, b_grp, ls):
    b_, s_, _ = zc.shape
    zf = zc.astype(jnp.float32)
    csum = jnp.concatenate([jnp.zeros((b_, 1, C_IN), jnp.float32), jnp.cumsum(zf, axis=1)], axis=1)
    t = jnp.arange(s_)
    outs = []
    for gi, w in enumerate(POOL_WINDOWS):
        sl = slice(gi * C_GROUP, (gi + 1) * C_GROUP)
        lo = jnp.clip(t - w // 2, 0, s_ - 1)
        hi = jnp.clip(t + (w - w // 2 - 1), 0, s_ - 1)
        cg = csum[..., sl]
        mean = (jnp.take(cg, hi + 1, axis=1) - jnp.take(cg, lo, axis=1)) / (hi - lo + 1).astype(jnp.float32)[None, :, None]
        outs.append(jnp.einsum('bsc,cd->bsd', mean - zf[..., sl], w_grp[gi].astype(jnp.float32)))
    out = (jnp.concatenate(outs, axis=-1) + b_grp.astype(jnp.float32)) * ls.astype(jnp.float32)
    return out.astype(zc.dtype)


def rwkv7(zr, mu, w0, w2, a0, a2, g2, k_k, k_a, r_k, gn_g, gn_b):
    b_, s_, _ = zr.shape
    f32 = jnp.float32
    zf = zr.astype(f32)
    mu = mu.astype(f32)
    z_prev = jnp.pad(zf[:, :-1], ((0, 0), (1, 0), (0, 0)))
    z_next = jnp.pad(zf[:, 1:], ((0, 0), (0, 1), (0, 0)))
    z = zf + mu[0] * (z_prev - zf) + mu[1] * (z_next - zf)
    r, k, v, wl_f, wl_b, al_f, al_b, gl = split_last(z, R_SPLITS)
    w0, w2, a0, a2 = w0.astype(f32), w2.astype(f32), a0.astype(f32), a2.astype(f32)

    def decay(wl, d):
        return jnp.exp(-R_DECAY_SCALE * jax.nn.sigmoid(w0[d] + jnp.tanh(wl) @ w2[d]))

    def iclr(al, d):
        return jax.nn.sigmoid(a0[d] + al @ a2[d])

    def heads(t):
        return t.reshape(b_, s_, R_HEADS, R_DH)

    w_f, w_b = decay(wl_f, 0), decay(wl_b, 1)
    a_f, a_b = iclr(al_f, 0), iclr(al_b, 1)
    g = jax.nn.sigmoid(gl) @ g2.astype(f32)
    k_k, k_a, r_k = k_k.astype(f32), k_a.astype(f32), r_k.astype(f32)
    kk = heads(k * k_k)
    kk = kk * lax.rsqrt(jnp.sum(kk * kk, axis=-1, keepdims=True) + 1e-12)
    ke_f = k * (1.0 + (a_f - 1.0) * k_a)
    ke_b = k * (1.0 + (a_b - 1.0) * k_a)

    def to_time(t_fwd, t_bwd):
        t = jnp.stack([t_fwd, jnp.flip(t_bwd, axis=1)])
        return t.transpose(2, 0, 1, 3, 4)

    rh, vh = heads(r), heads(v)
    xs = (to_time(rh, rh), to_time(heads(w_f), heads(w_b)), to_time(kk, kk),
          to_time(kk * heads(a_f), kk * heads(a_b)), to_time(vh, vh), to_time(heads(ke_f), heads(ke_b)))

    def step(state, inp):
        r_t, w_t, kk_t, b_t, v_t, k_t = inp
        sa = jnp.einsum('zbhvk,zbhk->zbhv', state, kk_t)
        state = state * w_t[..., None, :] - sa[..., None] * b_t[..., None, :] + v_t[..., None] * k_t[..., None, :]
        y = jnp.einsum('zbhvk,zbhk->zbhv', state, r_t)
        return state, y

    s0 = jnp.zeros((2, b_, R_HEADS, R_DH, R_DH), f32)
    _, ys = lax.scan(step, s0, xs)
    y = (ys[:, 0] + jnp.flip(ys[:, 1], axis=0)).transpose(1, 0, 2, 3)
    mean = jnp.mean(y, axis=-1, keepdims=True)
    var = jnp.mean(jnp.square(y - mean), axis=-1, keepdims=True)
    y = ((y - mean) * lax.rsqrt(var + R_GN_EPS)).reshape(b_, s_, GROUP_WIDTH) * gn_g.astype(f32) + gn_b.astype(f32)
    bonus = jnp.sum(heads(r * (0.5 * (ke_f + ke_b)) * r_k), axis=-1, keepdims=True) * vh
    out = (y + bonus.reshape(b_, s_, GROUP_WIDTH)) * g
    return out.astype(zr.dtype)


def setup_inputs(seed: int = 0) -> dict:
    key = jax.random.key(seed)
    ks = list(jax.random.split(key, 40))

    def nrm(idx, shape, scale):
        return jax.random.normal(ks[idx], shape, jnp.float32) * scale

    def gain(idx, shape):
        return 1.0 + 0.1 * jax.random.normal(ks[idx], shape, jnp.float32)

    return {
        'x': nrm(0, (BATCH, SEQ, D_MODEL), 1.0),
        'p': nrm(1, (DEPTH, BATCH, SEQ, PLE_DIM), 1.0),
        'mix_norm_g': gain(2, (DEPTH, D_MODEL)),
        'w_in': nrm(3, (DEPTH, D_MODEL, N_IN), D_MODEL ** -0.5),
        'w_out': nrm(4, (DEPTH, D_MIX, D_MODEL), D_MIX ** -0.5),
        'rel_bias': nrm(5, (NUM_BUCKETS, A_HEADS), 0.5),
        'a_qnorm': gain(6, (DEPTH, A_DH)),
        'a_knorm': gain(7, (DEPTH, A_DH)),
        'a_lambda': nrm(8, (DEPTH, 4, A_DH), 0.1),
        'a_subln': gain(9, (DEPTH, A_DV)),
        'h_lb_logits': nrm(10, (DEPTH, GROUP_WIDTH), 0.5),
        'h_onorm': gain(11, (DEPTH, H_DV)),
        'c_w': nrm(12, (DEPTH, len(POOL_WINDOWS), C_GROUP, C_GROUP), C_GROUP ** -0.5),
        'c_b': nrm(13, (DEPTH, C_IN), 0.02),
        'c_scale': gain(14, (DEPTH, C_IN)),
        'r_mu': jax.random.uniform(ks[15], (DEPTH, 2, R_IN), jnp.float32, 0.0, 0.5),
        'r_w0': nrm(16, (DEPTH, 2, GROUP_WIDTH), 0.5),
        'r_w2': nrm(17, (DEPTH, 2, R_DECAY_LORA, GROUP_WIDTH), 0.1),
        'r_a0': nrm(18, (DEPTH, 2, GROUP_WIDTH), 0.5),
        'r_a2': nrm(19, (DEPTH, 2, R_ICLR_LORA, GROUP_WIDTH), 0.1),
        'r_g2': nrm(20, (DEPTH, R_GATE_LORA, GROUP_WIDTH), R_GATE_LORA ** -0.5),
        'r_kk': gain(21, (DEPTH, GROUP_WIDTH)),
        'r_ka': gain(22, (DEPTH, GROUP_WIDTH)),
        'r_rk': nrm(23, (DEPTH, GROUP_WIDTH), 0.1),
        'r_gn_g': gain(24, (DEPTH, GROUP_WIDTH)),
        'r_gn_b': nrm(25, (DEPTH, GROUP_WIDTH), 0.02),
        'mlp_norm_g': gain(26, (DEPTH, D_MODEL)),
        'w_up': nrm(27, (DEPTH, D_MODEL, D_FF), D_MODEL ** -0.5),
        'w_down': nrm(28, (DEPTH, D_FF, D_MODEL), D_FF ** -0.5),
        'ple_norm_g': gain(29, (DEPTH, D_MODEL)),
        'w_ple': nrm(30, (DEPTH, PLE_DIM, D_MODEL), PLE_DIM ** -0.5),
        'w_ple_gate': nrm(31, (DEPTH, D_MODEL, D_MODEL), D_MODEL ** -0.5),
    }


def reference(x, p, mix_norm_g, w_in, w_out, rel_bias, a_qnorm, a_knorm, a_lambda, a_subln,
              h_lb_logits, h_onorm, c_w, c_b, c_scale, r_mu, r_w0, r_w2, r_a0, r_a2, r_g2,
              r_kk, r_ka, r_rk, r_gn_g, r_gn_b, mlp_norm_g, w_up, w_down, ple_norm_g, w_ple, w_ple_gate):
    lb_cum = jnp.cumsum(jax.nn.softmax(h_lb_logits.astype(jnp.float32), axis=0), axis=0)
    lbs = lb_cum - lb_cum[0:1]
    h = x
    for i in range(DEPTH):
        u = rmsnorm(h, mix_norm_g[i])
        z = u @ w_in[i]
        za, zh, zc, zr = split_last(z, GROUP_SPLITS)
        o_a = diff_attention(za, rel_bias, a_qnorm[i], a_knorm[i], a_lambda[i], a_subln[i], i)
        o_b = hgrn2(zh, lbs[i], h_onorm[i])
        o_c = pool_mixer(zc, c_w[i], c_b[i], c_scale[i])
        o_d = rwkv7(zr, r_mu[i], r_w0[i], r_w2[i], r_a0[i], r_a2[i], r_g2[i], r_kk[i], r_ka[i], r_rk[i], r_gn_g[i], r_gn_b[i])
        mix = jnp.concatenate([o_a.astype(h.dtype), o_b.astype(h.dtype), o_c.astype(h.dtype), o_d.astype(h.dtype)], axis=-1)
        h = h + mix @ w_out[i]
        u = rmsnorm(h, mlp_norm_g[i])
        h = h + jnp.square(jax.nn.relu(u @ w_up[i])) @ w_down[i]
        gate = jax.nn.sigmoid((rmsnorm(h, ple_norm_g[i]) @ w_ple_gate[i]).astype(jnp.float32))
        h = h + ((p[i] @ w_ple[i]).astype(jnp.float32) * gate).astype(h.dtype)
    return h
```

```python
import math
from contextlib import ExitStack
import numpy as np
import concourse.bass as bass
import concourse.mybir as mybir
from concourse.bass_utils import run_bass_kernel_spmd

F32 = mybir.dt.float32
BF16 = mybir.dt.bfloat16
AF = mybir.ActivationFunctionType
ALU = mybir.AluOpType
AX = mybir.AxisListType

D_MODEL = 2048
D_FF = 8192
PLE_DIM = 256
N_IN = 6528
NORM_EPS = 1e-6
NCORES = 8


class T:
    __slots__ = ("name", "w", "rs", "dsem", "dcount", "psum")

    def __init__(self, name, psum=False):
        self.name = name
        self.psum = psum
        self.w = None
        self.rs = []
        self.dsem = None
        self.dcount = 0


class Sched:
    ENG = ("pe", "dve", "act", "pool", "sp")

    def __init__(self, nc):
        self.nc = nc
        self.ops = []
        self.e = {"pe": nc.tensor, "dve": nc.vector, "act": nc.scalar, "pool": nc.gpsimd, "sp": nc.sync}

    def op(self, eng, fn, reads=(), writes=()):
        self.ops.append((eng, False, fn, tuple(reads), tuple(writes), None))

    def dma(self, eng, fn, reads=(), writes=(), parts=1):
        self.ops.append((eng, True, fn, tuple(reads), tuple(writes), parts))

    def final_wait(self, eng, tiles):
        self.ops.append((eng, False, lambda E: E.nop(), (), tuple(tiles), None))

    def emit(self, stack):
        nc = self.nc
        import os
        mx = int(os.environ.get("SCHED_MAXOPS", "0"))
        if mx:
            self.ops = self.ops[:mx]
            last = self.ops[-1]
            print("LAST OP", last[0], last[1], [t.name for t in last[3]], [t.name for t in last[4]])
        ops = self.ops
        n = len(ops)
        seq = {e: 0 for e in self.ENG}
        deps = [None] * n
        dma_tiles = []
        for i, (eng, is_dma, fn, reads, writes, parts) in enumerate(ops):
            d = []
            if is_dma:
                t0 = writes[0] if writes else reads[0]
                me = ("d", t0, t0.dcount + parts)
            else:
                seq[eng] += 1
                me = ("c", eng, seq[eng])
            for t in reads:
                if t.w is not None:
                    d.append(t.w)
                if t.psum:
                    d.extend(r for r in t.rs if r[0] == "c" and r[1] != eng)
            for t in writes:
                if t.w is not None:
                    d.append(t.w)
                d.extend(t.rs)
            if eng == "pe":
                d = [x for x in d if not (x[0] == "c" and x[1] == "pe")]
            if is_dma:
                t0.dcount += parts
                if t0 not in dma_tiles:
                    dma_tiles.append(t0)
            for t in reads:
                t.rs.append(me)
            for t in writes:
                t.w = me
                t.rs = []
            deps[i] = (d, me)
        need = {e: set() for e in self.ENG}
        for i in range(n):
            for dd in deps[i][0]:
                if dd[0] == "c":
                    need[dd[1]].add(dd[2])
        semval = {}
        for e in self.ENG:
            semval[e] = {q: k + 1 for k, q in enumerate(sorted(need[e]))}
        csem = {e: stack.enter_context(nc.semaphore("c_" + e)) for e in self.ENG if need[e]}
        for t in dma_tiles:
            t.dsem = stack.enter_context(nc.semaphore("d_" + t.name))
        waited = {e: {} for e in self.ENG}
        nwaits = 0
        for i, (eng, is_dma, fn, reads, writes, parts) in enumerate(ops):
            E = self.e[eng]
            d, me = deps[i]
            req = {}
            for dd in d:
                if dd[0] == "c":
                    key = ("c", dd[1]); val = semval[dd[1]][dd[2]]; sem = csem[dd[1]]
                else:
                    key = ("d", dd[1].name); val = 16 * dd[2]; sem = dd[1].dsem
                if req.get(key, (None, 0))[1] < val:
                    req[key] = (sem, val)
            for key, (sem, val) in req.items():
                if waited[eng].get(key, 0) >= val:
                    continue
                E.wait_ge(sem, val)
                if os.environ.get("SCHED_DBG") and i >= int(os.environ["SCHED_DBG"]):
                    print("  op", i, eng, "waits", key, val)
                waited[eng][key] = val
                nwaits += 1
            ins = fn(E)
            if is_dma:
                if not isinstance(ins, (list, tuple)):
                    ins = [ins]
                assert len(ins) == parts, (len(ins), parts)
                for x in ins:
                    x.then_inc(me[1].dsem, 16)
            elif me[2] in need[eng]:
                ins.then_inc(csem[eng], 1)
                if os.environ.get("SCHED_DBG") and i >= int(os.environ["SCHED_DBG"]):
                    print("  op", i, eng, "incs ->", semval[eng][me[2]])
        self.stats = dict(n_ops=n, n_waits=nwaits, n_sems=len(csem) + len(dma_tiles))


class Ctx:
    def __init__(self, nc, stack):
        self.nc = nc
        self.st = stack
        self.S = Sched(nc)
        self._n = 0

    def sb(self, name, shape, dt):
        t = self.st.enter_context(self.nc.sbuf_tensor("s_" + name, list(shape), dt))
        return t, T(name)

    def ps(self, name, shape, dt=F32):
        t = self.st.enter_context(self.nc.psum_tensor("p_" + name, list(shape), dt))
        return t, T(name, psum=True)


class Rot:
    def __init__(self, items):
        self.items = items
        self.i = 0

    def next(self):
        x = self.items[self.i % len(self.items)]
        self.i += 1
        return x


class Gemm:
    KC = 16
    MC = 256

    def __init__(self, cx, N, nslots=3, npsum=4):
        self.cx = cx
        self.N = N
        self.wb = Rot([cx.sb(f"wb{i}", [128, self.KC, self.MC], BF16) for i in range(nslots)])
        self.ws = Rot([cx.sb(f"ws{i}", [128, self.KC, self.MC], F32) for i in range(2)])
        self.ci = 0
        self.pp = Rot([cx.ps(f"pp{i}", [128, 512]) for i in range(npsum)])
        self.dq = 0

    def run(self, W, K, M, xb, xbT, epilogue, m_lo=0):
        S = self.cx.S
        N = self.N
        KT = K // 128
        kcs = [(k0, min(self.KC, KT - k0)) for k0 in range(0, KT, self.KC)]
        for m0 in range(0, M, self.MC):
            mw = min(self.MC, M - m0)
            nmt = mw // 128
            pst = [self.pp.next() for _ in range(nmt)]
            for ci, (k0, kn) in enumerate(kcs):
                wbt, wbT = self.wb.next()
                src = W[k0 * 128:(k0 + kn) * 128, m_lo + m0:m_lo + m0 + mw].rearrange("(kt p) m -> p kt m", p=128)
                wst, wsT = self.ws.next()
                S.dma("sp", lambda E, o=wst[:, 0:kn, 0:mw], s=src: E.dma_start(out=o, in_=s), writes=[wsT])
                if self.ci % 2 == 0:
                    S.op("dve", lambda E, o=wbt[:, 0:kn, 0:mw], i=wst[:, 0:kn, 0:mw]: E.tensor_copy(o, i), reads=[wsT], writes=[wbT])
                else:
                    S.op("act", lambda E, o=wbt[:, 0:kn, 0:mw], i=wst[:, 0:kn, 0:mw]: E.copy(o, i), reads=[wsT], writes=[wbT])
                self.ci += 1
                for j in range(nmt):
                    for kt in range(kn):
                        first = (ci == 0 and kt == 0)
                        last = (ci == len(kcs) - 1 and kt == kn - 1)
                        S.op("pe", lambda E, o=pst[j][0][:, 0:N], l=wbt[:, kt, j * 128:(j + 1) * 128],
                             r=xb[:, k0 + kt, :], a=first, b=last: E.matmul(o, l, r, start=a, stop=b),
                             reads=[wbT, xbT], writes=[pst[j][1]])
            for j in range(nmt):
                epilogue(m0 // 128 + j, pst[j][0][:, 0:N], pst[j][1])


def rms_stats(cx, hT, hTT, KT, N, ones, onesT, sq_rot, ps_rot, rstd, rstdT, dim):
    S = cx.S
    pst, psT = ps_rot.next()
    for kt in range(KT):
        sq, sqT = sq_rot.next()
        S.op("act", lambda E, o=sq[:, 0:N], i=hT[:, kt, :]: E.activation(o, i, AF.Square), reads=[hTT], writes=[sqT])
        S.op("pe", lambda E, o=pst[:, 0:N], l=ones[:, :], r=sq[:, 0:N], a=(kt == 0), b=(kt == KT - 1):
             E.matmul(o, l, r, start=a, stop=b), reads=[onesT, sqT], writes=[psT])
    S.op("dve", lambda E: E.tensor_scalar(rstd[:, 0:N], pst[:, 0:N], 1.0 / dim, NORM_EPS, ALU.mult, ALU.add),
         reads=[psT], writes=[rstdT])
    S.op("act", lambda E: E.activation(rstd[:, 0:N], rstd[:, 0:N], AF.Sqrt), reads=[rstdT], writes=[rstdT])
    S.op("dve", lambda E: E.reciprocal(rstd[:, 0:N], rstd[:, 0:N]), reads=[rstdT], writes=[rstdT])


def build_proj(ntok, n_out, N=512):
    nc = bass.Bass("TRN2", target_bir_lowering=False)
    KT = D_MODEL // 128
    hT_d = nc.dram_tensor("hT", [D_MODEL, ntok], F32, kind="ExternalInput").ap()
    w_d = nc.dram_tensor("w", [D_MODEL, n_out], F32, kind="ExternalInput").ap()
    g_d = nc.dram_tensor("g", [128, KT], F32, kind="ExternalInput").ap()
    z_d = nc.dram_tensor("zT", [n_out, ntok], F32, kind="ExternalOutput").ap()
    with ExitStack() as st:
        cx = Ctx(nc, st)
        S = cx.S
        hT, hTT = cx.sb("hT", [128, KT, N], F32)
        xb, xbT = cx.sb("xb", [128, KT, N], BF16)
        g, gT = cx.sb("g", [128, KT], F32)
        ones, onesT = cx.sb("ones", [128, 128], F32)
        rstd, rstdT = cx.sb("rstd", [128, N], F32)
        sq_rot = Rot([cx.sb(f"sq{i}", [128, N], F32) for i in range(2)])
        ob_rot = Rot([cx.sb(f"ob{i}", [128, N], F32) for i in range(3)])
        gm = Gemm(cx, N)
        ps_rot = Rot([cx.ps("pstat", [128, 512])])
        S.dma("sp", lambda E: E.dma_start(out=g[:, :], in_=g_d[:, :]), writes=[gT])
        S.op("dve", lambda E: E.memset(ones[:, :], 1.0), writes=[onesT])
        outs = []
        for t0 in range(0, ntok, N):
            S.dma("sp", lambda E, t0=t0: E.dma_start(out=hT[:, :, :], in_=hT_d[:, t0:t0 + N].rearrange("(kt p) n -> p kt n", p=128)),
                  writes=[hTT])
            rms_stats(cx, hT, hTT, KT, N, ones, onesT, sq_rot, ps_rot, rstd, rstdT, D_MODEL)
            for kt in range(KT):
                S.op("dve", lambda E, kt=kt: E.tensor_scalar(xb[:, kt, :], hT[:, kt, :], g[:, kt:kt + 1], None, ALU.mult),
                     reads=[hTT, gT], writes=[xbT])

            def epi(mt, ps, psT, t0=t0):
                ob, obT = ob_rot.next()
                S.op("dve", lambda E: E.tensor_tensor(ob[:, :], ps, rstd[:, 0:N], ALU.mult), reads=[psT, rstdT], writes=[obT])
                S.dma("sp", lambda E: E.dma_start(out=z_d[mt * 128:(mt + 1) * 128, t0:t0 + N], in_=ob[:, :]), reads=[obT])
                if obT not in outs:
                    outs.append(obT)

            gm.run(w_d, D_MODEL, n_out, xb, xbT, epi)
        S.final_wait("sp", outs)
        S.emit(st)
        nc._stats = S.stats
    return nc


def build_ffn(ntok, N=512, d_ff=D_FF):
    nc = bass.Bass("TRN2", target_bir_lowering=False)
    KT = D_MODEL // 128
    FT = d_ff // 128
    PT = PLE_DIM // 128
    hT_d = nc.dram_tensor("hT", [D_MODEL, ntok], F32, kind="ExternalInput").ap()
    mixT_d = nc.dram_tensor("mixT", [D_MODEL, ntok], F32, kind="ExternalInput").ap()
    pT_d = nc.dram_tensor("pT", [PLE_DIM, ntok], F32, kind="ExternalInput").ap()
    w_out_d = nc.dram_tensor("w_out", [D_MODEL, D_MODEL], F32, kind="ExternalInput").ap()
    w_up_d = nc.dram_tensor("w_up", [D_MODEL, d_ff], F32, kind="ExternalInput").ap()
    w_down_d = nc.dram_tensor("w_down", [d_ff, D_MODEL], F32, kind="ExternalInput").ap()
    w_gate_d = nc.dram_tensor("w_gate", [D_MODEL, D_MODEL], F32, kind="ExternalInput").ap()
    w_ple_d = nc.dram_tensor("w_ple", [PLE_DIM, D_MODEL], F32, kind="ExternalInput").ap()
    g2_d = nc.dram_tensor("g_mlp", [128, KT], F32, kind="ExternalInput").ap()
    g3_d = nc.dram_tensor("g_ple", [128, KT], F32, kind="ExternalInput").ap()
    o_d = nc.dram_tensor("oT", [D_MODEL, ntok], F32, kind="ExternalOutput").ap()
    with ExitStack() as st:
        cx = Ctx(nc, st)
        S = cx.S
        hT, hTT = cx.sb("hT", [128, KT, N], F32)
        xb, xbT = cx.sb("xb", [128, KT, N], BF16)
        aT, aTT = cx.sb("aT", [128, FT, N], BF16)
        pb, pbT = cx.sb("pb", [128, PT, N], BF16)
        g2, g2T = cx.sb("g2", [128, KT], F32)
        g3, g3T = cx.sb("g3", [128, KT], F32)
        ones, onesT = cx.sb("ones", [128, 128], F32)
        rstd, rstdT = cx.sb("rstd", [128, N], F32)
        sq_rot = Rot([cx.sb(f"sq{i}", [128, N], F32) for i in range(2)])
        tmp_rot = Rot([cx.sb(f"tmp{i}", [128, N], F32) for i in range(3)])
        gm = Gemm(cx, N)
        ps_rot = Rot([cx.ps("pstat", [128, 512])])
        pple_rot = Rot([cx.ps(f"pple{i}", [128, 512]) for i in range(2)])
        S.dma("sp", lambda E: E.dma_start(out=g2[:, :], in_=g2_d[:, :]), writes=[g2T])
        S.dma("sp", lambda E: E.dma_start(out=g3[:, :], in_=g3_d[:, :]), writes=[g3T])
        S.op("dve", lambda E: E.memset(ones[:, :], 1.0), writes=[onesT])
        for t0 in range(0, ntok, N):
            S.dma("sp", lambda E, t0=t0: E.dma_start(out=hT[:, :, :], in_=hT_d[:, t0:t0 + N].rearrange("(kt p) n -> p kt n", p=128)),
                  writes=[hTT])
            S.dma("pool", lambda E, t0=t0: E.dma_start(out=xb[:, :, :], in_=mixT_d[:, t0:t0 + N].rearrange("(kt p) n -> p kt n", p=128)),
                  writes=[xbT])
            S.dma("pool", lambda E, t0=t0: E.dma_start(out=pb[:, :, :], in_=pT_d[:, t0:t0 + N].rearrange("(kt p) n -> p kt n", p=128)),
                  writes=[pbT])

            def epi_add(mt, ps, psT):
                S.op("dve", lambda E: E.tensor_tensor(hT[:, mt, :], hT[:, mt, :], ps, ALU.add), reads=[psT, hTT], writes=[hTT])
            gm.run(w_out_d, D_MODEL, D_MODEL, xb, xbT, epi_add)

            def norm_to_xb(gv, gvT):
                rms_stats(cx, hT, hTT, KT, N, ones, onesT, sq_rot, ps_rot, rstd, rstdT, D_MODEL)
                for kt in range(KT):
                    S.op("dve", lambda E, kt=kt: E.scalar_tensor_tensor(out=xb[:, kt, :], in0=hT[:, kt, :], scalar=gv[:, kt:kt + 1],
                                                                        in1=rstd[:, 0:N], op0=ALU.mult, op1=ALU.mult),
                         reads=[hTT, gvT, rstdT], writes=[xbT])
            norm_to_xb(g2, g2T)

            def epi_relu2(mt, ps, psT):
                tmp, tmpT = tmp_rot.next()
                S.op("act", lambda E: E.activation(tmp[:, :], ps, AF.Relu), reads=[psT], writes=[tmpT])
                S.op("dve", lambda E: E.tensor_tensor(aT[:, mt, :], tmp[:, :], tmp[:, :], ALU.mult), reads=[tmpT], writes=[aTT])
            gm.run(w_up_d, D_MODEL, d_ff, xb, xbT, epi_relu2)

            gm.run(w_down_d, d_ff, D_MODEL, aT, aTT, epi_add)

            norm_to_xb(g3, g3T)

            def epi_gate(mt, ps, psT):
                tmp, tmpT = tmp_rot.next()
                S.op("act", lambda E: E.activation(tmp[:, :], ps, AF.Sigmoid), reads=[psT], writes=[tmpT])
                wbt, wbT = gm.wb.next()
                src = w_ple_d[:, mt * 128:(mt + 1) * 128].rearrange("(kt p) m -> p kt m", p=128)
                S.dma("pool", lambda E: E.dma_start(out=wbt[:, 0:PT, 0:128], in_=src), writes=[wbT])
                pp, ppT = pple_rot.next()
                for kt in range(PT):
                    S.op("pe", lambda E, kt=kt: E.matmul(pp[:, 0:N], wbt[:, kt, 0:128], pb[:, kt, :], start=(kt == 0), stop=(kt == PT - 1)),
                         reads=[wbT, pbT], writes=[ppT])
                S.op("dve", lambda E: E.tensor_tensor(tmp[:, :], tmp[:, :], pp[:, 0:N], ALU.mult), reads=[tmpT, ppT], writes=[tmpT])
                S.op("dve", lambda E: E.tensor_tensor(hT[:, mt, :], hT[:, mt, :], tmp[:, :], ALU.add), reads=[tmpT, hTT], writes=[hTT])
            gm.run(w_gate_d, D_MODEL, D_MODEL, xb, xbT, epi_gate)

            S.dma("sp", lambda E, t0=t0: E.dma_start(out=o_d[:, t0:t0 + N].rearrange("(kt p) n -> p kt n", p=128), in_=hT[:, :, :]),
                  reads=[hTT])
        S.final_wait("sp", [hTT])
        S.emit(st)
        nc._stats = S.stats
    return nc


def build_attn(S_len, layer_idx):
    nc = bass.Bass("TRN2", target_bir_lowering=False)
    NQ = 512
    nkt = S_len // 128
    nqb = S_len // NQ
    lam_init = 0.8 - 0.6 * math.exp(-0.3 * layer_idx)
    scale = 64 ** -0.5
    qT_d = nc.dram_tensor("qT", [128, S_len], F32, kind="ExternalInput").ap()
    kT_d = nc.dram_tensor("kT", [128, S_len], F32, kind="ExternalInput").ap()
    v_d = nc.dram_tensor("v", [S_len, 128], F32, kind="ExternalInput").ap()
    qg_d = nc.dram_tensor("qg", [128, 1], F32, kind="ExternalInput").ap()
    kg_d = nc.dram_tensor("kg", [128, 1], F32, kind="ExternalInput").ap()
    lam_d = nc.dram_tensor("lam", [128, 256], F32, kind="ExternalInput").ap()
    sub_d = nc.dram_tensor("subln", [128, 128], F32, kind="ExternalInput").ap()
    bias_d = nc.dram_tensor("biasT", [128, 3, 128], F32, kind="ExternalInput").ap()
    cfar_d = nc.dram_tensor("cfar", [2, 128, 1], F32, kind="ExternalInput").ap()
    o_d = nc.dram_tensor("o", [S_len, 128], F32, kind="ExternalOutput").ap()
    with ExitStack() as st:
        cx = Ctx(nc, st)
        S = cx.S
        qb_, qbT = cx.sb("qhat", [128, S_len], BF16)
        kb_, kbT = cx.sb("khat", [128, S_len], BF16)
        vb, vbT = cx.sb("vb", [128, nkt, 130], BF16)
        qg, qgT = cx.sb("qg", [128, 1], F32)
        kg, kgT = cx.sb("kg", [128, 1], F32)
        lam, lamT = cx.sb("lam", [128, 256], F32)
        sub, subT = cx.sb("sub", [128, 128], F32)
        biasT, biasTT = cx.sb("biasT", [128, 3, 128], F32)
        cfar, cfarT = cx.sb("cfar", [128, 2, 16], F32)
        ones, onesT = cx.sb("ones", [128, 128], F32)
        sm, smT = cx.sb("sm", [128, 8], F32)
        stg_rot = Rot([cx.sb(f"stg{i}", [128, NQ], F32) for i in range(2)])
        sq_rot = Rot([cx.sb(f"sq{i}", [128, NQ], F32) for i in range(2)])
        rs_rot = Rot([cx.sb(f"rs{i}", [128, NQ], F32) for i in range(2)])
        pT_rot = Rot([cx.sb(f"pT{i}", [128, NQ], BF16) for i in range(3)])
        tb_rot = Rot([cx.sb(f"tb{i}", [128, 128], F32) for i in range(3)])
        ep_rot = Rot([cx.sb(f"ep{i}", [128, 3, 128], F32) for i in range(2)])
        eps_rot = Rot([cx.sb(f"eps{i}", [128, 4], F32) for i in range(2)])
        ps_s = Rot([cx.ps(f"ps_s{i}", [128, 512]) for i in range(3)])
        ps_a = [cx.ps(f"ps_a{i}", [128, 512]) for i in range(3)]
        ps_n = Rot([cx.ps("ps_n", [128, 512])])

        for (t, tT, d) in ((qg, qgT, qg_d), (kg, kgT, kg_d), (lam, lamT, lam_d), (sub, subT, sub_d)):
            S.dma("sp", lambda E, t=t, d=d: E.dma_start(out=t[:, :], in_=d[:, :]), writes=[tT])
        S.dma("sp", lambda E: E.dma_start(out=biasT[:, :, :], in_=bias_d[:, :, :]), writes=[biasTT])
        S.dma("sp", lambda E: [E.dma_start(out=cfar[:, 0, 0:1], in_=cfar_d[0, :, :]), E.dma_start(out=cfar[:, 1, 0:1], in_=cfar_d[1, :, :])], writes=[cfarT], parts=2)
        S.op("dve", lambda E: E.memset(ones[:, :], 0.0), writes=[onesT])
        S.op("dve", lambda E: E.memset(ones[0:64, 0:64], 1.0), writes=[onesT])
        S.op("dve", lambda E: E.memset(ones[64:128, 64:128], 1.0), writes=[onesT])
        S.dma("pool", lambda E: E.dma_start(out=vb[:, :, 0:128], in_=v_d.rearrange("(kt p) d -> p kt d", p=128)), writes=[vbT])
        S.op("dve", lambda E: E.memset(vb[:, :, 128:130], 1.0), writes=[vbT])
        S.op("dve", lambda E: E.tensor_tensor(lam[:, 0:64], lam[:, 0:64], lam[:, 64:128], ALU.mult), reads=[lamT], writes=[lamT])
        S.op("dve", lambda E: E.tensor_tensor(lam[:, 128:192], lam[:, 128:192], lam[:, 192:256], ALU.mult), reads=[lamT], writes=[lamT])
        S.op("dve", lambda E: E.reduce_sum(sm[:, 0:1], lam[:, 0:64], AX.X), reads=[lamT], writes=[smT])
        S.op("dve", lambda E: E.reduce_sum(sm[:, 1:2], lam[:, 128:192], AX.X), reads=[lamT], writes=[smT])
        S.op("act", lambda E: E.activation(sm[:, 0:2], sm[:, 0:2], AF.Exp), reads=[smT], writes=[smT])
        S.op("dve", lambda E: E.tensor_tensor(sm[:, 2:3], sm[:, 1:2], sm[:, 0:1], ALU.subtract), reads=[smT], writes=[smT])
        S.op("dve", lambda E: E.tensor_scalar(sm[:, 2:3], sm[:, 2:3], -lam_init, None, ALU.add), reads=[smT], writes=[smT])
        S.op("dve", lambda E: E.tensor_scalar(sub[:, :], sub[:, :], 1.0 - lam_init, None, ALU.mult), reads=[subT], writes=[subT])

        for (src_d, g, gT, dst, dstT) in ((qT_d, qg, qgT, qb_, qbT), (kT_d, kg, kgT, kb_, kbT)):
            for c0 in range(0, S_len, NQ):
                stg, stgT = stg_rot.next()
                sq, sqT = sq_rot.next()
                rs, rsT = rs_rot.next()
                pn, pnT = ps_n.next()
                S.dma("sp", lambda E, stg=stg, c0=c0, src_d=src_d: E.dma_start(out=stg[:, :], in_=src_d[:, c0:c0 + NQ]), writes=[stgT])
                S.op("act", lambda E, sq=sq, stg=stg: E.activation(sq[:, :], stg[:, :], AF.Square), reads=[stgT], writes=[sqT])
                S.op("pe", lambda E, pn=pn, sq=sq: E.matmul(pn[:, :], ones[:, :], sq[:, :], start=True, stop=True), reads=[onesT, sqT], writes=[pnT])
                S.op("dve", lambda E, rs=rs, pn=pn: E.tensor_scalar(rs[:, :], pn[:, :], 1.0 / 64, NORM_EPS, ALU.mult, ALU.add), reads=[pnT], writes=[rsT])
                S.op("act", lambda E, rs=rs: E.activation(rs[:, :], rs[:, :], AF.Sqrt), reads=[rsT], writes=[rsT])
                S.op("dve", lambda E, rs=rs: E.reciprocal(rs[:, :], rs[:, :]), reads=[rsT], writes=[rsT])
                S.op("dve", lambda E, rs=rs, stg=stg, g=g, dst=dst, c0=c0: E.scalar_tensor_tensor(
                    out=dst[:, c0:c0 + NQ], in0=stg[:, :], scalar=g[:, 0:1], in1=rs[:, :], op0=ALU.mult, op1=ALU.mult),
                    reads=[stgT, gT, rsT], writes=[dstT])

        outs = []
        for qb in range(nqb):
            q0 = qb * NQ
            started = [False, False, False]

            def acc_of(m, s):
                if s < 3:
                    return m, s * 130
                return 2, m * 130

            units = [(kt, m) for kt in range(nkt) for m in range(2)]
            qk_out = {}

            def emit_qk(u):
                kt, m = units[u]
                pss, pssT = ps_s.next()
                S.op("pe", lambda E, pss=pss, m=m, kt=kt, q0=q0: E.matmul(pss[:, 0:NQ], kb_[64 * m:64 * m + 64, kt * 128:(kt + 1) * 128],
                                                                         qb_[64 * m:64 * m + 64, q0:q0 + NQ], start=True, stop=True),
                     reads=[kbT, qbT], writes=[pssT])
                qk_out[u] = (pss, pssT)

            LA = 2
            for u in range(min(LA, len(units))):
                emit_qk(u)
            for u, (kt, m) in enumerate(units):
                pss, pssT = qk_out.pop(u)
                pT, pTT = pT_rot.next()
                qs0 = 4 * qb
                if kt < qs0 - 1 or kt > qs0 + 4:
                    col = 0 if kt < qs0 else 1
                    S.op("act", lambda E, pT=pT, pss=pss, col=col: E.activation(pT[:, :], pss[:, 0:NQ], AF.Exp, bias=cfar[:, col, 0:1], scale=scale),
                         reads=[pssT, cfarT], writes=[pTT])
                else:
                    for s in range(4):
                        dlt = kt - (qs0 + s)
                        sl = slice(s * 128, (s + 1) * 128)
                        if abs(dlt) <= 1:
                            tb, tbT = tb_rot.next()
                            S.op("dve", lambda E, tb=tb, pss=pss, sl=sl, dlt=dlt: E.scalar_tensor_tensor(
                                out=tb[:, :], in0=pss[:, sl], scalar=scale, in1=biasT[:, dlt + 1, :], op0=ALU.mult, op1=ALU.add),
                                reads=[pssT, biasTT], writes=[tbT])
                            S.op("act", lambda E, pT=pT, tb=tb, sl=sl: E.activation(pT[:, sl], tb[:, :], AF.Exp), reads=[tbT], writes=[pTT])
                        else:
                            col = 0 if dlt < 0 else 1
                            S.op("act", lambda E, pT=pT, pss=pss, sl=sl, col=col: E.activation(pT[:, sl], pss[:, sl], AF.Exp, bias=cfar[:, col, 0:1], scale=scale),
                                 reads=[pssT, cfarT], writes=[pTT])
                if u + LA < len(units):
                    emit_qk(u + LA)
                for s in range(4):
                    bi, off = acc_of(m, s)
                    acc, accT = ps_a[bi]
                    stt = not started[bi]
                    started[bi] = True
                    S.op("pe", lambda E, acc=acc, off=off, pT=pT, s=s, kt=kt, stt=stt: E.matmul(
                        acc[:, off:off + 130], pT[:, s * 128:(s + 1) * 128], vb[:, kt, :], start=stt, stop=(kt == nkt - 1),
                        skip_group_check=True), reads=[pTT, vbT], writes=[accT])
            for s in range(4):
                ep, epT = ep_rot.next()
                es, esT = eps_rot.next()
                for m in range(2):
                    bi, off = acc_of(m, s)
                    acc, accT = ps_a[bi]
                    S.op("dve", lambda E, es=es, acc=acc, off=off, m=m: E.reciprocal(es[:, m:m + 1], acc[:, off + 128:off + 129]), reads=[accT], writes=[esT])
                    S.op("dve", lambda E, ep=ep, es=es, acc=acc, off=off, m=m: E.tensor_scalar(ep[:, m, :], acc[:, off:off + 128], es[:, m:m + 1], None, ALU.mult),
                         reads=[accT, esT], writes=[epT])
                S.op("dve", lambda E, ep=ep: E.scalar_tensor_tensor(out=ep[:, 2, :], in0=ep[:, 1, :], scalar=sm[:, 2:3], in1=ep[:, 0, :], op0=ALU.mult, op1=ALU.add),
                     reads=[epT, smT], writes=[epT])
                S.op("act", lambda E, ep=ep, es=es: E.activation(ep[:, 0, :], ep[:, 2, :], AF.Square, accum_out=es[:, 2:3]), reads=[epT], writes=[epT, esT])
                S.op("dve", lambda E, es=es: E.tensor_scalar(es[:, 2:3], es[:, 2:3], 1.0 / 128, NORM_EPS, ALU.mult, ALU.add), reads=[esT], writes=[esT])
                S.op("act", lambda E, es=es: E.activation(es[:, 2:3], es[:, 2:3], AF.Sqrt), reads=[esT], writes=[esT])
                S.op("dve", lambda E, es=es: E.reciprocal(es[:, 3:4], es[:, 2:3]), reads=[esT], writes=[esT])
                S.op("dve", lambda E, ep=ep, es=es: E.scalar_tensor_tensor(out=ep[:, 1, :], in0=ep[:, 2, :], scalar=es[:, 3:4], in1=sub[:, :], op0=ALU.mult, op1=ALU.mult),
                     reads=[epT, esT, subT], writes=[epT])
                S.dma("sp", lambda E, ep=ep, s=s, q0=q0: E.dma_start(out=o_d[q0 + s * 128:q0 + (s + 1) * 128, :], in_=ep[:, 1, :]), reads=[epT])
                if epT not in outs:
                    outs.append(epT)
        S.final_wait("sp", outs)
        S.emit(st)
        nc._stats = S.stats
    return nc


def t5_bucket_np(rel):
    half = 16
    max_exact = 8
    n = np.abs(rel)
    nf = np.maximum(n, 1).astype(np.float32)
    large = max_exact + (np.log(nf / max_exact) / np.float32(math.log(128 / max_exact)) * (half - max_exact)).astype(np.int32)
    large = np.minimum(large, half - 1)
    return np.where(rel > 0, half, 0) + np.where(n < max_exact, n, large)


def attn_inputs(zT_b, j, rel_bias, a_qnorm, a_knorm, a_lambda, a_subln):
    qT = np.ascontiguousarray(zT_b[j * 128:(j + 1) * 128])
    kT = np.ascontiguousarray(zT_b[512 + j * 128:512 + (j + 1) * 128])
    v = np.ascontiguousarray(zT_b[1024 + j * 128:1024 + (j + 1) * 128].T)
    kl = np.arange(128)[:, None]
    ql = np.arange(128)[None, :]
    tiles = []
    for o in (-1, 0, 1):
        rel = o * 128 + kl - ql
        tiles.append(rel_bias[t5_bucket_np(rel), j])
    biasT = np.ascontiguousarray(np.stack(tiles, axis=1).astype(np.float32))
    cfar = np.ascontiguousarray(np.broadcast_to(np.array([rel_bias[15, j], rel_bias[31, j]], np.float32)[:, None, None], (2, 128, 1)))
    return dict(qT=qT, kT=kT, v=v,
                qg=np.ascontiguousarray(np.tile(a_qnorm, 2)[:, None]), kg=np.ascontiguousarray(np.tile(a_knorm, 2)[:, None]),
                lam=np.ascontiguousarray(np.broadcast_to(a_lambda.reshape(1, 256), (128, 256))),
                subln=np.ascontiguousarray(np.broadcast_to(a_subln[None, :], (128, 128))),
                biasT=biasT, cfar=cfar)


def hgrn_masks():
    idx = np.arange(128)
    same = (idx[:, None] // 64) == (idx[None, :] // 64)
    s = idx[:, None]
    t = idx[None, :]
    out = {}
    for name, le, mid in (("f", lambda a, b: a <= b, 31), ("b", lambda a, b: a >= b, 32)):
        M = (same & le(s, t)).astype(np.float32)
        r = (idx // 64) * 64 + mid
        Mq = M - M[:, r]
        Mc = (same & ~le(s, t)).astype(np.float32)
        out[name] = np.stack([M, Mq, Mc, M], axis=1)
    return np.ascontiguousarray(np.concatenate([out["f"], out["b"]], axis=1).astype(np.float32))


def build_hgrn(S_len, layer_idx):
    nc = bass.Bass("TRN2", target_bir_lowering=False)
    nt = S_len // 128
    has_lb = layer_idx > 0
    zqT_d = nc.dram_tensor("zqT", [128, S_len], F32, kind="ExternalInput").ap()
    zgT_d = nc.dram_tensor("zgT", [128, S_len], F32, kind="ExternalInput").ap()
    zfT_d = nc.dram_tensor("zfT", [2, 128, S_len], F32, kind="ExternalInput").ap()
    zf_d = nc.dram_tensor("zf", [2, S_len, 128], F32, kind="ExternalInput").ap()
    zi_d = nc.dram_tensor("zi", [S_len, 128], F32, kind="ExternalInput").ap()
    lbr_d = nc.dram_tensor("lbrow", [128, 2, 128], F32, kind="ExternalInput").ap()
    lbc_d = nc.dram_tensor("lbcol", [128, 2], F32, kind="ExternalInput").ap()
    og_d = nc.dram_tensor("ogain", [128, 1], F32, kind="ExternalInput").ap()
    mk_d = nc.dram_tensor("masks", [128, 8, 128], F32, kind="ExternalInput").ap()
    o_d = nc.dram_tensor("oT", [128, S_len], F32, kind="ExternalOutput").ap()
    with ExitStack() as st:
        cx = Ctx(nc, st)
        S = cx.S
        oT, oTT = cx.sb("oacc", [128, S_len], F32)
        vtm, vtmT = cx.sb("vtm", [128, nt, 128], BF16)
        mk, mkT = cx.sb("mk", [128, 8, 128], F32)
        lbr, lbrT = cx.sb("lbr", [128, 2, 128], F32)
        lbc, lbcT = cx.sb("lbc", [128, 2], F32)
        og, ogT = cx.sb("og", [128, 1], F32)
        ones, onesT = cx.sb("ones", [128, 128], F32)
        St, StT = cx.sb("state", [128, 128], F32)
        Sb, SbT = cx.sb("stateb", [128, 128], BF16)
        R = {}
        for nm, dt, n in (("zf", F32, 2), ("zfT", F32, 2), ("zqT", F32, 2), ("sg", F32, 2), ("lf", F32, 2), ("ktm", F32, 2), ("e1", F32, 2),
                          ("kdec", BF16, 2), ("eq", F32, 2), ("ek", F32, 2), ("ed", F32, 3), ("qT", F32, 2), ("kT", F32, 2),
                          ("qq", BF16, 2), ("kk", BF16, 2), ("qd", BF16, 2), ("attm", BF16, 2)):
            R[nm] = Rot([cx.sb(f"{nm}{i}", [128, 128], dt) for i in range(n)])
        P = {nm: Rot([cx.ps(nm, [128, 512])]) for nm in ("p1", "p2", "p3", "p4", "p5", "p6", "pn")}
        S.dma("sp", lambda E: E.dma_start(out=mk[:, :, :], in_=mk_d[:, :, :]), writes=[mkT])
        S.dma("sp", lambda E: E.dma_start(out=lbr[:, :, :], in_=lbr_d[:, :, :]), writes=[lbrT])
        S.dma("sp", lambda E: E.dma_start(out=lbc[:, :], in_=lbc_d[:, :]), writes=[lbcT])
        S.dma("sp", lambda E: E.dma_start(out=og[:, :], in_=og_d[:, :]), writes=[ogT])
        S.dma("pool", lambda E: E.dma_start(out=vtm[:, :, :], in_=zi_d.rearrange("(kt p) d -> p kt d", p=128)), writes=[vtmT])
        S.op("dve", lambda E: E.memset(ones[:, :], 1.0), writes=[onesT])
        if has_lb:
            S.op("dve", lambda E: E.tensor_tensor(lbr[:, 0, :], lbr[:, 1, :], lbr[:, 0, :], ALU.subtract), reads=[lbrT], writes=[lbrT])
            S.op("act", lambda E: E.activation(lbr[:, 0, :], lbr[:, 0, :], AF.Sigmoid), reads=[lbrT], writes=[lbrT])
            S.op("dve", lambda E: E.tensor_scalar(lbr[:, 1, :], lbr[:, 0, :], -1.0, 1.0, ALU.mult, ALU.add), reads=[lbrT], writes=[lbrT])
            S.op("dve", lambda E: E.tensor_tensor(lbc[:, 0:1], lbc[:, 1:2], lbc[:, 0:1], ALU.subtract), reads=[lbcT], writes=[lbcT])
            S.op("act", lambda E: E.activation(lbc[:, 0:1], lbc[:, 0:1], AF.Sigmoid), reads=[lbcT], writes=[lbcT])
            S.op("dve", lambda E: E.tensor_scalar(lbc[:, 1:2], lbc[:, 0:1], -1.0, 1.0, ALU.mult, ALU.add), reads=[lbcT], writes=[lbcT])

        for z in range(2):
            mo = 4 * z
            S.op("dve", lambda E: E.memset(St[:, :], 0.0), writes=[StT])
            S.op("dve", lambda E: E.memset(Sb[:, :], 0.0), writes=[SbT])
            tiles = range(nt) if z == 0 else range(nt - 1, -1, -1)
            for i in tiles:
                c0 = i * 128
                zf, zfT_ = R["zf"].next()
                zfT, zfTT = R["zfT"].next()
                zqT, zqTT = R["zqT"].next()
                S.dma("sp", lambda E, zf=zf, z=z, c0=c0: E.dma_start(out=zf[:, :], in_=zf_d[z, c0:c0 + 128, :]), writes=[zfT_])
                S.dma("sp", lambda E, zfT=zfT, z=z, c0=c0: E.dma_start(out=zfT[:, :], in_=zfT_d[z, :, c0:c0 + 128]), writes=[zfTT])
                S.dma("sp", lambda E, zqT=zqT, c0=c0: E.dma_start(out=zqT[:, :], in_=zqT_d[:, c0:c0 + 128]), writes=[zqTT])
                sg, sgT = R["sg"].next()
                lf, lfT = R["lf"].next()
                ktm, ktmT = R["ktm"].next()
                S.op("act", lambda E, sg=sg, zf=zf: E.activation(sg[:, :], zf[:, :], AF.Sigmoid), reads=[zfT_], writes=[sgT])
                S.op("act", lambda E, ktm=ktm, zf=zf: E.activation(ktm[:, :], zf[:, :], AF.Sigmoid, scale=-1.0), reads=[zfT_], writes=[ktmT])
                if has_lb:
                    S.op("dve", lambda E, sg=sg: E.tensor_tensor(sg[:, :], sg[:, :], lbr[:, 1, :], ALU.mult), reads=[sgT, lbrT], writes=[sgT])
                    S.op("dve", lambda E, sg=sg: E.tensor_tensor(sg[:, :], sg[:, :], lbr[:, 0, :], ALU.add), reads=[sgT, lbrT], writes=[sgT])
                    S.op("dve", lambda E, ktm=ktm: E.tensor_tensor(ktm[:, :], ktm[:, :], lbr[:, 1, :], ALU.mult), reads=[ktmT, lbrT], writes=[ktmT])
                S.op("act", lambda E, lf=lf, sg=sg: E.activation(lf[:, :], sg[:, :], AF.Ln), reads=[sgT], writes=[lfT])
                p1, p1T = P["p1"].next()
                p2, p2T = P["p2"].next()
                p3, p3T = P["p3"].next()
                S.op("pe", lambda E, p1=p1, lf=lf, mo=mo: E.matmul(p1[:, 0:128], mk[:, mo + 2, :], lf[:, :], start=True, stop=True), reads=[mkT, lfT], writes=[p1T])
                S.op("pe", lambda E, p2=p2, lf=lf, mo=mo: E.matmul(p2[:, 0:128], lf[:, :], mk[:, mo + 1, :], start=True, stop=True), reads=[mkT, lfT], writes=[p2T])
                S.op("pe", lambda E, p3=p3, lf=lf, mo=mo: E.matmul(p3[:, 0:128], lf[:, :], mk[:, mo + 0, :], start=True, stop=True), reads=[mkT, lfT], writes=[p3T])
                e1, e1T = R["e1"].next()
                kdec, kdecT = R["kdec"].next()
                S.op("act", lambda E, e1=e1, p1=p1: E.activation(e1[:, :], p1[:, 0:128], AF.Exp), reads=[p1T], writes=[e1T])
                S.op("pool", lambda E, kdec=kdec, ktm=ktm, e1=e1: E.tensor_tensor(kdec[:, :], ktm[:, :], e1[:, :], ALU.mult), reads=[ktmT, e1T], writes=[kdecT])
                eq, eqT = R["eq"].next()
                ek, ekT = R["ek"].next()
                ed, edT = R["ed"].next()
                S.op("act", lambda E, eq=eq, p2=p2: E.activation(eq[:, :], p2[:, 0:128], AF.Exp), reads=[p2T], writes=[eqT])
                S.op("act", lambda E, ek=ek, p2=p2: E.activation(ek[:, :], p2[:, 0:128], AF.Exp, scale=-1.0), reads=[p2T], writes=[ekT])
                S.op("act", lambda E, ed=ed, p3=p3: E.activation(ed[:, :], p3[:, 0:128], AF.Exp), reads=[p3T], writes=[edT])
                qT, qTT = R["qT"].next()
                kT, kTT = R["kT"].next()
                S.op("act", lambda E, qT=qT, zqT=zqT: E.activation(qT[:, :], zqT[:, :], AF.Silu), reads=[zqTT], writes=[qTT])
                S.op("act", lambda E, kT=kT, zfT=zfT: E.activation(kT[:, :], zfT[:, :], AF.Sigmoid, scale=-1.0), reads=[zfTT], writes=[kTT])
                if has_lb:
                    S.op("dve", lambda E, kT=kT: E.tensor_scalar(kT[:, :], kT[:, :], lbc[:, 1:2], None, ALU.mult), reads=[kTT, lbcT], writes=[kTT])
                qq, qqT = R["qq"].next()
                kk, kkT = R["kk"].next()
                qd, qdT = R["qd"].next()
                S.op("pool", lambda E, qq=qq, qT=qT, eq=eq: E.tensor_tensor(qq[:, :], qT[:, :], eq[:, :], ALU.mult), reads=[qTT, eqT], writes=[qqT])
                S.op("dve", lambda E, kk=kk, kT=kT, ek=ek: E.tensor_tensor(kk[:, :], kT[:, :], ek[:, :], ALU.mult), reads=[kTT, ekT], writes=[kkT])
                S.op("pool", lambda E, qd=qd, qT=qT, ed=ed: E.tensor_tensor(qd[:, :], qT[:, :], ed[:, :], ALU.mult), reads=[qTT, edT], writes=[qdT])
                p4, p4T = P["p4"].next()
                S.op("pe", lambda E, p4=p4, kk=kk, qq=qq: E.matmul(p4[:, 0:128], kk[:, :], qq[:, :], start=True, stop=True), reads=[kkT, qqT], writes=[p4T])
                attm, attmT = R["attm"].next()
                S.op("dve", lambda E, attm=attm, p4=p4, mo=mo: E.tensor_tensor(attm[:, :], p4[:, 0:128], mk[:, mo + 3, :], ALU.mult), reads=[p4T, mkT], writes=[attmT])
                for c2 in ((0, 1) if z == 0 else (1, 0)):
                    r0 = 64 * c2
                    p5, p5T = P["p5"].next()
                    S.op("pe", lambda E, p5=p5, i=i, attm=attm, r0=r0: E.matmul(p5[:, 0:64], vtm[:, i, :], attm[:, r0:r0 + 64], start=True, stop=False),
                         reads=[vtmT, attmT], writes=[p5T])
                    S.op("pe", lambda E, p5=p5, qd=qd, r0=r0: E.matmul(p5[:, 0:64], Sb[:, :], qd[:, r0:r0 + 64], start=False, stop=True),
                         reads=[SbT, qdT], writes=[p5T])
                    if z == 0:
                        S.op("act", lambda E, p5=p5, c0=c0, r0=r0: E.copy(oT[:, c0 + r0:c0 + r0 + 64], p5[:, 0:64]), reads=[p5T], writes=[oTT])
                    else:
                        S.op("dve", lambda E, p5=p5, c0=c0, r0=r0: E.tensor_tensor(oT[:, c0 + r0:c0 + r0 + 64], oT[:, c0 + r0:c0 + r0 + 64], p5[:, 0:64], ALU.add),
                             reads=[p5T, oTT], writes=[oTT])
                    p6, p6T = P["p6"].next()
                    S.op("pe", lambda E, p6=p6, kdec=kdec, i=i, r0=r0: E.matmul(p6[:, 0:128], kdec[r0:r0 + 64, :], vtm[r0:r0 + 64, i, :], start=True, stop=True),
                         reads=[kdecT, vtmT], writes=[p6T])
                    dcol = r0 + 63 if z == 0 else r0
                    S.op("dve", lambda E, p6=p6, ed=ed, dcol=dcol: E.scalar_tensor_tensor(out=St[:, :], in0=St[:, :], scalar=ed[:, dcol:dcol + 1], in1=p6[:, 0:128],
                                                                                           op0=ALU.mult, op1=ALU.add), reads=[StT, edT, p6T], writes=[StT])
                    S.op("act", lambda E: E.copy(Sb[:, :], St[:, :]), reads=[StT], writes=[SbT])
        ob_rot = Rot([cx.sb(f"ob{i}", [128, 512], F32) for i in range(2)])
        zg_rot = Rot([cx.sb(f"zg{i}", [128, 512], F32) for i in range(2)])
        sq_rot = Rot([cx.sb(f"sq{i}", [128, 512], F32) for i in range(2)])
        rs_rot = Rot([cx.sb(f"rs{i}", [128, 512], F32) for i in range(2)])
        outs = []
        NB = min(512, S_len)
        for c0 in range(0, S_len, NB):
            ob, obT = ob_rot.next()
            zg, zgT = zg_rot.next()
            sq, sqT = sq_rot.next()
            rs, rsT = rs_rot.next()
            pn, pnT = P["pn"].next()
            S.dma("sp", lambda E, zg=zg, c0=c0: E.dma_start(out=zg[:, 0:NB], in_=zgT_d[:, c0:c0 + NB]), writes=[zgT])
            S.op("act", lambda E, zg=zg: E.activation(zg[:, 0:NB], zg[:, 0:NB], AF.Silu), reads=[zgT], writes=[zgT])
            S.op("act", lambda E, sq=sq, c0=c0: E.activation(sq[:, 0:NB], oT[:, c0:c0 + NB], AF.Square), reads=[oTT], writes=[sqT])
            S.op("pe", lambda E, pn=pn, sq=sq: E.matmul(pn[:, 0:NB], ones[:, :], sq[:, 0:NB], start=True, stop=True), reads=[onesT, sqT], writes=[pnT])
            S.op("dve", lambda E, rs=rs, pn=pn: E.tensor_scalar(rs[:, 0:NB], pn[:, 0:NB], 1.0 / 128, NORM_EPS, ALU.mult, ALU.add), reads=[pnT], writes=[rsT])
            S.op("act", lambda E, rs=rs: E.activation(rs[:, 0:NB], rs[:, 0:NB], AF.Sqrt), reads=[rsT], writes=[rsT])
            S.op("dve", lambda E, rs=rs: E.reciprocal(rs[:, 0:NB], rs[:, 0:NB]), reads=[rsT], writes=[rsT])
            S.op("dve", lambda E, ob=ob, rs=rs, c0=c0: E.scalar_tensor_tensor(out=ob[:, 0:NB], in0=oT[:, c0:c0 + NB], scalar=og[:, 0:1], in1=rs[:, 0:NB],
                                                                             op0=ALU.mult, op1=ALU.mult), reads=[oTT, ogT, rsT], writes=[obT])
            S.op("dve", lambda E, ob=ob, zg=zg: E.tensor_tensor(ob[:, 0:NB], ob[:, 0:NB], zg[:, 0:NB], ALU.mult), reads=[obT, zgT], writes=[obT])
            S.dma("sp", lambda E, ob=ob, c0=c0: E.dma_start(out=o_d[:, c0:c0 + NB], in_=ob[:, 0:NB]), reads=[obT])
            if obT not in outs:
                outs.append(obT)
        S.final_wait("sp", outs)
        S.emit(st)
        nc._stats = S.stats
    return nc


def hgrn_inputs(zT_b, j, h_lb_logits, h_onorm):
    base = 1536
    def rows(k):
        return zT_b[base + k * 512 + j * 128: base + k * 512 + (j + 1) * 128]
    zqT, zffT, zfbT, ziT, zgT = [rows(k) for k in range(5)]
    lb = h_lb_logits[:, j * 128:(j + 1) * 128]
    return dict(zqT=np.ascontiguousarray(zqT), zgT=np.ascontiguousarray(zgT),
                zfT=np.ascontiguousarray(np.stack([zffT, zfbT])),
                zf=np.ascontiguousarray(np.stack([zffT.T, zfbT.T])),
                zi=np.ascontiguousarray(ziT.T),
                lbrow=np.ascontiguousarray(np.broadcast_to(lb[None, :, :], (128, 2, 128))),
                lbcol=np.ascontiguousarray(lb.T),
                ogain=np.ascontiguousarray(h_onorm[:, None]),
                masks=hgrn_masks())


R_DECAY_SCALE = math.exp(-0.5)
R_GN_EPS = 64e-5


def build_rwkv(S_len, TB=256):
    nc = bass.Bass("TRN2", target_bir_lowering=False)
    nblk = S_len // TB
    n128 = S_len // 128
    n64 = S_len // 64
    zr_d = nc.dram_tensor("zr3T", [3, 768, S_len], F32, kind="ExternalInput").ap()
    vht_d = nc.dram_tensor("v3ht", [128, 3, n64, 64], F32, kind="ExternalInput").ap()
    vtm_d = nc.dram_tensor("v3tm", [128, 3, n128, 128], F32, kind="ExternalInput").ap()
    mucol_d = nc.dram_tensor("mucol", [128, 6, 2], F32, kind="ExternalInput").ap()
    muht_d = nc.dram_tensor("muht", [128, 2, 64], F32, kind="ExternalInput").ap()
    mutm_d = nc.dram_tensor("mutm", [128, 2, 128], F32, kind="ExternalInput").ap()
    w2_d = nc.dram_tensor("w2t", [128, 128], F32, kind="ExternalInput").ap()
    a2_d = nc.dram_tensor("a2t", [128, 128], F32, kind="ExternalInput").ap()
    g2_d = nc.dram_tensor("g2m", [128, 128], F32, kind="ExternalInput").ap()
    cols_d = nc.dram_tensor("cols", [8, 128, 1], F32, kind="ExternalInput").ap()
    gn_d = nc.dram_tensor("gnrow", [128, 2, 128], F32, kind="ExternalInput").ap()
    es_d = nc.dram_tensor("esel", [128, 64, 128], F32, kind="ExternalInput").ap()
    o_d = nc.dram_tensor("o", [S_len, 128], F32, kind="ExternalOutput").ap()
    with ExitStack() as st:
        cx = Ctx(nc, st)
        S = cx.S
        esel, eselT = cx.sb("esel", [128, 64, 128], BF16)
        vht, vhtT = cx.sb("vht", [128, 2, n64, 64], BF16)
        vtmp_rot = Rot([cx.sb(f"vtmp{i}", [128, 512], F32) for i in range(2)])
        vtmp2_rot = Rot([cx.sb(f"vtmpb{i}", [128, 512], F32) for i in range(2)])
        vtm, vtmT = cx.sb("vtm", [128, n128, 128], F32)
        ysb1 = cx.sb("ysb", [128, n128, 128], F32)
        mucol, mucolT = cx.sb("mucol", [128, 6, 3], F32)
        muht, muhtT = cx.sb("muht", [128, 3, 64], F32)
        mutm, mutmT = cx.sb("mutm", [128, 3, 128], F32)
        w2b, w2bT = cx.sb("w2b", [128, 128], BF16)
        a2b, a2bT = cx.sb("a2b", [128, 128], BF16)
        g2b, g2bT = cx.sb("g2b", [128, 128], BF16)
        cols, colsT = cx.sb("cols", [128, 8, 16], F32)
        gn, gnT = cx.sb("gn", [128, 2, 128], F32)
        bones, bonesT = cx.sb("bones", [128, 128], F32)
        hsel, hselT = cx.sb("hsel", [128, 2], F32)
        zwin, zwinT = cx.sb("zwin", [128, 256], BF16)
        Stt = [cx.sb(f"St{d}", [128, 64], F32) for d in range(2)]
        tmp_rot = [Rot([cx.sb(f"tmp{d}_{i}", [128, 64], F32) for i in range(2)]) for d in range(2)]
        t2_rot = [Rot([cx.sb(f"t2m{d}_{i}", [128, 2, 64], BF16) for i in range(2)]) for d in range(2)]
        sv_rot = [Rot([cx.ps(f"sv{d}_{i}", [128, 512]) for i in range(2)]) for d in range(2)]
        yps = [cx.ps(f"yps{d}", [128, 512]) for d in range(2)]
        pprep = Rot([cx.ps(f"pprep{i}", [128, 512]) for i in range(2)])

        S.dma("pool", lambda E: E.dma_start(out=esel[:, :, :], in_=es_d[:, :, :]), writes=[eselT])
        S.dma("sp", lambda E: E.dma_start(out=mucol[:, :, 0:2], in_=mucol_d[:, :, :]), writes=[mucolT])
        S.dma("sp", lambda E: E.dma_start(out=muht[:, 0:2, :], in_=muht_d[:, :, :]), writes=[muhtT])
        S.dma("sp", lambda E: E.dma_start(out=mutm[:, 0:2, :], in_=mutm_d[:, :, :]), writes=[mutmT])
        S.dma("sp", lambda E: E.dma_start(out=gn[:, :, :], in_=gn_d[:, :, :]), writes=[gnT])
        S.dma("sp", lambda E: [E.dma_start(out=cols[:, c, 0:1], in_=cols_d[c, :, :]) for c in range(8)], writes=[colsT], parts=8)
        S.dma("pool", lambda E: E.dma_start(out=w2b[:, :], in_=w2_d[:, :]), writes=[w2bT])
        S.dma("pool", lambda E: E.dma_start(out=a2b[:, :], in_=a2_d[:, :]), writes=[a2bT])
        S.dma("pool", lambda E: E.dma_start(out=g2b[:, :], in_=g2_d[:, :]), writes=[g2bT])
        S.op("dve", lambda E: E.memset(bones[:, :], 0.0), writes=[bonesT])
        S.op("dve", lambda E: E.memset(bones[0:64, 0:64], 1.0), writes=[bonesT])
        S.op("dve", lambda E: E.memset(bones[64:128, 64:128], 1.0), writes=[bonesT])
        S.op("dve", lambda E: E.memset(hsel[:, :], 0.0), writes=[hselT])
        S.op("dve", lambda E: E.memset(hsel[0:64, 0:1], 1.0), writes=[hselT])
        S.op("dve", lambda E: E.memset(hsel[64:128, 1:2], 1.0), writes=[hselT])
        S.op("dve", lambda E: E.memset(zwin[:, :], 0.0), writes=[zwinT])
        S.op("dve", lambda E: E.memset(zwin[:, 127:128], 1.0), writes=[zwinT])
        for (m, mT, sl) in ((mucol, mucolT, lambda i: mucol[:, :, i]), (muht, muhtT, lambda i: muht[:, i, :]), (mutm, mutmT, lambda i: mutm[:, i, :])):
            S.op("dve", lambda E, sl=sl: E.tensor_tensor(sl(2), sl(0), sl(1), ALU.add), reads=[mT], writes=[mT])
            S.op("dve", lambda E, sl=sl: E.tensor_scalar(sl(2), sl(2), -1.0, 1.0, ALU.mult, ALU.add), reads=[mT], writes=[mT])
        S.op("dve", lambda E: E.tensor_scalar(cols[:, 6, 0:1], cols[:, 6, 0:1], 0.5, None, ALU.mult), reads=[colsT], writes=[colsT])

        stg_rot = Rot([cx.sb(f"vstg{i}", [128, 3, 512], F32) for i in range(2)])
        for (which, dstT, src, mu_, muT_, nb_, w_) in (("ht", vhtT, vht_d, muht, muhtT, n64, 64), ("tm", vtmT, vtm_d, mutm, mutmT, n128, 128)):
            per = 512 // w_
            for b0 in range(0, nb_, per):
                bn = min(per, nb_ - b0)
                stg, stgT = stg_rot.next()
                S.dma("sp", lambda E, stg=stg, src=src, b0=b0, bn=bn, w_=w_: [E.dma_start(out=stg[:, i, 0:bn * w_].rearrange("p (b w) -> p b w", w=w_),
                                                                                            in_=src[:, i, b0:b0 + bn, :]) for i in range(3)], writes=[stgT], parts=3)
                if which == "tm":
                    acc = lambda b: vtm[:, b0 + b, :]
                    accT = vtmT
                else:
                    vt, vtT = vtmp_rot.next()
                    acc = lambda b, vt=vt, w_=w_: vt[:, b * w_:(b + 1) * w_]
                    accT = vtT
                for b in range(bn):
                    o = acc(b)
                    S.op("pool", lambda E, o=o, stg=stg, b=b, w_=w_, mu_=mu_: E.tensor_tensor(o, stg[:, 0, b * w_:(b + 1) * w_], mu_[:, 2, :], ALU.mult), reads=[stgT, muT_], writes=[accT])
                    for i in (1, 2):
                        S.op("pool", lambda E, stg=stg, b=b, w_=w_, mu_=mu_, i=i: E.tensor_tensor(stg[:, i, b * w_:(b + 1) * w_], stg[:, i, b * w_:(b + 1) * w_], mu_[:, i - 1, :], ALU.mult),
                             reads=[stgT, muT_], writes=[stgT])
                        S.op("pool", lambda E, o=o, stg=stg, b=b, w_=w_, i=i: E.tensor_tensor(o, o, stg[:, i, b * w_:(b + 1) * w_], ALU.add), reads=[stgT, accT], writes=[accT])
                if which == "ht":
                    v2, v2T = vtmp2_rot.next()
                    L = bn * w_
                    hi = vht[:, 0, b0:b0 + bn, :]
                    lo = vht[:, 1, b0:b0 + bn, :]
                    S.op("act", lambda E, hi=hi, vt=vt, L=L: E.copy(hi, vt[:, 0:L].rearrange("p (b w) -> p b w", w=64)), reads=[vtT], writes=[vhtT])
                    S.op("dve", lambda E, hi=hi, vt=vt, v2=v2, L=L: E.tensor_tensor(v2[:, 0:L].rearrange("p (b w) -> p b w", w=64), vt[:, 0:L].rearrange("p (b w) -> p b w", w=64), hi, ALU.subtract),
                         reads=[vtT, vhtT], writes=[v2T])
                    S.op("act", lambda E, lo=lo, v2=v2, L=L: E.copy(lo, v2[:, 0:L].rearrange("p (b w) -> p b w", w=64)), reads=[v2T], writes=[vhtT])

        def make_set(tag):
            names = ["raw", "r", "k", "wl", "al", "gl", "kx", "sq", "rn", "kk", "thb", "alb", "sgw", "a0", "a1", "w0", "w1", "nb0", "nb1", "ke0", "ke1", "t1"]
            d = {}
            for nm in names:
                if nm == "raw":
                    d[nm] = cx.sb(f"{tag}_{nm}", [128, 3, TB], F32)
                elif nm in ("thb", "alb"):
                    d[nm] = cx.sb(f"{tag}_{nm}", [128, TB], BF16)
                else:
                    d[nm] = cx.sb(f"{tag}_{nm}", [128, TB], F32)
            return d

        def shift(P_, tile_idx, out_name, c0):
            raw, rawT = P_["raw"]
            o, oT = P_[out_name]
            S.dma("sp", lambda E: [E.dma_start(out=raw[:, i, :], in_=zr_d[i, tile_idx * 128:(tile_idx + 1) * 128, c0:c0 + TB]) for i in range(3)], writes=[rawT], parts=3)
            S.op("pool", lambda E: E.tensor_scalar(o[:, :], raw[:, 0, :], mucol[:, tile_idx, 2:3], None, ALU.mult), reads=[rawT, mucolT], writes=[oT])
            for i in (1, 2):
                S.op("dve", lambda E, i=i: E.scalar_tensor_tensor(out=o[:, :], in0=raw[:, i, :], scalar=mucol[:, tile_idx, i - 1:i], in1=o[:, :], op0=ALU.mult, op1=ALU.add),
                     reads=[rawT, mucolT, oT], writes=[oT])

        def prep(P_, blk, dirs, want_gl=False):
            c0 = blk * TB
            shift(P_, 0, "r", c0)
            shift(P_, 1, "k", c0)
            shift(P_, 3, "wl", c0)
            shift(P_, 4, "al", c0)
            if want_gl:
                shift(P_, 5, "gl", c0)
            r, rT = P_["r"]; k, kT = P_["k"]; wl, wlT = P_["wl"]; al, alT = P_["al"]
            kx, kxT = P_["kx"]; sq, sqT = P_["sq"]; rn, rnT = P_["rn"]; kk, kkT = P_["kk"]
            thb, thbT = P_["thb"]; alb, albT = P_["alb"]; sgw, sgwT = P_["sgw"]; t1, t1T = P_["t1"]
            S.op("dve", lambda E: E.tensor_scalar(kx[:, :], k[:, :], cols[:, 4, 0:1], None, ALU.mult), reads=[kT, colsT], writes=[kxT])
            S.op("act", lambda E: E.activation(sq[:, :], kx[:, :], AF.Square), reads=[kxT], writes=[sqT])
            pp, ppT = pprep.next()
            S.op("pe", lambda E: E.matmul(pp[:, 0:TB], bones[:, :], sq[:, :], start=True, stop=True), reads=[bonesT, sqT], writes=[ppT])
            S.op("dve", lambda E: E.tensor_scalar(rn[:, :], pp[:, 0:TB], 1e-12, None, ALU.add), reads=[ppT], writes=[rnT])
            S.op("act", lambda E: E.activation(rn[:, :], rn[:, :], AF.Sqrt), reads=[rnT], writes=[rnT])
            S.op("dve", lambda E: E.reciprocal(rn[:, :], rn[:, :]), reads=[rnT], writes=[rnT])
            S.op("dve", lambda E: E.tensor_tensor(kk[:, :], kx[:, :], rn[:, :], ALU.mult), reads=[kxT, rnT], writes=[kkT])
            S.op("act", lambda E: E.activation(thb[:, :], wl[:, :], AF.Tanh), reads=[wlT], writes=[thbT])
            S.op("act", lambda E: E.copy(alb[:, :], al[:, :]), reads=[alT], writes=[albT])
            for d in dirs:
                a_, aT_ = P_[f"a{d}"]; w_, wT_ = P_[f"w{d}"]; nb_, nbT_ = P_[f"nb{d}"]; ke_, keT_ = P_[f"ke{d}"]
                pw, pwT = pprep.next()
                S.op("pe", lambda E, d=d, pw=pw: E.matmul(pw[:, 0:TB], w2b[64 * d:64 * d + 64, :], thb[64 * d:64 * d + 64, :], start=True, stop=True), reads=[w2bT, thbT], writes=[pwT])
                S.op("act", lambda E, d=d, pw=pw: E.activation(sgw[:, :], pw[:, 0:TB], AF.Sigmoid, bias=cols[:, d, 0:1]), reads=[pwT, colsT], writes=[sgwT])
                S.op("act", lambda E, w_=w_: E.activation(w_[:, :], sgw[:, :], AF.Exp, scale=-R_DECAY_SCALE), reads=[sgwT], writes=[wT_])
                pa, paT = pprep.next()
                S.op("pe", lambda E, d=d, pa=pa: E.matmul(pa[:, 0:TB], a2b[64 * d:64 * d + 64, :], alb[64 * d:64 * d + 64, :], start=True, stop=True), reads=[a2bT, albT], writes=[paT])
                S.op("act", lambda E, d=d, pa=pa, a_=a_: E.activation(a_[:, :], pa[:, 0:TB], AF.Sigmoid, bias=cols[:, 2 + d, 0:1]), reads=[paT, colsT], writes=[aT_])
                S.op("dve", lambda E, a_=a_: E.tensor_scalar(t1[:, :], a_[:, :], cols[:, 5, 0:1], cols[:, 5, 0:1], ALU.mult, ALU.subtract), reads=[aT_, colsT], writes=[t1T])
                S.op("dve", lambda E, ke_=ke_: E.scalar_tensor_tensor(out=ke_[:, :], in0=t1[:, :], scalar=1.0, in1=k[:, :], op0=ALU.add, op1=ALU.mult), reads=[t1T, kT], writes=[keT_])
                S.op("dve", lambda E, nb_=nb_, a_=a_: E.scalar_tensor_tensor(out=nb_[:, :], in0=kk[:, :], scalar=-1.0, in1=a_[:, :], op0=ALU.mult, op1=ALU.mult), reads=[kkT, aT_], writes=[nbT_])

        sets = [make_set("pf"), make_set("pb")]
        for d in range(2):
            S.op("dve", lambda E, d=d: E.memset(Stt[d][0][:, :], 0.0), writes=[Stt[d][1]])
            for (t2, t2T) in t2_rot[d].items:
                S.op("dve", lambda E, t2=t2: E.memset(t2[:, :, :], 0.0), writes=[t2T])

        St2 = [[Stt[d], cx.sb(f"StB{d}", [128, 64], F32)] for d in range(2)]
        kkb_rot = [Rot([cx.sb(f"kkb{d}_{i}", [128, 128], F32) for i in range(5)]) for d in range(2)]
        zwin2, zwin2T = cx.sb("zwin2", [128, 2, 256], BF16)
        S.op("dve", lambda E: E.memset(zwin2[:, :, :], 0.0), writes=[zwin2T])
        S.op("dve", lambda E: E.memset(zwin2[0:64, 0, 127:128], 1.0), writes=[zwin2T])
        S.op("dve", lambda E: E.memset(zwin2[64:128, 1, 127:128], 1.0), writes=[zwin2T])
        for d in range(2):
            S.op("dve", lambda E, d=d: E.memset(St2[d][1][0][:, :], 0.0), writes=[St2[d][1][1]])
        steps = []
        for n in range(nblk):
            for i in range(TB):
                steps.append((n, i))

        def tinfo(gi, d):
            n, i = steps[gi]
            if d == 0:
                return n * TB + i, i
            return (nblk - 1 - n) * TB + TB - 1 - i, TB - 1 - i

        sv_cur = [None, None]

        def emit_vbc(gi):
            for d in range(2):
                t, c = tinfo(gi, d)
                sv, svT = sv_rot[d].next()
                sv_cur[d] = (sv, svT)
                S.op("pe", lambda E, sv=sv, t=t: E.matmul(sv[:, 64:128], esel[:, t % 64, :], vht[:, 0, t // 64, :], start=True, stop=False), reads=[eselT, vhtT], writes=[svT])
                S.op("pe", lambda E, sv=sv, t=t: E.matmul(sv[:, 64:128], esel[:, t % 64, :], vht[:, 1, t // 64, :], start=False, stop=True), reads=[eselT, vhtT], writes=[svT])

        pending = []

        def flush_pending():
            for (d, t, c, new, newT, P_) in pending:
                r, rT = P_["r"]
                t2, t2T = t2_rot[d].next()
                S.op("act", lambda E, t2=t2, new=new, r=r, c=c: E.activation(t2[:, 0, :], new[:, :], AF.Copy, scale=r[:, c:c + 1]), reads=[newT, rT], writes=[t2T])
                tl = t % 128
                first = (tl == 0) if d == 0 else (tl == 127)
                last = (tl == 127) if d == 0 else (tl == 0)
                yp, ypT = yps[d]
                for h in range(2):
                    S.op("pe", lambda E, yp=yp, t2=t2, tl=tl, first=first, last=last, h=h: E.matmul(yp[:, 64 * h:64 * h + 64], zwin2[:, h, 127 - tl:255 - tl], t2[:, 0, :],
                                                                                                     start=(first and h == 0), stop=last, skip_group_check=True), reads=[zwin2T, t2T], writes=[ypT])
                if last:
                    ys, ysT = ysb1
                    ti = t // 128
                    if (d == 0) == (ti < (n128 + 1) // 2):
                        S.op("act", lambda E, ys=ys, yp=yp, ti=ti: E.copy(ys[:, ti, :], yp[:, 0:128]), reads=[ypT], writes=[ysT])
                    else:
                        S.op("dve", lambda E, ys=ys, yp=yp, ti=ti: E.tensor_tensor(ys[:, ti, :], ys[:, ti, :], yp[:, 0:128], ALU.add), reads=[ypT, ysT], writes=[ysT])
            pending.clear()

        kkb_q = [{}, {}]

        def emit_kkb(gi):
            for d in range(2):
                t, c = tinfo(gi, d)
                tmp, tmpT = kkb_rot[d].next()
                kk, kkT = sets[d]["kk"]
                S.op("act", lambda E, tmp=tmp, kk=kk, c=c: E.activation(tmp[:, :], bones[:, :], AF.Copy, scale=kk[:, c:c + 1]), reads=[bonesT, kkT], writes=[tmpT])
                kkb_q[d][gi] = (tmp, tmpT)

        LOOK = 2
        for gi, (n, i) in enumerate(steps):
            if i == 0:
                flush_pending()
                prep(sets[0], n, (0,))
                prep(sets[1], nblk - 1 - n, (1,))
                for la in range(min(LOOK, TB)):
                    emit_kkb(gi + la)
            if gi == 0:
                emit_vbc(0)
            par = gi % 2
            svs = list(sv_cur)
            info = []
            for d in range(2):
                t, c = tinfo(gi, d)
                cur, curT = St2[d][par]
                tmp, tmpT = kkb_q[d].pop(gi)
                info.append((t, c, cur, curT, tmp, tmpT))
            for d in range(2):
                t, c, cur, curT, tmp, tmpT = info[d]
                sv, svT = svs[d]
                S.op("pe", lambda E, sv=sv, tmp=tmp, cur=cur: E.matmul(sv[:, 0:64], tmp[:, :], cur[:, :], start=True, stop=True), reads=[tmpT, curT], writes=[svT])
            flush_pending()
            if i + LOOK < TB:
                emit_kkb(gi + LOOK)
            if gi + 1 < len(steps):
                emit_vbc(gi + 1)
            for d in range(2):
                t, c, cur, curT, tmp, tmpT = info[d]
                new, newT = St2[d][1 - par]
                sv, svT = svs[d]
                w_, wT_ = sets[d][f"w{d}"]; ke_, keT_ = sets[d][f"ke{d}"]
                S.op("dve", lambda E, new=new, cur=cur, w_=w_, c=c: E.tensor_scalar(new[:, :], cur[:, :], w_[:, c:c + 1], None, ALU.mult), reads=[curT, wT_], writes=[newT])
                S.op("dve", lambda E, new=new, sv=sv, ke_=ke_, c=c: E.scalar_tensor_tensor(out=new[:, :], in0=sv[:, 64:128], scalar=ke_[:, c:c + 1], in1=new[:, :], op0=ALU.mult, op1=ALU.add),
                     reads=[svT, keT_, newT], writes=[newT])
            for d in range(2):
                t, c, cur, curT, tmp, tmpT = info[d]
                new, newT = St2[d][1 - par]
                sv, svT = svs[d]
                nb_, nbT_ = sets[d][f"nb{d}"]
                S.op("dve", lambda E, new=new, sv=sv, nb_=nb_, c=c: E.scalar_tensor_tensor(out=new[:, :], in0=sv[:, 0:64], scalar=nb_[:, c:c + 1], in1=new[:, :], op0=ALU.mult, op1=ALU.add),
                     reads=[svT, nbT_, newT], writes=[newT])
                pending.append((d, t, c, new, newT, sets[d]))
        flush_pending()

        es = sets[0]
        ept = {nm: Rot([cx.sb(f"ep_{nm}{i}", [128, 128], dt) for i in range(2)]) for nm, dt in
               (("y", F32), ("yc", F32), ("junk", F32), ("g", F32), ("PT", F32), ("sglb", BF16), ("o", F32))}
        sm_rot = Rot([cx.sb(f"ep_sm{i}", [128, 16], F32) for i in range(2)])
        outs = []
        for blk in range(nblk):
            prep(es, blk, (0, 1), want_gl=True)
            r, rT = es["r"]; gl, glT = es["gl"]; ke0, ke0T = es["ke0"]; ke1, ke1T = es["ke1"]; t1, t1T = es["t1"]
            S.op("dve", lambda E: E.tensor_tensor(t1[:, :], ke0[:, :], ke1[:, :], ALU.add), reads=[ke0T, ke1T], writes=[t1T])
            S.op("dve", lambda E: E.scalar_tensor_tensor(out=t1[:, :], in0=t1[:, :], scalar=cols[:, 6, 0:1], in1=r[:, :], op0=ALU.mult, op1=ALU.mult), reads=[t1T, colsT, rT], writes=[t1T])
            for sub in range(TB // 128):
                ti = blk * (TB // 128) + sub
                cs = slice(sub * 128, (sub + 1) * 128)
                y, yT = ept["y"].next(); yc, ycT = ept["yc"].next(); junk, junkT = ept["junk"].next(); g, gT_ = ept["g"].next()
                sglb, sglbT = ept["sglb"].next(); ob, obT = ept["o"].next(); sm, smT = sm_rot.next()
                S.op("act", lambda E, sglb=sglb, cs=cs: E.activation(sglb[:, :], gl[:, cs], AF.Sigmoid), reads=[glT], writes=[sglbT])
                pg, pgT = pprep.next()
                S.op("pe", lambda E, pg=pg, sglb=sglb: E.matmul(pg[:, 0:128], sglb[:, :], g2b[:, :], start=True, stop=True), reads=[sglbT, g2bT], writes=[pgT])
                S.op("act", lambda E, g=g, pg=pg: E.copy(g[:, :], pg[:, 0:128]), reads=[pgT], writes=[gT_])
                pb_, pbT_ = pprep.next()
                S.op("pe", lambda E, pb_=pb_, cs=cs: E.matmul(pb_[:, 0:2], t1[:, cs], hsel[:, :], start=True, stop=True), reads=[t1T, hselT], writes=[pbT_])
                S.op("dve", lambda E, sm=sm, pb_=pb_: E.tensor_copy(sm[:, 8:10], pb_[:, 0:2]), reads=[pbT_], writes=[smT])
                S.op("dve", lambda E, y=y, ti=ti: E.tensor_copy(y[:, :], ysb1[0][:, ti, :]), reads=[ysb1[1]], writes=[yT])
                for h in range(2):
                    hs = slice(64 * h, 64 * h + 64)
                    S.op("dve", lambda E, sm=sm, y=y, hs=hs, h=h: E.reduce_sum(sm[:, h:h + 1], y[:, hs], AX.X), reads=[yT], writes=[smT])
                    S.op("dve", lambda E, sm=sm, h=h: E.tensor_scalar(sm[:, h:h + 1], sm[:, h:h + 1], 1.0 / 64, None, ALU.mult), reads=[smT], writes=[smT])
                    S.op("dve", lambda E, yc=yc, y=y, sm=sm, hs=hs, h=h: E.tensor_scalar(yc[:, hs], y[:, hs], sm[:, h:h + 1], None, ALU.subtract), reads=[yT, smT], writes=[ycT])
                    S.op("act", lambda E, junk=junk, yc=yc, sm=sm, hs=hs, h=h: E.activation(junk[:, hs], yc[:, hs], AF.Square, accum_out=sm[:, 2 + h:3 + h]), reads=[ycT], writes=[junkT, smT])
                    S.op("dve", lambda E, sm=sm, h=h: E.tensor_scalar(sm[:, 2 + h:3 + h], sm[:, 2 + h:3 + h], 1.0 / 64, R_GN_EPS, ALU.mult, ALU.add), reads=[smT], writes=[smT])
                    S.op("act", lambda E, sm=sm, h=h: E.activation(sm[:, 2 + h:3 + h], sm[:, 2 + h:3 + h], AF.Sqrt), reads=[smT], writes=[smT])
                    S.op("dve", lambda E, sm=sm, h=h: E.reciprocal(sm[:, 4 + h:5 + h], sm[:, 2 + h:3 + h]), reads=[smT], writes=[smT])
                    S.op("dve", lambda E, yc=yc, sm=sm, hs=hs, h=h: E.scalar_tensor_tensor(out=yc[:, hs], in0=yc[:, hs], scalar=sm[:, 4 + h:5 + h], in1=gn[:, 0, hs], op0=ALU.mult, op1=ALU.mult),
                         reads=[ycT, smT, gnT], writes=[ycT])
                    S.op("dve", lambda E, yc=yc, hs=hs: E.tensor_tensor(yc[:, hs], yc[:, hs], gn[:, 1, hs], ALU.add), reads=[ycT, gnT], writes=[ycT])
                    S.op("dve", lambda E, yc=yc, sm=sm, hs=hs, h=h, ti=ti: E.scalar_tensor_tensor(out=yc[:, hs], in0=vtm[:, ti, hs], scalar=sm[:, 8 + h:9 + h], in1=yc[:, hs], op0=ALU.mult, op1=ALU.add),
                         reads=[vtmT, smT, ycT], writes=[ycT])
                S.op("dve", lambda E, ob=ob, yc=yc, g=g: E.tensor_tensor(ob[:, :], yc[:, :], g[:, :], ALU.mult), reads=[ycT, gT_], writes=[obT])
                S.dma("sp", lambda E, ob=ob, ti=ti: E.dma_start(out=o_d[ti * 128:(ti + 1) * 128, :], in_=ob[:, :]), reads=[obT])
                if obT not in outs:
                    outs.append(obT)
        S.final_wait("sp", outs)
        S.emit(st)
        nc._stats = S.stats
    return nc


def rwkv_consts():
    es = np.zeros((128, 64, 128), np.float32)
    for h in range(2):
        for t in range(64):
            es[h * 64 + t, t, h * 64:(h + 1) * 64] = 1.0
    return es


def rwkv_inputs(zT_b, j, r_mu, r_w0, r_w2, r_a0, r_a2, r_g2, r_kk, r_ka, r_rk, r_gn_g, r_gn_b):
    S_len = zT_b.shape[1]
    base = 1536 + 2560 + 512
    zr = zT_b[base:base + 1920]
    my = np.arange(j * 128, (j + 1) * 128)
    rows = np.concatenate([my, 512 + my, 1024 + my, np.arange(1536, 1920)])
    cur = zr[rows]
    prev = np.zeros_like(cur); prev[:, 1:] = cur[:, :-1]
    nxt = np.zeros_like(cur); nxt[:, :-1] = cur[:, 1:]
    zr3T = np.ascontiguousarray(np.stack([cur, prev, nxt]))
    v3 = zr3T[:, 256:384, :]
    n64, n128 = S_len // 64, S_len // 128
    v3ht = np.ascontiguousarray(v3.reshape(3, 2, 64, n64, 64).transpose(1, 4, 0, 3, 2).reshape(128, 3, n64, 64))
    v3tm = np.ascontiguousarray(v3.reshape(3, 128, n128, 128).transpose(3, 0, 2, 1))
    mu = r_mu[:, rows]
    mucol = np.ascontiguousarray(mu.reshape(2, 6, 128).transpose(2, 1, 0))
    muv = mu[:, 256:384]
    muht = np.ascontiguousarray(np.repeat(muv.reshape(2, 2, 64).transpose(1, 0, 2), 64, axis=0))
    mutm = np.ascontiguousarray(np.broadcast_to(muv[None], (128, 2, 128)))
    w2t = np.ascontiguousarray(np.concatenate([r_w2[0][:, my], r_w2[1][:, my]], axis=0))
    a2t = np.ascontiguousarray(np.concatenate([r_a2[0][:, my], r_a2[1][:, my]], axis=0))
    g2m = np.ascontiguousarray(r_g2[:, my])
    cols = np.ascontiguousarray(np.stack([r_w0[0][my], r_w0[1][my], r_a0[0][my], r_a0[1][my], r_kk[my], r_ka[my], r_rk[my], np.zeros(128, np.float32)], axis=0)[:, :, None])
    gnrow = np.ascontiguousarray(np.broadcast_to(np.stack([r_gn_g[my], r_gn_b[my]])[None], (128, 2, 128)))
    return dict(zr3T=zr3T, v3ht=v3ht, v3tm=v3tm, mucol=mucol, muht=muht, mutm=mutm, w2t=w2t, a2t=a2t, g2m=g2m,
                cols=cols.astype(np.float32), gnrow=gnrow, esel=rwkv_consts())


POOL_WINDOWS = (2, 4, 8, 16)


def build_pool(ntok, NB=512):
    nc = bass.Bass("TRN2", target_bir_lowering=False)
    z_d = nc.dram_tensor("zc", [4, 128, ntok + 16], F32, kind="ExternalInput").ap()
    ci_d = nc.dram_tensor("cinv", [128, 4, ntok], F32, kind="ExternalInput").ap()
    cw_d = nc.dram_tensor("cw", [128, 4, 128], F32, kind="ExternalInput").ap()
    cc_d = nc.dram_tensor("ccol", [128, 4, 2], F32, kind="ExternalInput").ap()
    o_d = nc.dram_tensor("oT", [4, 128, ntok], F32, kind="ExternalOutput").ap()
    with ExitStack() as st:
        cx = Ctx(nc, st)
        S = cx.S
        cw, cwT = cx.sb("cw", [128, 4, 128], BF16)
        cc, ccT = cx.sb("cc", [128, 4, 2], F32)
        x_rot = Rot([cx.sb(f"x{i}", [128, NB + 16], F32) for i in range(2)])
        s_rot = Rot([cx.sb(f"s{i}", [128, NB + 16], F32) for i in range(3)])
        ci_rot = Rot([cx.sb(f"ci{i}", [128, NB], F32) for i in range(2)])
        d_rot = Rot([cx.sb(f"d{i}", [128, NB], BF16) for i in range(2)])
        ob_rot = Rot([cx.sb(f"ob{i}", [128, NB], F32) for i in range(2)])
        pp = Rot([cx.ps(f"pp{i}", [128, 512]) for i in range(2)])
        S.dma("pool", lambda E: E.dma_start(out=cw[:, :, :], in_=cw_d[:, :, :]), writes=[cwT])
        S.dma("sp", lambda E: E.dma_start(out=cc[:, :, :], in_=cc_d[:, :, :]), writes=[ccT])
        outs = []
        for g, w in enumerate(POOL_WINDOWS):
            for t0 in range(0, ntok, NB):
                x, xT = x_rot.next()
                ci, ciT = ci_rot.next()
                S.dma("sp", lambda E, x=x, g=g, t0=t0: E.dma_start(out=x[:, :], in_=z_d[g, :, t0:t0 + NB + 16]), writes=[xT])
                S.dma("sp", lambda E, ci=ci, g=g, t0=t0: E.dma_start(out=ci[:, :], in_=ci_d[:, g, t0:t0 + NB]), writes=[ciT])
                cur, curT = x, xT
                L = NB + 16
                step = 1
                while step < w:
                    nx, nxT = s_rot.next()
                    L2 = L - step
                    S.op("dve", lambda E, nx=nx, cur=cur, L2=L2, step=step: E.tensor_tensor(nx[:, 0:L2], cur[:, 0:L2], cur[:, step:step + L2], ALU.add),
                         reads=[curT], writes=[nxT])
                    cur, curT, L = nx, nxT, L2
                    step *= 2
                off = 8 - w // 2
                m, mT = s_rot.next()
                S.op("dve", lambda E, m=m, cur=cur, ci=ci, off=off: E.tensor_tensor(m[:, 0:NB], cur[:, off:off + NB], ci[:, :], ALU.mult), reads=[curT, ciT], writes=[mT])
                d, dT = d_rot.next()
                S.op("dve", lambda E, d=d, m=m, x=x: E.tensor_tensor(d[:, :], m[:, 0:NB], x[:, 8:8 + NB], ALU.subtract), reads=[mT, xT], writes=[dT])
                ps, psT = pp.next()
                S.op("pe", lambda E, ps=ps, d=d, g=g: E.matmul(ps[:, 0:NB], cw[:, g, :], d[:, :], start=True, stop=True), reads=[cwT, dT], writes=[psT])
                ob, obT = ob_rot.next()
                S.op("dve", lambda E, ob=ob, ps=ps, g=g: E.tensor_scalar(ob[:, :], ps[:, 0:NB], cc[:, g, 0:1], cc[:, g, 1:2], ALU.add, ALU.mult), reads=[psT, ccT], writes=[obT])
                S.dma("sp", lambda E, ob=ob, g=g, t0=t0: E.dma_start(out=o_d[g, :, t0:t0 + NB], in_=ob[:, :]), reads=[obT])
                if obT not in outs:
                    outs.append(obT)
        S.final_wait("sp", outs)
        S.emit(st)
        nc._stats = S.stats
    return nc


def pool_inputs(zT_b, q, ntok, S_len, c_w, c_b, c_scale):
    base = 1536 + 2560
    zc = zT_b[base:base + 512]
    pad = np.zeros((512, S_len + 16), np.float32)
    pad[:, 8:8 + S_len] = zc
    t0 = q * ntok
    zcp = np.ascontiguousarray(pad[:, t0:t0 + ntok + 16].reshape(4, 128, ntok + 16))
    t = np.arange(t0, t0 + ntok)
    cinv = np.zeros((4, ntok), np.float32)
    for g, w in enumerate(POOL_WINDOWS):
        lo = np.clip(t - w // 2, 0, S_len - 1)
        hi = np.clip(t + (w - w // 2 - 1), 0, S_len - 1)
        cinv[g] = 1.0 / (hi - lo + 1).astype(np.float32)
    cinv = np.ascontiguousarray(np.broadcast_to(cinv[None], (128, 4, ntok)))
    cw = np.ascontiguousarray(c_w.transpose(1, 0, 2))
    ccol = np.ascontiguousarray(np.stack([c_b.reshape(4, 128).T, c_scale.reshape(4, 128).T], axis=2))
    return dict(zc=zcp, cinv=cinv, cw=cw, ccol=ccol)


_PROGS = {}


def _prog(key, fn):
    if key not in _PROGS:
        _PROGS[key] = fn()
    return _PROGS[key]


def _run(nc, in_maps):
    res = run_bass_kernel_spmd(nc, in_maps, core_ids=list(range(NCORES)))
    return res.results


def _gl(g):
    return np.ascontiguousarray(np.asarray(g, np.float32).reshape(16, 128).T)


def kernel(x, p, mix_norm_g, w_in, w_out, rel_bias, a_qnorm, a_knorm, a_lambda, a_subln,
           h_lb_logits, h_onorm, c_w, c_b, c_scale, r_mu, r_w0, r_w2, r_a0, r_a2, r_g2,
           r_kk, r_ka, r_rk, r_gn_g, r_gn_b, mlp_norm_g, w_up, w_down, ple_norm_g, w_ple, w_ple_gate):
    f = lambda a: np.asarray(a, np.float32)
    x = f(x); p = f(p)
    B, S_len, D = x.shape
    depth = w_in.shape[0]
    ntok = B * S_len // NCORES
    qpb = S_len // ntok
    hT = np.ascontiguousarray(x.reshape(B * S_len, D).T)
    for li in range(depth):
        nc1 = _prog(("proj", ntok), lambda: build_proj(ntok, N_IN))
        g1 = _gl(mix_norm_g[li])
        wi = np.ascontiguousarray(f(w_in[li]))
        res = _run(nc1, [dict(hT=np.ascontiguousarray(hT[:, c * ntok:(c + 1) * ntok]), w=wi, g=g1) for c in range(NCORES)])
        zT = np.concatenate([res[c]["zT"] for c in range(NCORES)], axis=1)
        zTb = [zT[:, b * S_len:(b + 1) * S_len] for b in range(B)]
        mixT = np.empty((D, B * S_len), np.float32)
        nca = _prog(("attn", S_len, li), lambda: build_attn(S_len, li))
        res = _run(nca, [attn_inputs(zTb[c // 4], c % 4, f(rel_bias), f(a_qnorm[li]), f(a_knorm[li]), f(a_lambda[li]), f(a_subln[li])) for c in range(NCORES)])
        for c in range(NCORES):
            b, j = c // 4, c % 4
            mixT[j * 128:(j + 1) * 128, b * S_len:(b + 1) * S_len] = res[c]["o"].T
        ncb = _prog(("hgrn", S_len, min(li, 1)), lambda: build_hgrn(S_len, li))
        lbl = f(h_lb_logits)[[0, li]] if li > 0 else f(h_lb_logits)[[0, 0]]
        res = _run(ncb, [hgrn_inputs(zTb[c // 4], c % 4, lbl, f(h_onorm[li])) for c in range(NCORES)])
        for c in range(NCORES):
            b, j = c // 4, c % 4
            mixT[512 + j * 128:512 + (j + 1) * 128, b * S_len:(b + 1) * S_len] = res[c]["oT"]
        ncc = _prog(("pool", ntok), lambda: build_pool(ntok))
        res = _run(ncc, [pool_inputs(zTb[c // qpb], c % qpb, ntok, S_len, f(c_w[li]), f(c_b[li]), f(c_scale[li])) for c in range(NCORES)])
        for c in range(NCORES):
            mixT[1024:1536, c * ntok:(c + 1) * ntok] = res[c]["oT"].reshape(512, ntok)
        ncd = _prog(("rwkv", S_len), lambda: build_rwkv(S_len))
        res = _run(ncd, [rwkv_inputs(zTb[c // 4], c % 4, f(r_mu[li]), f(r_w0[li]), f(r_w2[li]), f(r_a0[li]), f(r_a2[li]), f(r_g2[li]),
                                     f(r_kk[li]), f(r_ka[li]), f(r_rk[li]), f(r_gn_g[li]), f(r_gn_b[li])) for c in range(NCORES)])
        for c in range(NCORES):
            b, j = c // 4, c % 4
            mixT[1536 + j * 128:1536 + (j + 1) * 128, b * S_len:(b + 1) * S_len] = res[c]["o"].T
        nc3 = _prog(("ffn", ntok), lambda: build_ffn(ntok))
        pT = np.ascontiguousarray(p[li].reshape(B * S_len, PLE_DIM).T)
        wts = dict(w_out=np.ascontiguousarray(f(w_out[li])), w_up=np.ascontiguousarray(f(w_up[li])), w_down=np.ascontiguousarray(f(w_down[li])),
                   w_gate=np.ascontiguousarray(f(w_ple_gate[li])), w_ple=np.ascontiguousarray(f(w_ple[li])),
                   g_mlp=_gl(mlp_norm_g[li]), g_ple=_gl(ple_norm_g[li]))
        res = _run(nc3, [dict(hT=np.ascontiguousarray(hT[:, c * ntok:(c + 1) * ntok]), mixT=np.ascontiguousarray(mixT[:, c * ntok:(c + 1) * ntok]),
                              pT=np.ascontiguousarray(pT[:, c * ntok:(c + 1) * ntok]), **wts) for c in range(NCORES)])
        hT = np.concatenate([res[c]["oT"] for c in range(NCORES)], axis=1)
    return np.ascontiguousarray(hT.T).reshape(B, S_len, D).astype(np.float32)
```

```python
import math
from contextlib import ExitStack
import numpy as np
import concourse.bass as bass
import concourse.mybir as mybir
from concourse.bass_utils import run_bass_kernel_spmd

F32 = mybir.dt.float32
BF16 = mybir.dt.bfloat16
AF = mybir.ActivationFunctionType
ALU = mybir.AluOpType
AX = mybir.AxisListType

D_MODEL = 2048
D_FF = 8192
PLE_DIM = 256
N_IN = 6528
NORM_EPS = 1e-6
NCORES = 8


class T:
    __slots__ = ("name", "w", "rs", "dsem", "dcount", "psum")

    def __init__(self, name, psum=False):
        self.name = name
        self.psum = psum
        self.w = None
        self.rs = []
        self.dsem = None
        self.dcount = 0


class Sched:
    ENG = ("pe", "dve", "act", "pool", "sp")

    def __init__(self, nc):
        self.nc = nc
        self.ops = []
        self.e = {"pe": nc.tensor, "dve": nc.vector, "act": nc.scalar, "pool": nc.gpsimd, "sp": nc.sync}

    def op(self, eng, fn, reads=(), writes=()):
        self.ops.append((eng, False, fn, tuple(reads), tuple(writes), None))

    def dma(self, eng, fn, reads=(), writes=(), parts=1):
        self.ops.append((eng, True, fn, tuple(reads), tuple(writes), parts))

    def final_wait(self, eng, tiles):
        self.ops.append((eng, False, lambda E: E.nop(), (), tuple(tiles), None))

    def emit(self, stack):
        nc = self.nc
        import os
        mx = int(os.environ.get("SCHED_MAXOPS", "0"))
        if mx:
            self.ops = self.ops[:mx]
            last = self.ops[-1]
            print("LAST OP", last[0], last[1], [t.name for t in last[3]], [t.name for t in last[4]])
        ops = self.ops
        n = len(ops)
        seq = {e: 0 for e in self.ENG}
        deps = [None] * n
        dma_tiles = []
        for i, (eng, is_dma, fn, reads, writes, parts) in enumerate(ops):
            d = []
            if is_dma:
                t0 = writes[0] if writes else reads[0]
                me = ("d", t0, t0.dcount + parts)
            else:
                seq[eng] += 1
                me = ("c", eng, seq[eng])
            for t in reads:
                if t.w is not None:
                    d.append(t.w)
                if t.psum:
                    d.extend(r for r in t.rs if r[0] == "c" and r[1] != eng)
            for t in writes:
                if t.w is not None:
                    d.append(t.w)
                d.extend(t.rs)
            if eng == "pe":
                d = [x for x in d if not (x[0] == "c" and x[1] == "pe")]
            if is_dma:
                t0.dcount += parts
                if t0 not in dma_tiles:
                    dma_tiles.append(t0)
            for t in reads:
                t.rs.append(me)
            for t in writes:
                t.w = me
                t.rs = []
            deps[i] = (d, me)
        need = {e: set() for e in self.ENG}
        for i in range(n):
            for dd in deps[i][0]:
                if dd[0] == "c":
                    need[dd[1]].add(dd[2])
        semval = {}
        for e in self.ENG:
            semval[e] = {q: k + 1 for k, q in enumerate(sorted(need[e]))}
        csem = {e: stack.enter_context(nc.semaphore("c_" + e)) for e in self.ENG if need[e]}
        for t in dma_tiles:
            t.dsem = stack.enter_context(nc.semaphore("d_" + t.name))
        waited = {e: {} for e in self.ENG}
        nwaits = 0
        for i, (eng, is_dma, fn, reads, writes, parts) in enumerate(ops):
            E = self.e[eng]
            d, me = deps[i]
            req = {}
            for dd in d:
                if dd[0] == "c":
                    key = ("c", dd[1]); val = semval[dd[1]][dd[2]]; sem = csem[dd[1]]
                else:
                    key = ("d", dd[1].name); val = 16 * dd[2]; sem = dd[1].dsem
                if req.get(key, (None, 0))[1] < val:
                    req[key] = (sem, val)
            for key, (sem, val) in req.items():
                if waited[eng].get(key, 0) >= val:
                    continue
                E.wait_ge(sem, val)
                if os.environ.get("SCHED_DBG") and i >= int(os.environ["SCHED_DBG"]):
                    print("  op", i, eng, "waits", key, val)
                waited[eng][key] = val
                nwaits += 1
            ins = fn(E)
            if is_dma:
                if not isinstance(ins, (list, tuple)):
                    ins = [ins]
                assert len(ins) == parts, (len(ins), parts)
                for x in ins:
                    x.then_inc(me[1].dsem, 16)
            elif me[2] in need[eng]:
                ins.then_inc(csem[eng], 1)
                if os.environ.get("SCHED_DBG") and i >= int(os.environ["SCHED_DBG"]):
                    print("  op", i, eng, "incs ->", semval[eng][me[2]])
        self.stats = dict(n_ops=n, n_waits=nwaits, n_sems=len(csem) + len(dma_tiles))


class Ctx:
    def __init__(self, nc, stack):
        self.nc = nc
        self.st = stack
        self.S = Sched(nc)
        self._n = 0

    def sb(self, name, shape, dt):
        t = self.st.enter_context(self.nc.sbuf_tensor("s_" + name, list(shape), dt))
        return t, T(name)

    def ps(self, name, shape, dt=F32):
        t = self.st.enter_context(self.nc.psum_tensor("p_" + name, list(shape), dt))
        return t, T(name, psum=True)


class Rot:
    def __init__(self, items):
        self.items = items
        self.i = 0

    def next(self):
        x = self.items[self.i % len(self.items)]
        self.i += 1
        return x


class Gemm:
    KC = 8
    MC = 512

    def __init__(self, cx, N, nslots=3, npsum=4):
        self.cx = cx
        self.N = N
        self.wb = Rot([cx.sb(f"wb{i}", [128, self.KC, self.MC], BF16) for i in range(nslots)])
        self.ws = Rot([cx.sb(f"ws{i}", [128, self.KC, self.MC], F32) for i in range(2)])
        self.ci = 0
        self.pp = Rot([cx.ps(f"pp{i}", [128, 512]) for i in range(npsum)])
        self.dq = 0

    def run(self, W, K, M, xb, xbT, epilogue, m_lo=0):
        S = self.cx.S
        N = self.N
        KT = K // 128
        kcs = [(k0, min(self.KC, KT - k0)) for k0 in range(0, KT, self.KC)]
        for m0 in range(0, M, self.MC):
            mw = min(self.MC, M - m0)
            nmt = mw // 128
            pst = [self.pp.next() for _ in range(nmt)]
            for ci, (k0, kn) in enumerate(kcs):
                wbt, wbT = self.wb.next()
                src = W[k0 * 128:(k0 + kn) * 128, m_lo + m0:m_lo + m0 + mw].rearrange("(kt p) m -> p kt m", p=128)
                wst, wsT = self.ws.next()
                S.dma("sp", lambda E, o=wst[:, 0:kn, 0:mw], s=src: E.dma_start(out=o, in_=s), writes=[wsT])
                if self.ci % 2 == 0:
                    S.op("dve", lambda E, o=wbt[:, 0:kn, 0:mw], i=wst[:, 0:kn, 0:mw]: E.tensor_copy(o, i), reads=[wsT], writes=[wbT])
                else:
                    S.op("act", lambda E, o=wbt[:, 0:kn, 0:mw], i=wst[:, 0:kn, 0:mw]: E.copy(o, i), reads=[wsT], writes=[wbT])
                self.ci += 1
                for j in range(nmt):
                    for kt in range(kn):
                        first = (ci == 0 and kt == 0)
                        last = (ci == len(kcs) - 1 and kt == kn - 1)
                        S.op("pe", lambda E, o=pst[j][0][:, 0:N], l=wbt[:, kt, j * 128:(j + 1) * 128],
                             r=xb[:, k0 + kt, :], a=first, b=last: E.matmul(o, l, r, start=a, stop=b),
                             reads=[wbT, xbT], writes=[pst[j][1]])
            for j in range(nmt):
                epilogue(m0 // 128 + j, pst[j][0][:, 0:N], pst[j][1])


def rms_stats(cx, hT, hTT, KT, N, ones, onesT, sq_rot, ps_rot, rstd, rstdT, dim):
    S = cx.S
    pst, psT = ps_rot.next()
    for kt in range(KT):
        sq, sqT = sq_rot.next()
        S.op("act", lambda E, o=sq[:, 0:N], i=hT[:, kt, :]: E.activation(o, i, AF.Square), reads=[hTT], writes=[sqT])
        S.op("pe", lambda E, o=pst[:, 0:N], l=ones[:, :], r=sq[:, 0:N], a=(kt == 0), b=(kt == KT - 1):
             E.matmul(o, l, r, start=a, stop=b), reads=[onesT, sqT], writes=[psT])
    S.op("dve", lambda E: E.tensor_scalar(rstd[:, 0:N], pst[:, 0:N], 1.0 / dim, NORM_EPS, ALU.mult, ALU.add),
         reads=[psT], writes=[rstdT])
    S.op("act", lambda E: E.activation(rstd[:, 0:N], rstd[:, 0:N], AF.Sqrt), reads=[rstdT], writes=[rstdT])
    S.op("dve", lambda E: E.reciprocal(rstd[:, 0:N], rstd[:, 0:N]), reads=[rstdT], writes=[rstdT])


def build_proj(ntok, n_out, N=512):
    nc = bass.Bass("TRN2", target_bir_lowering=False)
    KT = D_MODEL // 128
    hT_d = nc.dram_tensor("hT", [D_MODEL, ntok], F32, kind="ExternalInput").ap()
    w_d = nc.dram_tensor("w", [D_MODEL, n_out], F32, kind="ExternalInput").ap()
    g_d = nc.dram_tensor("g", [128, KT], F32, kind="ExternalInput").ap()
    z_d = nc.dram_tensor("zT", [n_out, ntok], F32, kind="ExternalOutput").ap()
    with ExitStack() as st:
        cx = Ctx(nc, st)
        S = cx.S
        hT, hTT = cx.sb("hT", [128, KT, N], F32)
        xb, xbT = cx.sb("xb", [128, KT, N], BF16)
        g, gT = cx.sb("g", [128, KT], F32)
        ones, onesT = cx.sb("ones", [128, 128], F32)
        rstd, rstdT = cx.sb("rstd", [128, N], F32)
        sq_rot = Rot([cx.sb(f"sq{i}", [128, N], F32) for i in range(2)])
        ob_rot = Rot([cx.sb(f"ob{i}", [128, N], F32) for i in range(3)])
        gm = Gemm(cx, N)
        ps_rot = Rot([cx.ps("pstat", [128, 512])])
        S.dma("sp", lambda E: E.dma_start(out=g[:, :], in_=g_d[:, :]), writes=[gT])
        S.op("dve", lambda E: E.memset(ones[:, :], 1.0), writes=[onesT])
        outs = []
        for t0 in range(0, ntok, N):
            S.dma("sp", lambda E, t0=t0: E.dma_start(out=hT[:, :, :], in_=hT_d[:, t0:t0 + N].rearrange("(kt p) n -> p kt n", p=128)),
                  writes=[hTT])
            rms_stats(cx, hT, hTT, KT, N, ones, onesT, sq_rot, ps_rot, rstd, rstdT, D_MODEL)
            for kt in range(KT):
                S.op("dve", lambda E, kt=kt: E.tensor_scalar(xb[:, kt, :], hT[:, kt, :], g[:, kt:kt + 1], None, ALU.mult),
                     reads=[hTT, gT], writes=[xbT])

            def epi(mt, ps, psT, t0=t0):
                ob, obT = ob_rot.next()
                S.op("dve", lambda E: E.tensor_tensor(ob[:, :], ps, rstd[:, 0:N], ALU.mult), reads=[psT, rstdT], writes=[obT])
                S.dma("sp", lambda E: E.dma_start(out=z_d[mt * 128:(mt + 1) * 128, t0:t0 + N], in_=ob[:, :]), reads=[obT])
                if obT not in outs:
                    outs.append(obT)

            gm.run(w_d, D_MODEL, n_out, xb, xbT, epi)
        S.final_wait("sp", outs)
        S.emit(st)
        nc._stats = S.stats
    return nc


def build_ffn(ntok, N=512, d_ff=D_FF):
    nc = bass.Bass("TRN2", target_bir_lowering=False)
    KT = D_MODEL // 128
    FT = d_ff // 128
    PT = PLE_DIM // 128
    hT_d = nc.dram_tensor("hT", [D_MODEL, ntok], F32, kind="ExternalInput").ap()
    mixT_d = nc.dram_tensor("mixT", [D_MODEL, ntok], F32, kind="ExternalInput").ap()
    pT_d = nc.dram_tensor("pT", [PLE_DIM, ntok], F32, kind="ExternalInput").ap()
    w_out_d = nc.dram_tensor("w_out", [D_MODEL, D_MODEL], F32, kind="ExternalInput").ap()
    w_up_d = nc.dram_tensor("w_up", [D_MODEL, d_ff], F32, kind="ExternalInput").ap()
    w_down_d = nc.dram_tensor("w_down", [d_ff, D_MODEL], F32, kind="ExternalInput").ap()
    w_gate_d = nc.dram_tensor("w_gate", [D_MODEL, D_MODEL], F32, kind="ExternalInput").ap()
    w_ple_d = nc.dram_tensor("w_ple", [PLE_DIM, D_MODEL], F32, kind="ExternalInput").ap()
    g2_d = nc.dram_tensor("g_mlp", [128, KT], F32, kind="ExternalInput").ap()
    g3_d = nc.dram_tensor("g_ple", [128, KT], F32, kind="ExternalInput").ap()
    o_d = nc.dram_tensor("oT", [D_MODEL, ntok], F32, kind="ExternalOutput").ap()
    with ExitStack() as st:
        cx = Ctx(nc, st)
        S = cx.S
        hT, hTT = cx.sb("hT", [128, KT, N], F32)
        xb, xbT = cx.sb("xb", [128, KT, N], BF16)
        aT, aTT = cx.sb("aT", [128, FT, N], BF16)
        pb, pbT = cx.sb("pb", [128, PT, N], BF16)
        g2, g2T = cx.sb("g2", [128, KT], F32)
        g3, g3T = cx.sb("g3", [128, KT], F32)
        ones, onesT = cx.sb("ones", [128, 128], F32)
        rstd, rstdT = cx.sb("rstd", [128, N], F32)
        sq_rot = Rot([cx.sb(f"sq{i}", [128, N], F32) for i in range(2)])
        tmp_rot = Rot([cx.sb(f"tmp{i}", [128, N], F32) for i in range(3)])
        gm = Gemm(cx, N)
        ps_rot = Rot([cx.ps("pstat", [128, 512])])
        pple_rot = Rot([cx.ps(f"pple{i}", [128, 512]) for i in range(2)])
        S.dma("sp", lambda E: E.dma_start(out=g2[:, :], in_=g2_d[:, :]), writes=[g2T])
        S.dma("sp", lambda E: E.dma_start(out=g3[:, :], in_=g3_d[:, :]), writes=[g3T])
        S.op("dve", lambda E: E.memset(ones[:, :], 1.0), writes=[onesT])
        for t0 in range(0, ntok, N):
            S.dma("sp", lambda E, t0=t0: E.dma_start(out=hT[:, :, :], in_=hT_d[:, t0:t0 + N].rearrange("(kt p) n -> p kt n", p=128)),
                  writes=[hTT])
            S.dma("pool", lambda E, t0=t0: E.dma_start(out=xb[:, :, :], in_=mixT_d[:, t0:t0 + N].rearrange("(kt p) n -> p kt n", p=128)),
                  writes=[xbT])
            S.dma("pool", lambda E, t0=t0: E.dma_start(out=pb[:, :, :], in_=pT_d[:, t0:t0 + N].rearrange("(kt p) n -> p kt n", p=128)),
                  writes=[pbT])

            def epi_add(mt, ps, psT):
                S.op("dve", lambda E: E.tensor_tensor(hT[:, mt, :], hT[:, mt, :], ps, ALU.add), reads=[psT, hTT], writes=[hTT])
            gm.run(w_out_d, D_MODEL, D_MODEL, xb, xbT, epi_add)

            def norm_to_xb(gv, gvT):
                rms_stats(cx, hT, hTT, KT, N, ones, onesT, sq_rot, ps_rot, rstd, rstdT, D_MODEL)
                for kt in range(KT):
                    S.op("dve", lambda E, kt=kt: E.scalar_tensor_tensor(out=xb[:, kt, :], in0=hT[:, kt, :], scalar=gv[:, kt:kt + 1],
                                                                        in1=rstd[:, 0:N], op0=ALU.mult, op1=ALU.mult),
                         reads=[hTT, gvT, rstdT], writes=[xbT])
            norm_to_xb(g2, g2T)

            def epi_relu2(mt, ps, psT):
                tmp, tmpT = tmp_rot.next()
                S.op("act", lambda E: E.activation(tmp[:, :], ps, AF.Relu), reads=[psT], writes=[tmpT])
                S.op("dve", lambda E: E.tensor_tensor(aT[:, mt, :], tmp[:, :], tmp[:, :], ALU.mult), reads=[tmpT], writes=[aTT])
            gm.run(w_up_d, D_MODEL, d_ff, xb, xbT, epi_relu2)

            gm.run(w_down_d, d_ff, D_MODEL, aT, aTT, epi_add)

            norm_to_xb(g3, g3T)

            def epi_gate(mt, ps, psT):
                tmp, tmpT = tmp_rot.next()
                S.op("act", lambda E: E.activation(tmp[:, :], ps, AF.Sigmoid), reads=[psT], writes=[tmpT])
                wbt, wbT = gm.wb.next()
                src = w_ple_d[:, mt * 128:(mt + 1) * 128].rearrange("(kt p) m -> p kt m", p=128)
                S.dma("pool", lambda E: E.dma_start(out=wbt[:, 0:PT, 0:128], in_=src), writes=[wbT])
                pp, ppT = pple_rot.next()
                for kt in range(PT):
                    S.op("pe", lambda E, kt=kt: E.matmul(pp[:, 0:N], wbt[:, kt, 0:128], pb[:, kt, :], start=(kt == 0), stop=(kt == PT - 1)),
                         reads=[wbT, pbT], writes=[ppT])
                S.op("dve", lambda E: E.tensor_tensor(tmp[:, :], tmp[:, :], pp[:, 0:N], ALU.mult), reads=[tmpT, ppT], writes=[tmpT])
                S.op("dve", lambda E: E.tensor_tensor(hT[:, mt, :], hT[:, mt, :], tmp[:, :], ALU.add), reads=[tmpT, hTT], writes=[hTT])
            gm.run(w_gate_d, D_MODEL, D_MODEL, xb, xbT, epi_gate)

            S.dma("sp", lambda E, t0=t0: E.dma_start(out=o_d[:, t0:t0 + N].rearrange("(kt p) n -> p kt n", p=128), in_=hT[:, :, :]),
                  reads=[hTT])
        S.final_wait("sp", [hTT])
        S.emit(st)
        nc._stats = S.stats
    return nc


def build_attn(S_len, layer_idx):
    nc = bass.Bass("TRN2", target_bir_lowering=False)
    NQ = 512
    nkt = S_len // 128
    nqb = S_len // NQ
    lam_init = 0.8 - 0.6 * math.exp(-0.3 * layer_idx)
    scale = 64 ** -0.5
    qT_d = nc.dram_tensor("qT", [128, S_len], F32, kind="ExternalInput").ap()
    kT_d = nc.dram_tensor("kT", [128, S_len], F32, kind="ExternalInput").ap()
    v_d = nc.dram_tensor("v", [S_len, 128], F32, kind="ExternalInput").ap()
    qg_d = nc.dram_tensor("qg", [128, 1], F32, kind="ExternalInput").ap()
    kg_d = nc.dram_tensor("kg", [128, 1], F32, kind="ExternalInput").ap()
    lam_d = nc.dram_tensor("lam", [128, 256], F32, kind="ExternalInput").ap()
    sub_d = nc.dram_tensor("subln", [128, 128], F32, kind="ExternalInput").ap()
    bias_d = nc.dram_tensor("biasT", [128, 3, 128], F32, kind="ExternalInput").ap()
    cfar_d = nc.dram_tensor("cfar", [2, 128, 1], F32, kind="ExternalInput").ap()
    o_d = nc.dram_tensor("o", [S_len, 128], F32, kind="ExternalOutput").ap()
    with ExitStack() as st:
        cx = Ctx(nc, st)
        S = cx.S
        qb_, qbT = cx.sb("qhat", [128, S_len], BF16)
        kb_, kbT = cx.sb("khat", [128, S_len], BF16)
        vb, vbT = cx.sb("vb", [128, nkt, 130], BF16)
        qg, qgT = cx.sb("qg", [128, 1], F32)
        kg, kgT = cx.sb("kg", [128, 1], F32)
        lam, lamT = cx.sb("lam", [128, 256], F32)
        sub, subT = cx.sb("sub", [128, 128], F32)
        biasT, biasTT = cx.sb("biasT", [128, 3, 128], F32)
        cfar, cfarT = cx.sb("cfar", [128, 2, 16], F32)
        ones, onesT = cx.sb("ones", [128, 128], F32)
        sm, smT = cx.sb("sm", [128, 8], F32)
        stg_rot = Rot([cx.sb(f"stg{i}", [128, NQ], F32) for i in range(2)])
        sq_rot = Rot([cx.sb(f"sq{i}", [128, NQ], F32) for i in range(2)])
        rs_rot = Rot([cx.sb(f"rs{i}", [128, NQ], F32) for i in range(2)])
        pT_rot = Rot([cx.sb(f"pT{i}", [128, NQ], BF16) for i in range(4)])
        tb_rot = Rot([cx.sb(f"tb{i}", [128, 128], F32) for i in range(3)])
        ep_rot = Rot([cx.sb(f"ep{i}", [128, 3, 128], F32) for i in range(2)])
        eps_rot = Rot([cx.sb(f"eps{i}", [128, 4], F32) for i in range(2)])
        ps_s = Rot([cx.ps(f"ps_s{i}", [128, 512]) for i in range(4)])
        ps_a = [cx.ps(f"ps_a{i}", [128, 512]) for i in range(3)]
        ps_n = Rot([cx.ps("ps_n", [128, 512])])

        for (t, tT, d) in ((qg, qgT, qg_d), (kg, kgT, kg_d), (lam, lamT, lam_d), (sub, subT, sub_d)):
            S.dma("sp", lambda E, t=t, d=d: E.dma_start(out=t[:, :], in_=d[:, :]), writes=[tT])
        S.dma("sp", lambda E: E.dma_start(out=biasT[:, :, :], in_=bias_d[:, :, :]), writes=[biasTT])
        S.dma("sp", lambda E: [E.dma_start(out=cfar[:, 0, 0:1], in_=cfar_d[0, :, :]), E.dma_start(out=cfar[:, 1, 0:1], in_=cfar_d[1, :, :])], writes=[cfarT], parts=2)
        S.op("dve", lambda E: E.memset(ones[:, :], 0.0), writes=[onesT])
        S.op("dve", lambda E: E.memset(ones[0:64, 0:64], 1.0), writes=[onesT])
        S.op("dve", lambda E: E.memset(ones[64:128, 64:128], 1.0), writes=[onesT])
        S.dma("pool", lambda E: E.dma_start(out=vb[:, :, 0:128], in_=v_d.rearrange("(kt p) d -> p kt d", p=128)), writes=[vbT])
        S.op("dve", lambda E: E.memset(vb[:, :, 128:130], 1.0), writes=[vbT])
        S.op("dve", lambda E: E.tensor_tensor(lam[:, 0:64], lam[:, 0:64], lam[:, 64:128], ALU.mult), reads=[lamT], writes=[lamT])
        S.op("dve", lambda E: E.tensor_tensor(lam[:, 128:192], lam[:, 128:192], lam[:, 192:256], ALU.mult), reads=[lamT], writes=[lamT])
        S.op("dve", lambda E: E.reduce_sum(sm[:, 0:1], lam[:, 0:64], AX.X), reads=[lamT], writes=[smT])
        S.op("dve", lambda E: E.reduce_sum(sm[:, 1:2], lam[:, 128:192], AX.X), reads=[lamT], writes=[smT])
        S.op("act", lambda E: E.activation(sm[:, 0:2], sm[:, 0:2], AF.Exp), reads=[smT], writes=[smT])
        S.op("dve", lambda E: E.tensor_tensor(sm[:, 2:3], sm[:, 1:2], sm[:, 0:1], ALU.subtract), reads=[smT], writes=[smT])
        S.op("dve", lambda E: E.tensor_scalar(sm[:, 2:3], sm[:, 2:3], -lam_init, None, ALU.add), reads=[smT], writes=[smT])
        S.op("dve", lambda E: E.tensor_scalar(sub[:, :], sub[:, :], 1.0 - lam_init, None, ALU.mult), reads=[subT], writes=[subT])

        for (src_d, g, gT, dst, dstT) in ((qT_d, qg, qgT, qb_, qbT), (kT_d, kg, kgT, kb_, kbT)):
            for c0 in range(0, S_len, NQ):
                stg, stgT = stg_rot.next()
                sq, sqT = sq_rot.next()
                rs, rsT = rs_rot.next()
                pn, pnT = ps_n.next()
                S.dma("sp", lambda E, stg=stg, c0=c0, src_d=src_d: E.dma_start(out=stg[:, :], in_=src_d[:, c0:c0 + NQ]), writes=[stgT])
                S.op("act", lambda E, sq=sq, stg=stg: E.activation(sq[:, :], stg[:, :], AF.Square), reads=[stgT], writes=[sqT])
                S.op("pe", lambda E, pn=pn, sq=sq: E.matmul(pn[:, :], ones[:, :], sq[:, :], start=True, stop=True), reads=[onesT, sqT], writes=[pnT])
                S.op("dve", lambda E, rs=rs, pn=pn: E.tensor_scalar(rs[:, :], pn[:, :], 1.0 / 64, NORM_EPS, ALU.mult, ALU.add), reads=[pnT], writes=[rsT])
                S.op("act", lambda E, rs=rs: E.activation(rs[:, :], rs[:, :], AF.Sqrt), reads=[rsT], writes=[rsT])
                S.op("dve", lambda E, rs=rs: E.reciprocal(rs[:, :], rs[:, :]), reads=[rsT], writes=[rsT])
                S.op("dve", lambda E, rs=rs, stg=stg, g=g, dst=dst, c0=c0: E.scalar_tensor_tensor(
                    out=dst[:, c0:c0 + NQ], in0=stg[:, :], scalar=g[:, 0:1], in1=rs[:, :], op0=ALU.mult, op1=ALU.mult),
                    reads=[stgT, gT, rsT], writes=[dstT])

        outs = []
        for qb in range(nqb):
            q0 = qb * NQ
            started = [False, False, False]

            def acc_of(m, s):
                if s < 3:
                    return m, s * 130
                return 2, m * 130

            units = [(kt, m) for kt in range(nkt) for m in range(2)]
            qk_out = {}

            def emit_qk(u):
                kt, m = units[u]
                pss, pssT = ps_s.next()
                S.op("pe", lambda E, pss=pss, m=m, kt=kt, q0=q0: E.matmul(pss[:, 0:NQ], kb_[64 * m:64 * m + 64, kt * 128:(kt + 1) * 128],
                                                                         qb_[64 * m:64 * m + 64, q0:q0 + NQ], start=True, stop=True),
                     reads=[kbT, qbT], writes=[pssT])
                qk_out[u] = (pss, pssT)

            LA = 3
            for u in range(min(LA, len(units))):
                emit_qk(u)
            for u, (kt, m) in enumerate(units):
                pss, pssT = qk_out.pop(u)
                pT, pTT = pT_rot.next()
                qs0 = 4 * qb
                if kt < qs0 - 1 or kt > qs0 + 4:
                    col = 0 if kt < qs0 else 1
                    S.op("act", lambda E, pT=pT, pss=pss, col=col: E.activation(pT[:, :], pss[:, 0:NQ], AF.Exp, bias=cfar[:, col, 0:1], scale=scale),
                         reads=[pssT, cfarT], writes=[pTT])
                else:
                    for s in range(4):
                        dlt = kt - (qs0 + s)
                        sl = slice(s * 128, (s + 1) * 128)
                        if abs(dlt) <= 1:
                            tb, tbT = tb_rot.next()
                            S.op("dve", lambda E, tb=tb, pss=pss, sl=sl, dlt=dlt: E.scalar_tensor_tensor(
                                out=tb[:, :], in0=pss[:, sl], scalar=scale, in1=biasT[:, dlt + 1, :], op0=ALU.mult, op1=ALU.add),
                                reads=[pssT, biasTT], writes=[tbT])
                            S.op("act", lambda E, pT=pT, tb=tb, sl=sl: E.activation(pT[:, sl], tb[:, :], AF.Exp), reads=[tbT], writes=[pTT])
                        else:
                            col = 0 if dlt < 0 else 1
                            S.op("act", lambda E, pT=pT, pss=pss, sl=sl, col=col: E.activation(pT[:, sl], pss[:, sl], AF.Exp, bias=cfar[:, col, 0:1], scale=scale),
                                 reads=[pssT, cfarT], writes=[pTT])
                if u + LA < len(units):
                    emit_qk(u + LA)
                for s in range(4):
                    bi, off = acc_of(m, s)
                    acc, accT = ps_a[bi]
                    stt = not started[bi]
                    started[bi] = True
                    S.op("pe", lambda E, acc=acc, off=off, pT=pT, s=s, kt=kt, stt=stt: E.matmul(
                        acc[:, off:off + 130], pT[:, s * 128:(s + 1) * 128], vb[:, kt, :], start=stt, stop=(kt == nkt - 1),
                        skip_group_check=True), reads=[pTT, vbT], writes=[accT])
            for s in range(4):
                ep, epT = ep_rot.next()
                es, esT = eps_rot.next()
                for m in range(2):
                    bi, off = acc_of(m, s)
                    acc, accT = ps_a[bi]
                    S.op("dve", lambda E, es=es, acc=acc, off=off, m=m: E.reciprocal(es[:, m:m + 1], acc[:, off + 128:off + 129]), reads=[accT], writes=[esT])
                    S.op("dve", lambda E, ep=ep, es=es, acc=acc, off=off, m=m: E.tensor_scalar(ep[:, m, :], acc[:, off:off + 128], es[:, m:m + 1], None, ALU.mult),
                         reads=[accT, esT], writes=[epT])
                S.op("dve", lambda E, ep=ep: E.scalar_tensor_tensor(out=ep[:, 2, :], in0=ep[:, 1, :], scalar=sm[:, 2:3], in1=ep[:, 0, :], op0=ALU.mult, op1=ALU.add),
                     reads=[epT, smT], writes=[epT])
                S.op("act", lambda E, ep=ep, es=es: E.activation(ep[:, 0, :], ep[:, 2, :], AF.Square, accum_out=es[:, 2:3]), reads=[epT], writes=[epT, esT])
                S.op("dve", lambda E, es=es: E.tensor_scalar(es[:, 2:3], es[:, 2:3], 1.0 / 128, NORM_EPS, ALU.mult, ALU.add), reads=[esT], writes=[esT])
                S.op("act", lambda E, es=es: E.activation(es[:, 2:3], es[:, 2:3], AF.Sqrt), reads=[esT], writes=[esT])
                S.op("dve", lambda E, es=es: E.reciprocal(es[:, 3:4], es[:, 2:3]), reads=[esT], writes=[esT])
                S.op("dve", lambda E, ep=ep, es=es: E.scalar_tensor_tensor(out=ep[:, 1, :], in0=ep[:, 2, :], scalar=es[:, 3:4], in1=sub[:, :], op0=ALU.mult, op1=ALU.mult),
                     reads=[epT, esT, subT], writes=[epT])
                S.dma("sp", lambda E, ep=ep, s=s, q0=q0: E.dma_start(out=o_d[q0 + s * 128:q0 + (s + 1) * 128, :], in_=ep[:, 1, :]), reads=[epT])
                if epT not in outs:
                    outs.append(epT)
        S.final_wait("sp", outs)
        S.emit(st)
        nc._stats = S.stats
    return nc


def t5_bucket_np(rel):
    half = 16
    max_exact = 8
    n = np.abs(rel)
    nf = np.maximum(n, 1).astype(np.float32)
    large = max_exact + (np.log(nf / max_exact) / np.float32(math.log(128 / max_exact)) * (half - max_exact)).astype(np.int32)
    large = np.minimum(large, half - 1)
    return np.where(rel > 0, half, 0) + np.where(n < max_exact, n, large)


def attn_inputs(zT_b, j, rel_bias, a_qnorm, a_knorm, a_lambda, a_subln):
    qT = np.ascontiguousarray(zT_b[j * 128:(j + 1) * 128])
    kT = np.ascontiguousarray(zT_b[512 + j * 128:512 + (j + 1) * 128])
    v = np.ascontiguousarray(zT_b[1024 + j * 128:1024 + (j + 1) * 128].T)
    kl = np.arange(128)[:, None]
    ql = np.arange(128)[None, :]
    tiles = []
    for o in (-1, 0, 1):
        rel = o * 128 + kl - ql
        tiles.append(rel_bias[t5_bucket_np(rel), j])
    biasT = np.ascontiguousarray(np.stack(tiles, axis=1).astype(np.float32))
    cfar = np.ascontiguousarray(np.broadcast_to(np.array([rel_bias[15, j], rel_bias[31, j]], np.float32)[:, None, None], (2, 128, 1)))
    return dict(qT=qT, kT=kT, v=v,
                qg=np.ascontiguousarray(np.tile(a_qnorm, 2)[:, None]), kg=np.ascontiguousarray(np.tile(a_knorm, 2)[:, None]),
                lam=np.ascontiguousarray(np.broadcast_to(a_lambda.reshape(1, 256), (128, 256))),
                subln=np.ascontiguousarray(np.broadcast_to(a_subln[None, :], (128, 128))),
                biasT=biasT, cfar=cfar)


def hgrn_masks():
    idx = np.arange(128)
    same = (idx[:, None] // 64) == (idx[None, :] // 64)
    s = idx[:, None]
    t = idx[None, :]
    out = {}
    for name, le, mid in (("f", lambda a, b: a <= b, 31), ("b", lambda a, b: a >= b, 32)):
        M = (same & le(s, t)).astype(np.float32)
        r = (idx // 64) * 64 + mid
        Mq = M - M[:, r]
        Mc = (same & ~le(s, t)).astype(np.float32)
        out[name] = np.stack([M, Mq, Mc, M], axis=1)
    return np.ascontiguousarray(np.concatenate([out["f"], out["b"]], axis=1).astype(np.float32))


def build_hgrn(S_len, layer_idx):
    nc = bass.Bass("TRN2", target_bir_lowering=False)
    nt = S_len // 128
    has_lb = layer_idx > 0
    zqT_d = nc.dram_tensor("zqT", [128, S_len], F32, kind="ExternalInput").ap()
    zgT_d = nc.dram_tensor("zgT", [128, S_len], F32, kind="ExternalInput").ap()
    zfT_d = nc.dram_tensor("zfT", [2, 128, S_len], F32, kind="ExternalInput").ap()
    zf_d = nc.dram_tensor("zf", [2, S_len, 128], F32, kind="ExternalInput").ap()
    zi_d = nc.dram_tensor("zi", [S_len, 128], F32, kind="ExternalInput").ap()
    lbr_d = nc.dram_tensor("lbrow", [128, 2, 128], F32, kind="ExternalInput").ap()
    lbc_d = nc.dram_tensor("lbcol", [128, 2], F32, kind="ExternalInput").ap()
    og_d = nc.dram_tensor("ogain", [128, 1], F32, kind="ExternalInput").ap()
    mk_d = nc.dram_tensor("masks", [128, 8, 128], F32, kind="ExternalInput").ap()
    o_d = nc.dram_tensor("oT", [128, S_len], F32, kind="ExternalOutput").ap()
    with ExitStack() as st:
        cx = Ctx(nc, st)
        S = cx.S
        oT, oTT = cx.sb("oacc", [128, S_len], F32)
        vtm, vtmT = cx.sb("vtm", [128, nt, 128], BF16)
        mk, mkT = cx.sb("mk", [128, 8, 128], F32)
        lbr, lbrT = cx.sb("lbr", [128, 2, 128], F32)
        lbc, lbcT = cx.sb("lbc", [128, 2], F32)
        og, ogT = cx.sb("og", [128, 1], F32)
        ones, onesT = cx.sb("ones", [128, 128], F32)
        St, StT = cx.sb("state", [128, 128], F32)
        Sb, SbT = cx.sb("stateb", [128, 128], BF16)
        R = {}
        for nm, dt, n in (("zf", F32, 2), ("zfT", F32, 2), ("zqT", F32, 2), ("sg", F32, 2), ("lf", F32, 2), ("ktm", F32, 2), ("e1", F32, 2),
                          ("kdec", BF16, 2), ("eq", F32, 2), ("ek", F32, 2), ("ed", F32, 3), ("qT", F32, 2), ("kT", F32, 2),
                          ("qq", BF16, 2), ("kk", BF16, 2), ("qd", BF16, 2), ("attm", BF16, 2)):
            R[nm] = Rot([cx.sb(f"{nm}{i}", [128, 128], dt) for i in range(n)])
        P = {nm: Rot([cx.ps(nm, [128, 512])]) for nm in ("p1", "p2", "p3", "p4", "p5", "p6", "pn")}
        S.dma("sp", lambda E: E.dma_start(out=mk[:, :, :], in_=mk_d[:, :, :]), writes=[mkT])
        S.dma("sp", lambda E: E.dma_start(out=lbr[:, :, :], in_=lbr_d[:, :, :]), writes=[lbrT])
        S.dma("sp", lambda E: E.dma_start(out=lbc[:, :], in_=lbc_d[:, :]), writes=[lbcT])
        S.dma("sp", lambda E: E.dma_start(out=og[:, :], in_=og_d[:, :]), writes=[ogT])
        S.dma("pool", lambda E: E.dma_start(out=vtm[:, :, :], in_=zi_d.rearrange("(kt p) d -> p kt d", p=128)), writes=[vtmT])
        S.op("dve", lambda E: E.memset(ones[:, :], 1.0), writes=[onesT])
        if has_lb:
            S.op("dve", lambda E: E.tensor_tensor(lbr[:, 0, :], lbr[:, 1, :], lbr[:, 0, :], ALU.subtract), reads=[lbrT], writes=[lbrT])
            S.op("act", lambda E: E.activation(lbr[:, 0, :], lbr[:, 0, :], AF.Sigmoid), reads=[lbrT], writes=[lbrT])
            S.op("dve", lambda E: E.tensor_scalar(lbr[:, 1, :], lbr[:, 0, :], -1.0, 1.0, ALU.mult, ALU.add), reads=[lbrT], writes=[lbrT])
            S.op("dve", lambda E: E.tensor_tensor(lbc[:, 0:1], lbc[:, 1:2], lbc[:, 0:1], ALU.subtract), reads=[lbcT], writes=[lbcT])
            S.op("act", lambda E: E.activation(lbc[:, 0:1], lbc[:, 0:1], AF.Sigmoid), reads=[lbcT], writes=[lbcT])
            S.op("dve", lambda E: E.tensor_scalar(lbc[:, 1:2], lbc[:, 0:1], -1.0, 1.0, ALU.mult, ALU.add), reads=[lbcT], writes=[lbcT])

        for z in range(2):
            mo = 4 * z
            S.op("dve", lambda E: E.memset(St[:, :], 0.0), writes=[StT])
            S.op("dve", lambda E: E.memset(Sb[:, :], 0.0), writes=[SbT])
            tiles = range(nt) if z == 0 else range(nt - 1, -1, -1)
            for i in tiles:
                c0 = i * 128
                zf, zfT_ = R["zf"].next()
                zfT, zfTT = R["zfT"].next()
                zqT, zqTT = R["zqT"].next()
                S.dma("sp", lambda E, zf=zf, z=z, c0=c0: E.dma_start(out=zf[:, :], in_=zf_d[z, c0:c0 + 128, :]), writes=[zfT_])
                S.dma("sp", lambda E, zfT=zfT, z=z, c0=c0: E.dma_start(out=zfT[:, :], in_=zfT_d[z, :, c0:c0 + 128]), writes=[zfTT])
                S.dma("sp", lambda E, zqT=zqT, c0=c0: E.dma_start(out=zqT[:, :], in_=zqT_d[:, c0:c0 + 128]), writes=[zqTT])
                sg, sgT = R["sg"].next()
                lf, lfT = R["lf"].next()
                ktm, ktmT = R["ktm"].next()
                S.op("act", lambda E, sg=sg, zf=zf: E.activation(sg[:, :], zf[:, :], AF.Sigmoid), reads=[zfT_], writes=[sgT])
                S.op("act", lambda E, ktm=ktm, zf=zf: E.activation(ktm[:, :], zf[:, :], AF.Sigmoid, scale=-1.0), reads=[zfT_], writes=[ktmT])
                if has_lb:
                    S.op("dve", lambda E, sg=sg: E.tensor_tensor(sg[:, :], sg[:, :], lbr[:, 1, :], ALU.mult), reads=[sgT, lbrT], writes=[sgT])
                    S.op("dve", lambda E, sg=sg: E.tensor_tensor(sg[:, :], sg[:, :], lbr[:, 0, :], ALU.add), reads=[sgT, lbrT], writes=[sgT])
                    S.op("dve", lambda E, ktm=ktm: E.tensor_tensor(ktm[:, :], ktm[:, :], lbr[:, 1, :], ALU.mult), reads=[ktmT, lbrT], writes=[ktmT])
                S.op("act", lambda E, lf=lf, sg=sg: E.activation(lf[:, :], sg[:, :], AF.Ln), reads=[sgT], writes=[lfT])
                p1, p1T = P["p1"].next()
                p2, p2T = P["p2"].next()
                p3, p3T = P["p3"].next()
                S.op("pe", lambda E, p1=p1, lf=lf, mo=mo: E.matmul(p1[:, 0:128], mk[:, mo + 2, :], lf[:, :], start=True, stop=True), reads=[mkT, lfT], writes=[p1T])
                S.op("pe", lambda E, p2=p2, lf=lf, mo=mo: E.matmul(p2[:, 0:128], lf[:, :], mk[:, mo + 1, :], start=True, stop=True), reads=[mkT, lfT], writes=[p2T])
                S.op("pe", lambda E, p3=p3, lf=lf, mo=mo: E.matmul(p3[:, 0:128], lf[:, :], mk[:, mo + 0, :], start=True, stop=True), reads=[mkT, lfT], writes=[p3T])
                e1, e1T = R["e1"].next()
                kdec, kdecT = R["kdec"].next()
                S.op("act", lambda E, e1=e1, p1=p1: E.activation(e1[:, :], p1[:, 0:128], AF.Exp), reads=[p1T], writes=[e1T])
                S.op("dve", lambda E, kdec=kdec, ktm=ktm, e1=e1: E.tensor_tensor(kdec[:, :], ktm[:, :], e1[:, :], ALU.mult), reads=[ktmT, e1T], writes=[kdecT])
                eq, eqT = R["eq"].next()
                ek, ekT = R["ek"].next()
                ed, edT = R["ed"].next()
                S.op("act", lambda E, eq=eq, p2=p2: E.activation(eq[:, :], p2[:, 0:128], AF.Exp), reads=[p2T], writes=[eqT])
                S.op("act", lambda E, ek=ek, p2=p2: E.activation(ek[:, :], p2[:, 0:128], AF.Exp, scale=-1.0), reads=[p2T], writes=[ekT])
                S.op("act", lambda E, ed=ed, p3=p3: E.activation(ed[:, :], p3[:, 0:128], AF.Exp), reads=[p3T], writes=[edT])
                qT, qTT = R["qT"].next()
                kT, kTT = R["kT"].next()
                S.op("act", lambda E, qT=qT, zqT=zqT: E.activation(qT[:, :], zqT[:, :], AF.Silu), reads=[zqTT], writes=[qTT])
                S.op("act", lambda E, kT=kT, zfT=zfT: E.activation(kT[:, :], zfT[:, :], AF.Sigmoid, scale=-1.0), reads=[zfTT], writes=[kTT])
                if has_lb:
                    S.op("dve", lambda E, kT=kT: E.tensor_scalar(kT[:, :], kT[:, :], lbc[:, 1:2], None, ALU.mult), reads=[kTT, lbcT], writes=[kTT])
                qq, qqT = R["qq"].next()
                kk, kkT = R["kk"].next()
                qd, qdT = R["qd"].next()
                S.op("dve", lambda E, qq=qq, qT=qT, eq=eq: E.tensor_tensor(qq[:, :], qT[:, :], eq[:, :], ALU.mult), reads=[qTT, eqT], writes=[qqT])
                S.op("dve", lambda E, kk=kk, kT=kT, ek=ek: E.tensor_tensor(kk[:, :], kT[:, :], ek[:, :], ALU.mult), reads=[kTT, ekT], writes=[kkT])
                S.op("dve", lambda E, qd=qd, qT=qT, ed=ed: E.tensor_tensor(qd[:, :], qT[:, :], ed[:, :], ALU.mult), reads=[qTT, edT], writes=[qdT])
                p4, p4T = P["p4"].next()
                S.op("pe", lambda E, p4=p4, kk=kk, qq=qq: E.matmul(p4[:, 0:128], kk[:, :], qq[:, :], start=True, stop=True), reads=[kkT, qqT], writes=[p4T])
                attm, attmT = R["attm"].next()
                S.op("dve", lambda E, attm=attm, p4=p4, mo=mo: E.tensor_tensor(attm[:, :], p4[:, 0:128], mk[:, mo + 3, :], ALU.mult), reads=[p4T, mkT], writes=[attmT])
                for c2 in ((0, 1) if z == 0 else (1, 0)):
                    r0 = 64 * c2
                    p5, p5T = P["p5"].next()
                    S.op("pe", lambda E, p5=p5, i=i, attm=attm, r0=r0: E.matmul(p5[:, 0:64], vtm[:, i, :], attm[:, r0:r0 + 64], start=True, stop=False),
                         reads=[vtmT, attmT], writes=[p5T])
                    S.op("pe", lambda E, p5=p5, qd=qd, r0=r0: E.matmul(p5[:, 0:64], Sb[:, :], qd[:, r0:r0 + 64], start=False, stop=True),
                         reads=[SbT, qdT], writes=[p5T])
                    if z == 0:
                        S.op("act", lambda E, p5=p5, c0=c0, r0=r0: E.copy(oT[:, c0 + r0:c0 + r0 + 64], p5[:, 0:64]), reads=[p5T], writes=[oTT])
                    else:
                        S.op("dve", lambda E, p5=p5, c0=c0, r0=r0: E.tensor_tensor(oT[:, c0 + r0:c0 + r0 + 64], oT[:, c0 + r0:c0 + r0 + 64], p5[:, 0:64], ALU.add),
                             reads=[p5T, oTT], writes=[oTT])
                    p6, p6T = P["p6"].next()
                    S.op("pe", lambda E, p6=p6, kdec=kdec, i=i, r0=r0: E.matmul(p6[:, 0:128], kdec[r0:r0 + 64, :], vtm[r0:r0 + 64, i, :], start=True, stop=True),
                         reads=[kdecT, vtmT], writes=[p6T])
                    dcol = r0 + 63 if z == 0 else r0
                    S.op("dve", lambda E, p6=p6, ed=ed, dcol=dcol: E.scalar_tensor_tensor(out=St[:, :], in0=St[:, :], scalar=ed[:, dcol:dcol + 1], in1=p6[:, 0:128],
                                                                                           op0=ALU.mult, op1=ALU.add), reads=[StT, edT, p6T], writes=[StT])
                    S.op("act", lambda E: E.copy(Sb[:, :], St[:, :]), reads=[StT], writes=[SbT])
        ob_rot = Rot([cx.sb(f"ob{i}", [128, 512], F32) for i in range(2)])
        zg_rot = Rot([cx.sb(f"zg{i}", [128, 512], F32) for i in range(2)])
        sq_rot = Rot([cx.sb(f"sq{i}", [128, 512], F32) for i in range(2)])
        rs_rot = Rot([cx.sb(f"rs{i}", [128, 512], F32) for i in range(2)])
        outs = []
        NB = min(512, S_len)
        for c0 in range(0, S_len, NB):
            ob, obT = ob_rot.next()
            zg, zgT = zg_rot.next()
            sq, sqT = sq_rot.next()
            rs, rsT = rs_rot.next()
            pn, pnT = P["pn"].next()
            S.dma("sp", lambda E, zg=zg, c0=c0: E.dma_start(out=zg[:, 0:NB], in_=zgT_d[:, c0:c0 + NB]), writes=[zgT])
            S.op("act", lambda E, zg=zg: E.activation(zg[:, 0:NB], zg[:, 0:NB], AF.Silu), reads=[zgT], writes=[zgT])
            S.op("act", lambda E, sq=sq, c0=c0: E.activation(sq[:, 0:NB], oT[:, c0:c0 + NB], AF.Square), reads=[oTT], writes=[sqT])
            S.op("pe", lambda E, pn=pn, sq=sq: E.matmul(pn[:, 0:NB], ones[:, :], sq[:, 0:NB], start=True, stop=True), reads=[onesT, sqT], writes=[pnT])
            S.op("dve", lambda E, rs=rs, pn=pn: E.tensor_scalar(rs[:, 0:NB], pn[:, 0:NB], 1.0 / 128, NORM_EPS, ALU.mult, ALU.add), reads=[pnT], writes=[rsT])
            S.op("act", lambda E, rs=rs: E.activation(rs[:, 0:NB], rs[:, 0:NB], AF.Sqrt), reads=[rsT], writes=[rsT])
            S.op("dve", lambda E, rs=rs: E.reciprocal(rs[:, 0:NB], rs[:, 0:NB]), reads=[rsT], writes=[rsT])
            S.op("dve", lambda E, ob=ob, rs=rs, c0=c0: E.scalar_tensor_tensor(out=ob[:, 0:NB], in0=oT[:, c0:c0 + NB], scalar=og[:, 0:1], in1=rs[:, 0:NB],
                                                                             op0=ALU.mult, op1=ALU.mult), reads=[oTT, ogT, rsT], writes=[obT])
            S.op("dve", lambda E, ob=ob, zg=zg: E.tensor_tensor(ob[:, 0:NB], ob[:, 0:NB], zg[:, 0:NB], ALU.mult), reads=[obT, zgT], writes=[obT])
            S.dma("sp", lambda E, ob=ob, c0=c0: E.dma_start(out=o_d[:, c0:c0 + NB], in_=ob[:, 0:NB]), reads=[obT])
            if obT not in outs:
                outs.append(obT)
        S.final_wait("sp", outs)
        S.emit(st)
        nc._stats = S.stats
    return nc


def hgrn_inputs(zT_b, j, h_lb_logits, h_onorm):
    base = 1536
    def rows(k):
        return zT_b[base + k * 512 + j * 128: base + k * 512 + (j + 1) * 128]
    zqT, zffT, zfbT, ziT, zgT = [rows(k) for k in range(5)]
    lb = h_lb_logits[:, j * 128:(j + 1) * 128]
    return dict(zqT=np.ascontiguousarray(zqT), zgT=np.ascontiguousarray(zgT),
                zfT=np.ascontiguousarray(np.stack([zffT, zfbT])),
                zf=np.ascontiguousarray(np.stack([zffT.T, zfbT.T])),
                zi=np.ascontiguousarray(ziT.T),
                lbrow=np.ascontiguousarray(np.broadcast_to(lb[None, :, :], (128, 2, 128))),
                lbcol=np.ascontiguousarray(lb.T),
                ogain=np.ascontiguousarray(h_onorm[:, None]),
                masks=hgrn_masks())


R_DECAY_SCALE = math.exp(-0.5)
R_GN_EPS = 64e-5


def build_rwkv(S_len, TB=256):
    nc = bass.Bass("TRN2", target_bir_lowering=False)
    nblk = S_len // TB
    n128 = S_len // 128
    n64 = S_len // 64
    zr_d = nc.dram_tensor("zr3T", [3, 768, S_len], F32, kind="ExternalInput").ap()
    vht_d = nc.dram_tensor("v3ht", [128, 3, n64, 64], F32, kind="ExternalInput").ap()
    vtm_d = nc.dram_tensor("v3tm", [128, 3, n128, 128], F32, kind="ExternalInput").ap()
    mucol_d = nc.dram_tensor("mucol", [128, 6, 2], F32, kind="ExternalInput").ap()
    muht_d = nc.dram_tensor("muht", [128, 2, 64], F32, kind="ExternalInput").ap()
    mutm_d = nc.dram_tensor("mutm", [128, 2, 128], F32, kind="ExternalInput").ap()
    w2_d = nc.dram_tensor("w2t", [128, 128], F32, kind="ExternalInput").ap()
    a2_d = nc.dram_tensor("a2t", [128, 128], F32, kind="ExternalInput").ap()
    g2_d = nc.dram_tensor("g2m", [128, 128], F32, kind="ExternalInput").ap()
    cols_d = nc.dram_tensor("cols", [8, 128, 1], F32, kind="ExternalInput").ap()
    gn_d = nc.dram_tensor("gnrow", [128, 2, 128], F32, kind="ExternalInput").ap()
    es_d = nc.dram_tensor("esel", [128, 64, 128], F32, kind="ExternalInput").ap()
    o_d = nc.dram_tensor("o", [S_len, 128], F32, kind="ExternalOutput").ap()
    with ExitStack() as st:
        cx = Ctx(nc, st)
        S = cx.S
        esel, eselT = cx.sb("esel", [128, 64, 128], BF16)
        vht, vhtT = cx.sb("vht", [128, 2, n64, 64], BF16)
        vtmp_rot = Rot([cx.sb(f"vtmp{i}", [128, 512], F32) for i in range(2)])
        vtmp2_rot = Rot([cx.sb(f"vtmpb{i}", [128, 512], F32) for i in range(2)])
        vtm, vtmT = cx.sb("vtm", [128, n128, 128], F32)
        ysb1 = cx.sb("ysb", [128, n128, 128], F32)
        mucol, mucolT = cx.sb("mucol", [128, 6, 3], F32)
        muht, muhtT = cx.sb("muht", [128, 3, 64], F32)
        mutm, mutmT = cx.sb("mutm", [128, 3, 128], F32)
        w2b, w2bT = cx.sb("w2b", [128, 128], BF16)
        a2b, a2bT = cx.sb("a2b", [128, 128], BF16)
        g2b, g2bT = cx.sb("g2b", [128, 128], BF16)
        cols, colsT = cx.sb("cols", [128, 8, 16], F32)
        gn, gnT = cx.sb("gn", [128, 2, 128], F32)
        bones, bonesT = cx.sb("bones", [128, 128], F32)
        hsel, hselT = cx.sb("hsel", [128, 2], F32)
        zwin, zwinT = cx.sb("zwin", [128, 256], BF16)
        Stt = [cx.sb(f"St{d}", [128, 64], F32) for d in range(2)]
        tmp_rot = [Rot([cx.sb(f"tmp{d}_{i}", [128, 64], F32) for i in range(2)]) for d in range(2)]
        t2_rot = [Rot([cx.sb(f"t2m{d}_{i}", [128, 2, 64], BF16) for i in range(2)]) for d in range(2)]
        sa_ps = [cx.ps(f"sa{d}", [128, 512]) for d in range(2)]
        sv_rot = [Rot([cx.ps(f"vbc{d}_{i}", [128, 512]) for i in range(2)]) for d in range(2)]
        yps1 = cx.ps("yps", [128, 512])
        pprep = Rot([cx.ps("pprep0", [128, 512])])

        S.dma("pool", lambda E: E.dma_start(out=esel[:, :, :], in_=es_d[:, :, :]), writes=[eselT])
        S.dma("sp", lambda E: E.dma_start(out=mucol[:, :, 0:2], in_=mucol_d[:, :, :]), writes=[mucolT])
        S.dma("sp", lambda E: E.dma_start(out=muht[:, 0:2, :], in_=muht_d[:, :, :]), writes=[muhtT])
        S.dma("sp", lambda E: E.dma_start(out=mutm[:, 0:2, :], in_=mutm_d[:, :, :]), writes=[mutmT])
        S.dma("sp", lambda E: E.dma_start(out=gn[:, :, :], in_=gn_d[:, :, :]), writes=[gnT])
        S.dma("sp", lambda E: [E.dma_start(out=cols[:, c, 0:1], in_=cols_d[c, :, :]) for c in range(8)], writes=[colsT], parts=8)
        S.dma("pool", lambda E: E.dma_start(out=w2b[:, :], in_=w2_d[:, :]), writes=[w2bT])
        S.dma("pool", lambda E: E.dma_start(out=a2b[:, :], in_=a2_d[:, :]), writes=[a2bT])
        S.dma("pool", lambda E: E.dma_start(out=g2b[:, :], in_=g2_d[:, :]), writes=[g2bT])
        S.op("dve", lambda E: E.memset(bones[:, :], 0.0), writes=[bonesT])
        S.op("dve", lambda E: E.memset(bones[0:64, 0:64], 1.0), writes=[bonesT])
        S.op("dve", lambda E: E.memset(bones[64:128, 64:128], 1.0), writes=[bonesT])
        S.op("dve", lambda E: E.memset(hsel[:, :], 0.0), writes=[hselT])
        S.op("dve", lambda E: E.memset(hsel[0:64, 0:1], 1.0), writes=[hselT])
        S.op("dve", lambda E: E.memset(hsel[64:128, 1:2], 1.0), writes=[hselT])
        S.op("dve", lambda E: E.memset(zwin[:, :], 0.0), writes=[zwinT])
        S.op("dve", lambda E: E.memset(zwin[:, 127:128], 1.0), writes=[zwinT])
        for (m, mT, sl) in ((mucol, mucolT, lambda i: mucol[:, :, i]), (muht, muhtT, lambda i: muht[:, i, :]), (mutm, mutmT, lambda i: mutm[:, i, :])):
            S.op("dve", lambda E, sl=sl: E.tensor_tensor(sl(2), sl(0), sl(1), ALU.add), reads=[mT], writes=[mT])
            S.op("dve", lambda E, sl=sl: E.tensor_scalar(sl(2), sl(2), -1.0, 1.0, ALU.mult, ALU.add), reads=[mT], writes=[mT])
        S.op("dve", lambda E: E.tensor_scalar(cols[:, 6, 0:1], cols[:, 6, 0:1], 0.5, None, ALU.mult), reads=[colsT], writes=[colsT])

        stg_rot = Rot([cx.sb(f"vstg{i}", [128, 3, 512], F32) for i in range(2)])
        for (which, dstT, src, mu_, muT_, nb_, w_) in (("ht", vhtT, vht_d, muht, muhtT, n64, 64), ("tm", vtmT, vtm_d, mutm, mutmT, n128, 128)):
            per = 512 // w_
            for b0 in range(0, nb_, per):
                bn = min(per, nb_ - b0)
                stg, stgT = stg_rot.next()
                S.dma("sp", lambda E, stg=stg, src=src, b0=b0, bn=bn, w_=w_: [E.dma_start(out=stg[:, i, 0:bn * w_].rearrange("p (b w) -> p b w", w=w_),
                                                                                            in_=src[:, i, b0:b0 + bn, :]) for i in range(3)], writes=[stgT], parts=3)
                if which == "tm":
                    acc = lambda b: vtm[:, b0 + b, :]
                    accT = vtmT
                else:
                    vt, vtT = vtmp_rot.next()
                    acc = lambda b, vt=vt, w_=w_: vt[:, b * w_:(b + 1) * w_]
                    accT = vtT
                L_ = bn * w_
                v3 = lambda ap_, w_=w_: ap_.rearrange("p (b w) -> p b w", w=w_)
                mub = lambda i, mu_=mu_, bn=bn, w_=w_: mu_[:, i, :].unsqueeze(1).broadcast_to([128, bn, w_])
                if which == "tm":
                    o3 = vtm[:, b0:b0 + bn, :]
                else:
                    o3 = v3(vt[:, 0:L_])
                S.op("dve", lambda E, o3=o3, stg=stg, L_=L_, mub=mub, v3=v3: E.tensor_tensor(o3, v3(stg[:, 0, 0:L_]), mub(2), ALU.mult), reads=[stgT, muT_], writes=[accT])
                for i in (1, 2):
                    S.op("dve", lambda E, stg=stg, L_=L_, mub=mub, v3=v3, i=i: E.tensor_tensor(v3(stg[:, i, 0:L_]), v3(stg[:, i, 0:L_]), mub(i - 1), ALU.mult),
                         reads=[stgT, muT_], writes=[stgT])
                    S.op("dve", lambda E, o3=o3, stg=stg, L_=L_, v3=v3, i=i: E.tensor_tensor(o3, o3, v3(stg[:, i, 0:L_]), ALU.add), reads=[stgT, accT], writes=[accT])
                if which == "ht":
                    v2, v2T = vtmp2_rot.next()
                    L = bn * w_
                    hi = vht[:, 0, b0:b0 + bn, :]
                    lo = vht[:, 1, b0:b0 + bn, :]
                    S.op("act", lambda E, hi=hi, vt=vt, L=L: E.copy(hi, vt[:, 0:L].rearrange("p (b w) -> p b w", w=64)), reads=[vtT], writes=[vhtT])
                    S.op("dve", lambda E, hi=hi, vt=vt, v2=v2, L=L: E.tensor_tensor(v2[:, 0:L].rearrange("p (b w) -> p b w", w=64), vt[:, 0:L].rearrange("p (b w) -> p b w", w=64), hi, ALU.subtract),
                         reads=[vtT, vhtT], writes=[v2T])
                    S.op("act", lambda E, lo=lo, v2=v2, L=L: E.copy(lo, v2[:, 0:L].rearrange("p (b w) -> p b w", w=64)), reads=[v2T], writes=[vhtT])

        def make_set(tag):
            names = ["raw", "r", "k", "wl", "al", "gl", "kx", "sq", "rn", "kk", "thb", "alb", "sgw", "a0", "a1", "w0", "w1", "nb0", "nb1", "ke0", "ke1", "t1"]
            d = {}
            for nm in names:
                if nm == "raw":
                    d[nm] = cx.sb(f"{tag}_{nm}", [128, 3, TB], F32)
                elif nm in ("thb", "alb"):
                    d[nm] = cx.sb(f"{tag}_{nm}", [128, TB], BF16)
                else:
                    d[nm] = cx.sb(f"{tag}_{nm}", [128, TB], F32)
            return d

        def shift(P_, tile_idx, out_name, c0):
            raw, rawT = P_["raw"]
            o, oT = P_[out_name]
            S.dma("sp", lambda E: [E.dma_start(out=raw[:, i, :], in_=zr_d[i, tile_idx * 128:(tile_idx + 1) * 128, c0:c0 + TB]) for i in range(3)], writes=[rawT], parts=3)
            S.op("act", lambda E: E.activation(o[:, :], raw[:, 0, :], AF.Copy, scale=mucol[:, tile_idx, 2:3]), reads=[rawT, mucolT], writes=[oT])
            for i in (1, 2):
                S.op("dve", lambda E, i=i: E.scalar_tensor_tensor(out=o[:, :], in0=raw[:, i, :], scalar=mucol[:, tile_idx, i - 1:i], in1=o[:, :], op0=ALU.mult, op1=ALU.add),
                     reads=[rawT, mucolT, oT], writes=[oT])

        def prep(P_, blk, dirs, want_gl=False):
            c0 = blk * TB
            shift(P_, 0, "r", c0)
            shift(P_, 1, "k", c0)
            shift(P_, 3, "wl", c0)
            shift(P_, 4, "al", c0)
            if want_gl:
                shift(P_, 5, "gl", c0)
            r, rT = P_["r"]; k, kT = P_["k"]; wl, wlT = P_["wl"]; al, alT = P_["al"]
            kx, kxT = P_["kx"]; sq, sqT = P_["sq"]; rn, rnT = P_["rn"]; kk, kkT = P_["kk"]
            thb, thbT = P_["thb"]; alb, albT = P_["alb"]; sgw, sgwT = P_["sgw"]; t1, t1T = P_["t1"]
            S.op("dve", lambda E: E.tensor_scalar(kx[:, :], k[:, :], cols[:, 4, 0:1], None, ALU.mult), reads=[kT, colsT], writes=[kxT])
            S.op("act", lambda E: E.activation(sq[:, :], kx[:, :], AF.Square), reads=[kxT], writes=[sqT])
            pp, ppT = pprep.next()
            S.op("pe", lambda E: E.matmul(pp[:, 0:TB], bones[:, :], sq[:, :], start=True, stop=True), reads=[bonesT, sqT], writes=[ppT])
            S.op("dve", lambda E: E.tensor_scalar(rn[:, :], pp[:, 0:TB], 1e-12, None, ALU.add), reads=[ppT], writes=[rnT])
            S.op("act", lambda E: E.activation(rn[:, :], rn[:, :], AF.Sqrt), reads=[rnT], writes=[rnT])
            S.op("dve", lambda E: E.reciprocal(rn[:, :], rn[:, :]), reads=[rnT], writes=[rnT])
            S.op("dve", lambda E: E.tensor_tensor(kk[:, :], kx[:, :], rn[:, :], ALU.mult), reads=[kxT, rnT], writes=[kkT])
            S.op("act", lambda E: E.activation(thb[:, :], wl[:, :], AF.Tanh), reads=[wlT], writes=[thbT])
            S.op("act", lambda E: E.copy(alb[:, :], al[:, :]), reads=[alT], writes=[albT])
            for d in dirs:
                a_, aT_ = P_[f"a{d}"]; w_, wT_ = P_[f"w{d}"]; nb_, nbT_ = P_[f"nb{d}"]; ke_, keT_ = P_[f"ke{d}"]
                pw, pwT = pprep.next()
                S.op("pe", lambda E, d=d, pw=pw: E.matmul(pw[:, 0:TB], w2b[64 * d:64 * d + 64, :], thb[64 * d:64 * d + 64, :], start=True, stop=True), reads=[w2bT, thbT], writes=[pwT])
                S.op("act", lambda E, d=d, pw=pw: E.activation(sgw[:, :], pw[:, 0:TB], AF.Sigmoid, bias=cols[:, d, 0:1]), reads=[pwT, colsT], writes=[sgwT])
                S.op("act", lambda E, w_=w_: E.activation(w_[:, :], sgw[:, :], AF.Exp, scale=-R_DECAY_SCALE), reads=[sgwT], writes=[wT_])
                pa, paT = pprep.next()
                S.op("pe", lambda E, d=d, pa=pa: E.matmul(pa[:, 0:TB], a2b[64 * d:64 * d + 64, :], alb[64 * d:64 * d + 64, :], start=True, stop=True), reads=[a2bT, albT], writes=[paT])
                S.op("act", lambda E, d=d, pa=pa, a_=a_: E.activation(a_[:, :], pa[:, 0:TB], AF.Sigmoid, bias=cols[:, 2 + d, 0:1]), reads=[paT, colsT], writes=[aT_])
                S.op("dve", lambda E, a_=a_: E.tensor_scalar(t1[:, :], a_[:, :], cols[:, 5, 0:1], cols[:, 5, 0:1], ALU.mult, ALU.subtract), reads=[aT_, colsT], writes=[t1T])
                S.op("dve", lambda E, ke_=ke_: E.scalar_tensor_tensor(out=ke_[:, :], in0=t1[:, :], scalar=1.0, in1=k[:, :], op0=ALU.add, op1=ALU.mult), reads=[t1T, kT], writes=[keT_])
                S.op("dve", lambda E, nb_=nb_, a_=a_: E.scalar_tensor_tensor(out=nb_[:, :], in0=kk[:, :], scalar=-1.0, in1=a_[:, :], op0=ALU.mult, op1=ALU.mult), reads=[kkT, aT_], writes=[nbT_])

        sets = [make_set("pf"), make_set("pb")]
        for d in range(2):
            S.op("dve", lambda E, d=d: E.memset(Stt[d][0][:, :], 0.0), writes=[Stt[d][1]])
            for (t2, t2T) in t2_rot[d].items:
                S.op("dve", lambda E, t2=t2: E.memset(t2[:, :, :], 0.0), writes=[t2T])

        St2 = [[Stt[d], cx.sb(f"StB{d}", [128, 64], F32)] for d in range(2)]
        kkb_rot = [Rot([cx.sb(f"kkb{d}_{i}", [128, 128], F32) for i in range(5)]) for d in range(2)]
        zwin2, zwin2T = cx.sb("zwin2", [128, 2, 256], BF16)
        S.op("dve", lambda E: E.memset(zwin2[:, :, :], 0.0), writes=[zwin2T])
        S.op("dve", lambda E: E.memset(zwin2[0:64, 0, 127:128], 1.0), writes=[zwin2T])
        S.op("dve", lambda E: E.memset(zwin2[64:128, 1, 127:128], 1.0), writes=[zwin2T])
        for d in range(2):
            S.op("dve", lambda E, d=d: E.memset(St2[d][1][0][:, :], 0.0), writes=[St2[d][1][1]])
        steps = []
        for n in range(nblk):
            for i in range(TB):
                steps.append((n, i))

        def tinfo(gi, d):
            n, i = steps[gi]
            if d == 0:
                return n * TB + i, i
            return (nblk - 1 - n) * TB + TB - 1 - i, TB - 1 - i

        sv_cur = [None, None]

        def emit_vbc(gi):
            for d in range(2):
                t, c = tinfo(gi, d)
                sv, svT = sv_rot[d].next()
                sv_cur[d] = (sv, svT)
                S.op("pe", lambda E, sv=sv, t=t: E.matmul(sv[:, 0:64], esel[:, t % 64, :], vht[:, 0, t // 64, :], start=True, stop=False), reads=[eselT, vhtT], writes=[svT])
                S.op("pe", lambda E, sv=sv, t=t: E.matmul(sv[:, 0:64], esel[:, t % 64, :], vht[:, 1, t // 64, :], start=False, stop=True), reads=[eselT, vhtT], writes=[svT])

        pending = []

        def flush_pending():
            for (d, t, c, new, newT, P_) in pending:
                r, rT = P_["r"]
                t2, t2T = t2_rot[d].next()
                S.op("act", lambda E, t2=t2, new=new, r=r, c=c: E.activation(t2[:, 0, :], new[:, :], AF.Copy, scale=r[:, c:c + 1]), reads=[newT, rT], writes=[t2T])
                tl = t % 128
                first = (tl == 0) if d == 0 else (tl == 127)
                last = (tl == 127) if d == 0 else (tl == 0)
                yp, ypT = yps1
                yo = 128 * d
                for h in range(2):
                    S.op("pe", lambda E, yp=yp, t2=t2, tl=tl, first=first, last=last, h=h, yo=yo, d=d: E.matmul(yp[:, yo + 64 * h:yo + 64 * h + 64], zwin2[:, h, 127 - tl:255 - tl], t2[:, 0, :],
                                                                                                     start=(first and h == 0 and d == 0), stop=last, skip_group_check=True), reads=[zwin2T, t2T], writes=[ypT])
                if last:
                    ys, ysT = ysb1
                    ti = t // 128
                    if (d == 0) == (ti < (n128 + 1) // 2):
                        S.op("act", lambda E, ys=ys, yp=yp, ti=ti, yo=yo: E.copy(ys[:, ti, :], yp[:, yo:yo + 128]), reads=[ypT], writes=[ysT])
                    else:
                        S.op("dve", lambda E, ys=ys, yp=yp, ti=ti, yo=yo: E.tensor_tensor(ys[:, ti, :], ys[:, ti, :], yp[:, yo:yo + 128], ALU.add), reads=[ypT, ysT], writes=[ysT])
            pending.clear()

        kkb_q = [{}, {}]

        def emit_kkb(gi):
            for d in range(2):
                t, c = tinfo(gi, d)
                tmp, tmpT = kkb_rot[d].next()
                kk, kkT = sets[d]["kk"]
                S.op("act", lambda E, tmp=tmp, kk=kk, c=c: E.activation(tmp[:, :], bones[:, :], AF.Copy, scale=kk[:, c:c + 1]), reads=[bonesT, kkT], writes=[tmpT])
                kkb_q[d][gi] = (tmp, tmpT)

        LOOK = 2
        for gi, (n, i) in enumerate(steps):
            if i == 0:
                flush_pending()
                prep(sets[0], n, (0,))
                prep(sets[1], nblk - 1 - n, (1,))
                for la in range(min(LOOK, TB)):
                    emit_kkb(gi + la)
            if gi == 0:
                emit_vbc(0)
            par = gi % 2
            svs = list(sv_cur)
            info = []
            for d in range(2):
                t, c = tinfo(gi, d)
                cur, curT = St2[d][par]
                tmp, tmpT = kkb_q[d].pop(gi)
                info.append((t, c, cur, curT, tmp, tmpT))
            for d in range(2):
                t, c, cur, curT, tmp, tmpT = info[d]
                sa, saT = sa_ps[d]
                S.op("pe", lambda E, sa=sa, tmp=tmp, cur=cur: E.matmul(sa[:, 0:64], tmp[:, :], cur[:, :], start=True, stop=True), reads=[tmpT, curT], writes=[saT])
            flush_pending()
            if i + LOOK < TB:
                emit_kkb(gi + LOOK)
            if gi + 1 < len(steps):
                emit_vbc(gi + 1)
            for phase in range(3):
                for d in range(2):
                    t, c, cur, curT, tmp, tmpT = info[d]
                    new, newT = St2[d][1 - par]
                    sv, svT = svs[d]
                    w_, wT_ = sets[d][f"w{d}"]; ke_, keT_ = sets[d][f"ke{d}"]; nb_, nbT_ = sets[d][f"nb{d}"]
                    if phase == 0:
                        S.op("dve", lambda E, new=new, cur=cur, w_=w_, c=c: E.tensor_scalar(new[:, :], cur[:, :], w_[:, c:c + 1], None, ALU.mult), reads=[curT, wT_], writes=[newT])
                    elif phase == 1:
                        S.op("dve", lambda E, new=new, sv=sv, ke_=ke_, c=c: E.scalar_tensor_tensor(out=new[:, :], in0=sv[:, 0:64], scalar=ke_[:, c:c + 1], in1=new[:, :], op0=ALU.mult, op1=ALU.add),
                             reads=[svT, keT_, newT], writes=[newT])
                    else:
                        sa, saT = sa_ps[d]
                        S.op("dve", lambda E, new=new, sa=sa, nb_=nb_, c=c: E.scalar_tensor_tensor(out=new[:, :], in0=sa[:, 0:64], scalar=nb_[:, c:c + 1], in1=new[:, :], op0=ALU.mult, op1=ALU.add),
                             reads=[saT, nbT_, newT], writes=[newT])
                        pending.append((d, t, c, new, newT, sets[d]))
        flush_pending()

        es = sets[0]
        ept = {nm: Rot([cx.sb(f"ep_{nm}{i}", [128, 128], dt) for i in range(2)]) for nm, dt in
               (("y", F32), ("yc", F32), ("junk", F32), ("g", F32), ("PT", F32), ("sglb", BF16), ("o", F32))}
        sm_rot = Rot([cx.sb(f"ep_sm{i}", [128, 16], F32) for i in range(2)])
        outs = []
        for blk in range(nblk):
            prep(es, blk, (0, 1), want_gl=True)
            r, rT = es["r"]; gl, glT = es["gl"]; ke0, ke0T = es["ke0"]; ke1, ke1T = es["ke1"]; t1, t1T = es["t1"]
            S.op("dve", lambda E: E.tensor_tensor(t1[:, :], ke0[:, :], ke1[:, :], ALU.add), reads=[ke0T, ke1T], writes=[t1T])
            S.op("dve", lambda E: E.scalar_tensor_tensor(out=t1[:, :], in0=t1[:, :], scalar=cols[:, 6, 0:1], in1=r[:, :], op0=ALU.mult, op1=ALU.mult), reads=[t1T, colsT, rT], writes=[t1T])
            for sub in range(TB // 128):
                ti = blk * (TB // 128) + sub
                cs = slice(sub * 128, (sub + 1) * 128)
                y, yT = ept["y"].next(); yc, ycT = ept["yc"].next(); junk, junkT = ept["junk"].next(); g, gT_ = ept["g"].next()
                sglb, sglbT = ept["sglb"].next(); ob, obT = ept["o"].next(); sm, smT = sm_rot.next()
                S.op("act", lambda E, sglb=sglb, cs=cs: E.activation(sglb[:, :], gl[:, cs], AF.Sigmoid), reads=[glT], writes=[sglbT])
                pg, pgT = pprep.next()
                S.op("pe", lambda E, pg=pg, sglb=sglb: E.matmul(pg[:, 0:128], sglb[:, :], g2b[:, :], start=True, stop=True), reads=[sglbT, g2bT], writes=[pgT])
                S.op("act", lambda E, g=g, pg=pg: E.copy(g[:, :], pg[:, 0:128]), reads=[pgT], writes=[gT_])
                pb_, pbT_ = pprep.next()
                S.op("pe", lambda E, pb_=pb_, cs=cs: E.matmul(pb_[:, 0:2], t1[:, cs], hsel[:, :], start=True, stop=True), reads=[t1T, hselT], writes=[pbT_])
                S.op("dve", lambda E, sm=sm, pb_=pb_: E.tensor_copy(sm[:, 8:10], pb_[:, 0:2]), reads=[pbT_], writes=[smT])
                S.op("dve", lambda E, y=y, ti=ti: E.tensor_copy(y[:, :], ysb1[0][:, ti, :]), reads=[ysb1[1]], writes=[yT])
                for h in range(2):
                    hs = slice(64 * h, 64 * h + 64)
                    S.op("dve", lambda E, sm=sm, y=y, hs=hs, h=h: E.reduce_sum(sm[:, h:h + 1], y[:, hs], AX.X), reads=[yT], writes=[smT])
                    S.op("dve", lambda E, sm=sm, h=h: E.tensor_scalar(sm[:, h:h + 1], sm[:, h:h + 1], 1.0 / 64, None, ALU.mult), reads=[smT], writes=[smT])
                    S.op("dve", lambda E, yc=yc, y=y, sm=sm, hs=hs, h=h: E.tensor_scalar(yc[:, hs], y[:, hs], sm[:, h:h + 1], None, ALU.subtract), reads=[yT, smT], writes=[ycT])
                    S.op("act", lambda E, junk=junk, yc=yc, sm=sm, hs=hs, h=h: E.activation(junk[:, hs], yc[:, hs], AF.Square, accum_out=sm[:, 2 + h:3 + h]), reads=[ycT], writes=[junkT, smT])
                    S.op("dve", lambda E, sm=sm, h=h: E.tensor_scalar(sm[:, 2 + h:3 + h], sm[:, 2 + h:3 + h], 1.0 / 64, R_GN_EPS, ALU.mult, ALU.add), reads=[smT], writes=[smT])
                    S.op("act", lambda E, sm=sm, h=h: E.activation(sm[:, 2 + h:3 + h], sm[:, 2 + h:3 + h], AF.Sqrt), reads=[smT], writes=[smT])
                    S.op("dve", lambda E, sm=sm, h=h: E.reciprocal(sm[:, 4 + h:5 + h], sm[:, 2 + h:3 + h]), reads=[smT], writes=[smT])
                    S.op("dve", lambda E, yc=yc, sm=sm, hs=hs, h=h: E.scalar_tensor_tensor(out=yc[:, hs], in0=yc[:, hs], scalar=sm[:, 4 + h:5 + h], in1=gn[:, 0, hs], op0=ALU.mult, op1=ALU.mult),
                         reads=[ycT, smT, gnT], writes=[ycT])
                    S.op("dve", lambda E, yc=yc, hs=hs: E.tensor_tensor(yc[:, hs], yc[:, hs], gn[:, 1, hs], ALU.add), reads=[ycT, gnT], writes=[ycT])
                    S.op("dve", lambda E, yc=yc, sm=sm, hs=hs, h=h, ti=ti: E.scalar_tensor_tensor(out=yc[:, hs], in0=vtm[:, ti, hs], scalar=sm[:, 8 + h:9 + h], in1=yc[:, hs], op0=ALU.mult, op1=ALU.add),
                         reads=[vtmT, smT, ycT], writes=[ycT])
                S.op("dve", lambda E, ob=ob, yc=yc, g=g: E.tensor_tensor(ob[:, :], yc[:, :], g[:, :], ALU.mult), reads=[ycT, gT_], writes=[obT])
                S.dma("sp", lambda E, ob=ob, ti=ti: E.dma_start(out=o_d[ti * 128:(ti + 1) * 128, :], in_=ob[:, :]), reads=[obT])
                if obT not in outs:
                    outs.append(obT)
        S.final_wait("sp", outs)
        S.emit(st)
        nc._stats = S.stats
    return nc


def rwkv_consts():
    es = np.zeros((128, 64, 128), np.float32)
    for h in range(2):
        for t in range(64):
            es[h * 64 + t, t, h * 64:(h + 1) * 64] = 1.0
    return es


def rwkv_inputs(zT_b, j, r_mu, r_w0, r_w2, r_a0, r_a2, r_g2, r_kk, r_ka, r_rk, r_gn_g, r_gn_b):
    S_len = zT_b.shape[1]
    base = 1536 + 2560 + 512
    zr = zT_b[base:base + 1920]
    my = np.arange(j * 128, (j + 1) * 128)
    rows = np.concatenate([my, 512 + my, 1024 + my, np.arange(1536, 1920)])
    cur = zr[rows]
    prev = np.zeros_like(cur); prev[:, 1:] = cur[:, :-1]
    nxt = np.zeros_like(cur); nxt[:, :-1] = cur[:, 1:]
    zr3T = np.ascontiguousarray(np.stack([cur, prev, nxt]))
    v3 = zr3T[:, 256:384, :]
    n64, n128 = S_len // 64, S_len // 128
    v3ht = np.ascontiguousarray(v3.reshape(3, 2, 64, n64, 64).transpose(1, 4, 0, 3, 2).reshape(128, 3, n64, 64))
    v3tm = np.ascontiguousarray(v3.reshape(3, 128, n128, 128).transpose(3, 0, 2, 1))
    mu = r_mu[:, rows]
    mucol = np.ascontiguousarray(mu.reshape(2, 6, 128).transpose(2, 1, 0))
    muv = mu[:, 256:384]
    muht = np.ascontiguousarray(np.repeat(muv.reshape(2, 2, 64).transpose(1, 0, 2), 64, axis=0))
    mutm = np.ascontiguousarray(np.broadcast_to(muv[None], (128, 2, 128)))
    w2t = np.ascontiguousarray(np.concatenate([r_w2[0][:, my], r_w2[1][:, my]], axis=0))
    a2t = np.ascontiguousarray(np.concatenate([r_a2[0][:, my], r_a2[1][:, my]], axis=0))
    g2m = np.ascontiguousarray(r_g2[:, my])
    cols = np.ascontiguousarray(np.stack([r_w0[0][my], r_w0[1][my], r_a0[0][my], r_a0[1][my], r_kk[my], r_ka[my], r_rk[my], np.zeros(128, np.float32)], axis=0)[:, :, None])
    gnrow = np.ascontiguousarray(np.broadcast_to(np.stack([r_gn_g[my], r_gn_b[my]])[None], (128, 2, 128)))
    return dict(zr3T=zr3T, v3ht=v3ht, v3tm=v3tm, mucol=mucol, muht=muht, mutm=mutm, w2t=w2t, a2t=a2t, g2m=g2m,
                cols=cols.astype(np.float32), gnrow=gnrow, esel=rwkv_consts())


POOL_WINDOWS = (2, 4, 8, 16)


def build_pool(ntok, NB=512):
    nc = bass.Bass("TRN2", target_bir_lowering=False)
    z_d = nc.dram_tensor("zc", [4, 128, ntok + 16], F32, kind="ExternalInput").ap()
    ci_d = nc.dram_tensor("cinv", [128, 4, ntok], F32, kind="ExternalInput").ap()
    cw_d = nc.dram_tensor("cw", [128, 4, 128], F32, kind="ExternalInput").ap()
    cc_d = nc.dram_tensor("ccol", [128, 4, 2], F32, kind="ExternalInput").ap()
    o_d = nc.dram_tensor("oT", [4, 128, ntok], F32, kind="ExternalOutput").ap()
    with ExitStack() as st:
        cx = Ctx(nc, st)
        S = cx.S
        cw, cwT = cx.sb("cw", [128, 4, 128], BF16)
        cc, ccT = cx.sb("cc", [128, 4, 2], F32)
        x_rot = Rot([cx.sb(f"x{i}", [128, NB + 16], F32) for i in range(2)])
        s_rot = Rot([cx.sb(f"s{i}", [128, NB + 16], F32) for i in range(3)])
        ci_rot = Rot([cx.sb(f"ci{i}", [128, NB], F32) for i in range(2)])
        d_rot = Rot([cx.sb(f"d{i}", [128, NB], BF16) for i in range(2)])
        ob_rot = Rot([cx.sb(f"ob{i}", [128, NB], F32) for i in range(2)])
        pp = Rot([cx.ps(f"pp{i}", [128, 512]) for i in range(2)])
        S.dma("pool", lambda E: E.dma_start(out=cw[:, :, :], in_=cw_d[:, :, :]), writes=[cwT])
        S.dma("sp", lambda E: E.dma_start(out=cc[:, :, :], in_=cc_d[:, :, :]), writes=[ccT])
        outs = []
        for g, w in enumerate(POOL_WINDOWS):
            for t0 in range(0, ntok, NB):
                x, xT = x_rot.next()
                ci, ciT = ci_rot.next()
                S.dma("sp", lambda E, x=x, g=g, t0=t0: E.dma_start(out=x[:, :], in_=z_d[g, :, t0:t0 + NB + 16]), writes=[xT])
                S.dma("sp", lambda E, ci=ci, g=g, t0=t0: E.dma_start(out=ci[:, :], in_=ci_d[:, g, t0:t0 + NB]), writes=[ciT])
                cur, curT = x, xT
                L = NB + 16
                step = 1
                while step < w:
                    nx, nxT = s_rot.next()
                    L2 = L - step
                    S.op("dve", lambda E, nx=nx, cur=cur, L2=L2, step=step: E.tensor_tensor(nx[:, 0:L2], cur[:, 0:L2], cur[:, step:step + L2], ALU.add),
                         reads=[curT], writes=[nxT])
                    cur, curT, L = nx, nxT, L2
                    step *= 2
                off = 8 - w // 2
                m, mT = s_rot.next()
                S.op("dve", lambda E, m=m, cur=cur, ci=ci, off=off: E.tensor_tensor(m[:, 0:NB], cur[:, off:off + NB], ci[:, :], ALU.mult), reads=[curT, ciT], writes=[mT])
                d, dT = d_rot.next()
                S.op("dve", lambda E, d=d, m=m, x=x: E.tensor_tensor(d[:, :], m[:, 0:NB], x[:, 8:8 + NB], ALU.subtract), reads=[mT, xT], writes=[dT])
                ps, psT = pp.next()
                S.op("pe", lambda E, ps=ps, d=d, g=g: E.matmul(ps[:, 0:NB], cw[:, g, :], d[:, :], start=True, stop=True), reads=[cwT, dT], writes=[psT])
                ob, obT = ob_rot.next()
                S.op("dve", lambda E, ob=ob, ps=ps, g=g: E.tensor_scalar(ob[:, :], ps[:, 0:NB], cc[:, g, 0:1], cc[:, g, 1:2], ALU.add, ALU.mult), reads=[psT, ccT], writes=[obT])
                S.dma("sp", lambda E, ob=ob, g=g, t0=t0: E.dma_start(out=o_d[g, :, t0:t0 + NB], in_=ob[:, :]), reads=[obT])
                if obT not in outs:
                    outs.append(obT)
        S.final_wait("sp", outs)
        S.emit(st)
        nc._stats = S.stats
    return nc


def pool_inputs(zT_b, q, ntok, S_len, c_w, c_b, c_scale):
    base = 1536 + 2560
    zc = zT_b[base:base + 512]
    pad = np.zeros((512, S_len + 16), np.float32)
    pad[:, 8:8 + S_len] = zc
    t0 = q * ntok
    zcp = np.ascontiguousarray(pad[:, t0:t0 + ntok + 16].reshape(4, 128, ntok + 16))
    t = np.arange(t0, t0 + ntok)
    cinv = np.zeros((4, ntok), np.float32)
    for g, w in enumerate(POOL_WINDOWS):
        lo = np.clip(t - w // 2, 0, S_len - 1)
        hi = np.clip(t + (w - w // 2 - 1), 0, S_len - 1)
        cinv[g] = 1.0 / (hi - lo + 1).astype(np.float32)
    cinv = np.ascontiguousarray(np.broadcast_to(cinv[None], (128, 4, ntok)))
    cw = np.ascontiguousarray(c_w.transpose(1, 0, 2))
    ccol = np.ascontiguousarray(np.stack([c_b.reshape(4, 128).T, c_scale.reshape(4, 128).T], axis=2))
    return dict(zc=zcp, cinv=cinv, cw=cw, ccol=ccol)


_PROGS = {}


def _prog(key, fn):
    if key not in _PROGS:
        _PROGS[key] = fn()
    return _PROGS[key]


def _run(nc, in_maps):
    res = run_bass_kernel_spmd(nc, in_maps, core_ids=list(range(NCORES)))
    return res.results


def _gl(g):
    return np.ascontiguousarray(np.asarray(g, np.float32).reshape(16, 128).T)


def kernel(x, p, mix_norm_g, w_in, w_out, rel_bias, a_qnorm, a_knorm, a_lambda, a_subln,
           h_lb_logits, h_onorm, c_w, c_b, c_scale, r_mu, r_w0, r_w2, r_a0, r_a2, r_g2,
           r_kk, r_ka, r_rk, r_gn_g, r_gn_b, mlp_norm_g, w_up, w_down, ple_norm_g, w_ple, w_ple_gate):
    f = lambda a: np.asarray(a, np.float32)
    x = f(x); p = f(p)
    B, S_len, D = x.shape
    depth = w_in.shape[0]
    ntok = B * S_len // NCORES
    qpb = S_len // ntok
    hT = np.ascontiguousarray(x.reshape(B * S_len, D).T)
    for li in range(depth):
        nc1 = _prog(("proj", ntok), lambda: build_proj(ntok, N_IN))
        g1 = _gl(mix_norm_g[li])
        wi = np.ascontiguousarray(f(w_in[li]))
        res = _run(nc1, [dict(hT=np.ascontiguousarray(hT[:, c * ntok:(c + 1) * ntok]), w=wi, g=g1) for c in range(NCORES)])
        zT = np.concatenate([res[c]["zT"] for c in range(NCORES)], axis=1)
        zTb = [zT[:, b * S_len:(b + 1) * S_len] for b in range(B)]
        mixT = np.empty((D, B * S_len), np.float32)
        nca = _prog(("attn", S_len, li), lambda: build_attn(S_len, li))
        res = _run(nca, [attn_inputs(zTb[c // 4], c % 4, f(rel_bias), f(a_qnorm[li]), f(a_knorm[li]), f(a_lambda[li]), f(a_subln[li])) for c in range(NCORES)])
        for c in range(NCORES):
            b, j = c // 4, c % 4
            mixT[j * 128:(j + 1) * 128, b * S_len:(b + 1) * S_len] = res[c]["o"].T
        ncb = _prog(("hgrn", S_len, min(li, 1)), lambda: build_hgrn(S_len, li))
        lbl = f(h_lb_logits)[[0, li]] if li > 0 else f(h_lb_logits)[[0, 0]]
        res = _run(ncb, [hgrn_inputs(zTb[c // 4], c % 4, lbl, f(h_onorm[li])) for c in range(NCORES)])
        for c in range(NCORES):
            b, j = c // 4, c % 4
            mixT[512 + j * 128:512 + (j + 1) * 128, b * S_len:(b + 1) * S_len] = res[c]["oT"]
        ncc = _prog(("pool", ntok), lambda: build_pool(ntok))
        res = _run(ncc, [pool_inputs(zTb[c // qpb], c % qpb, ntok, S_len, f(c_w[li]), f(c_b[li]), f(c_scale[li])) for c in range(NCORES)])
        for c in range(NCORES):
            mixT[1024:1536, c * ntok:(c + 1) * ntok] = res[c]["oT"].reshape(512, ntok)
        ncd = _prog(("rwkv", S_len), lambda: build_rwkv(S_len))
        res = _run(ncd, [rwkv_inputs(zTb[c // 4], c % 4, f(r_mu[li]), f(r_w0[li]), f(r_w2[li]), f(r_a0[li]), f(r_a2[li]), f(r_g2[li]),
                                     f(r_kk[li]), f(r_ka[li]), f(r_rk[li]), f(r_gn_g[li]), f(r_gn_b[li])) for c in range(NCORES)])
        for c in range(NCORES):
            b, j = c // 4, c % 4
            mixT[1536 + j * 128:1536 + (j + 1) * 128, b * S_len:(b + 1) * S_len] = res[c]["o"].T
        nc3 = _prog(("ffn", ntok), lambda: build_ffn(ntok))
        pT = np.ascontiguousarray(p[li].reshape(B * S_len, PLE_DIM).T)
        wts = dict(w_out=np.ascontiguousarray(f(w_out[li])), w_up=np.ascontiguousarray(f(w_up[li])), w_down=np.ascontiguousarray(f(w_down[li])),
                   w_gate=np.ascontiguousarray(f(w_ple_gate[li])), w_ple=np.ascontiguousarray(f(w_ple[li])),
                   g_mlp=_gl(mlp_norm_g[li]), g_ple=_gl(ple_norm_g[li]))
        res = _run(nc3, [dict(hT=np.ascontiguousarray(hT[:, c * ntok:(c + 1) * ntok]), mixT=np.ascontiguousarray(mixT[:, c * ntok:(c + 1) * ntok]),
                              pT=np.ascontiguousarray(pT[:, c * ntok:(c + 1) * ntok]), **wts) for c in range(NCORES)])
        hT = np.concatenate([res[c]["oT"] for c in range(NCORES)], axis=1)
    return np.ascontiguousarray(hT.T).reshape(B, S_len, D).astype(np.float32)
```

```python
import math
from contextlib import ExitStack
import numpy as np
import concourse.bass as bass
import concourse.mybir as mybir
from concourse.bass_utils import run_bass_kernel_spmd

F32 = mybir.dt.float32
BF16 = mybir.dt.bfloat16
AF = mybir.ActivationFunctionType
ALU = mybir.AluOpType
AX = mybir.AxisListType

D_MODEL = 2048
D_FF = 8192
PLE_DIM = 256
N_IN = 6528
NORM_EPS = 1e-6
NCORES = 8


class T:
    __slots__ = ("name", "w", "rs", "dsem", "dcount", "psum")

    def __init__(self, name, psum=False):
        self.name = name
        self.psum = psum
        self.w = None
        self.rs = []
        self.dsem = None
        self.dcount = 0


class Sched:
    ENG = ("pe", "dve", "act", "pool", "sp")

    def __init__(self, nc):
        self.nc = nc
        self.ops = []
        self.e = {"pe": nc.tensor, "dve": nc.vector, "act": nc.scalar, "pool": nc.gpsimd, "sp": nc.sync}

    def op(self, eng, fn, reads=(), writes=()):
        self.ops.append((eng, False, fn, tuple(reads), tuple(writes), None))

    def dma(self, eng, fn, reads=(), writes=(), parts=1):
        self.ops.append((eng, True, fn, tuple(reads), tuple(writes), parts))

    def final_wait(self, eng, tiles):
        self.ops.append((eng, False, lambda E: E.nop(), (), tuple(tiles), None))

    def emit(self, stack):
        nc = self.nc
        import os
        mx = int(os.environ.get("SCHED_MAXOPS", "0"))
        if mx:
            self.ops = self.ops[:mx]
            last = self.ops[-1]
            print("LAST OP", last[0], last[1], [t.name for t in last[3]], [t.name for t in last[4]])
        ops = self.ops
        n = len(ops)
        seq = {e: 0 for e in self.ENG}
        deps = [None] * n
        dma_tiles = []
        for i, (eng, is_dma, fn, reads, writes, parts) in enumerate(ops):
            d = []
            if is_dma:
                t0 = writes[0] if writes else reads[0]
                me = ("d", t0, t0.dcount + parts)
            else:
                seq[eng] += 1
                me = ("c", eng, seq[eng])
            for t in reads:
                if t.w is not None:
                    d.append(t.w)
                if t.psum:
                    d.extend(r for r in t.rs if r[0] == "c" and r[1] != eng)
            for t in writes:
                if t.w is not None:
                    d.append(t.w)
                d.extend(t.rs)
            if eng == "pe":
                d = [x for x in d if not (x[0] == "c" and x[1] == "pe")]
            if is_dma:
                t0.dcount += parts
                if t0 not in dma_tiles:
                    dma_tiles.append(t0)
            for t in reads:
                t.rs.append(me)
            for t in writes:
                t.w = me
                t.rs = []
            deps[i] = (d, me)
        need = {e: set() for e in self.ENG}
        for i in range(n):
            for dd in deps[i][0]:
                if dd[0] == "c":
                    need[dd[1]].add(dd[2])
        semval = {}
        for e in self.ENG:
            semval[e] = {q: k + 1 for k, q in enumerate(sorted(need[e]))}
        csem = {e: stack.enter_context(nc.semaphore("c_" + e)) for e in self.ENG if need[e]}
        for t in dma_tiles:
            t.dsem = stack.enter_context(nc.semaphore("d_" + t.name))
        waited = {e: {} for e in self.ENG}
        nwaits = 0
        for i, (eng, is_dma, fn, reads, writes, parts) in enumerate(ops):
            E = self.e[eng]
            d, me = deps[i]
            req = {}
            for dd in d:
                if dd[0] == "c":
                    key = ("c", dd[1]); val = semval[dd[1]][dd[2]]; sem = csem[dd[1]]
                else:
                    key = ("d", dd[1].name); val = 16 * dd[2]; sem = dd[1].dsem
                if req.get(key, (None, 0))[1] < val:
                    req[key] = (sem, val)
            for key, (sem, val) in req.items():
                if waited[eng].get(key, 0) >= val:
                    continue
                E.wait_ge(sem, val)
                if os.environ.get("SCHED_DBG") and i >= int(os.environ["SCHED_DBG"]):
                    print("  op", i, eng, "waits", key, val)
                waited[eng][key] = val
                nwaits += 1
            ins = fn(E)
            if is_dma:
                if not isinstance(ins, (list, tuple)):
                    ins = [ins]
                assert len(ins) == parts, (len(ins), parts)
                for x in ins:
                    x.then_inc(me[1].dsem, 16)
            elif me[2] in need[eng]:
                ins.then_inc(csem[eng], 1)
                if os.environ.get("SCHED_DBG") and i >= int(os.environ["SCHED_DBG"]):
                    print("  op", i, eng, "incs ->", semval[eng][me[2]])
        self.stats = dict(n_ops=n, n_waits=nwaits, n_sems=len(csem) + len(dma_tiles))


class Ctx:
    def __init__(self, nc, stack):
        self.nc = nc
        self.st = stack
        self.S = Sched(nc)
        self._n = 0

    def sb(self, name, shape, dt):
        t = self.st.enter_context(self.nc.sbuf_tensor("s_" + name, list(shape), dt))
        return t, T(name)

    def ps(self, name, shape, dt=F32):
        t = self.st.enter_context(self.nc.psum_tensor("p_" + name, list(shape), dt))
        return t, T(name, psum=True)


class Rot:
    def __init__(self, items):
        self.items = items
        self.i = 0

    def next(self):
        x = self.items[self.i % len(self.items)]
        self.i += 1
        return x


class Gemm:
    KC = 8
    MC = 512

    def __init__(self, cx, N, nslots=3, npsum=4):
        self.cx = cx
        self.N = N
        self.wb = Rot([cx.sb(f"wb{i}", [128, self.KC, self.MC], BF16) for i in range(nslots)])
        self.ws = Rot([cx.sb(f"ws{i}", [128, self.KC, self.MC], F32) for i in range(2)])
        self.ci = 0
        self.pp = Rot([cx.ps(f"pp{i}", [128, 512]) for i in range(npsum)])
        self.dq = 0

    def run(self, W, K, M, xb, xbT, epilogue, m_lo=0, cache=None, cached=False):
        S = self.cx.S
        N = self.N
        KT = K // 128
        kcs = [(k0, min(self.KC, KT - k0)) for k0 in range(0, KT, self.KC)]
        for m0 in range(0, M, self.MC):
            mw = min(self.MC, M - m0)
            nmt = mw // 128
            pst = [self.pp.next() for _ in range(nmt)]
            for ci, (k0, kn) in enumerate(kcs):
                wbt, wbT = self.wb.next()
                src = W[k0 * 128:(k0 + kn) * 128, m_lo + m0:m_lo + m0 + mw].rearrange("(kt p) m -> p kt m", p=128)
                if cache is not None and cached:
                    csrc = cache[0][k0 * 128:(k0 + kn) * 128, m0:m0 + mw].rearrange("(kt p) m -> p kt m", p=128)
                    S.dma("sp", lambda E, o=wbt[:, 0:kn, 0:mw], s=csrc: E.dma_start(out=o, in_=s), reads=[cache[1]], writes=[wbT])
                else:
                    wst, wsT = self.ws.next()
                    S.dma("sp", lambda E, o=wst[:, 0:kn, 0:mw], s=src: E.dma_start(out=o, in_=s), writes=[wsT])
                    if self.ci % 2 == 0:
                        S.op("dve", lambda E, o=wbt[:, 0:kn, 0:mw], i=wst[:, 0:kn, 0:mw]: E.tensor_copy(o, i), reads=[wsT], writes=[wbT])
                    else:
                        S.op("act", lambda E, o=wbt[:, 0:kn, 0:mw], i=wst[:, 0:kn, 0:mw]: E.copy(o, i), reads=[wsT], writes=[wbT])
                    self.ci += 1
                    if cache is not None:
                        cdst = cache[0][k0 * 128:(k0 + kn) * 128, m0:m0 + mw].rearrange("(kt p) m -> p kt m", p=128)
                        S.dma("pool", lambda E, o=cdst, i=wbt[:, 0:kn, 0:mw]: E.dma_start(out=o, in_=i), reads=[wbT], writes=[cache[1]])
                for j in range(nmt):
                    for kt in range(kn):
                        first = (ci == 0 and kt == 0)
                        last = (ci == len(kcs) - 1 and kt == kn - 1)
                        S.op("pe", lambda E, o=pst[j][0][:, 0:N], l=wbt[:, kt, j * 128:(j + 1) * 128],
                             r=xb[:, k0 + kt, :], a=first, b=last: E.matmul(o, l, r, start=a, stop=b),
                             reads=[wbT, xbT], writes=[pst[j][1]])
            for j in range(nmt):
                epilogue(m0 // 128 + j, pst[j][0][:, 0:N], pst[j][1])


def rms_stats(cx, hT, hTT, KT, N, ones, onesT, sq_rot, ps_rot, rstd, rstdT, dim):
    S = cx.S
    pst, psT = ps_rot.next()
    for kt in range(KT):
        sq, sqT = sq_rot.next()
        S.op("act", lambda E, o=sq[:, 0:N], i=hT[:, kt, :]: E.activation(o, i, AF.Square), reads=[hTT], writes=[sqT])
        S.op("pe", lambda E, o=pst[:, 0:N], l=ones[:, :], r=sq[:, 0:N], a=(kt == 0), b=(kt == KT - 1):
             E.matmul(o, l, r, start=a, stop=b), reads=[onesT, sqT], writes=[psT])
    S.op("dve", lambda E: E.tensor_scalar(rstd[:, 0:N], pst[:, 0:N], 1.0 / dim, NORM_EPS, ALU.mult, ALU.add),
         reads=[psT], writes=[rstdT])
    S.op("act", lambda E: E.activation(rstd[:, 0:N], rstd[:, 0:N], AF.Sqrt), reads=[rstdT], writes=[rstdT])
    S.op("dve", lambda E: E.reciprocal(rstd[:, 0:N], rstd[:, 0:N]), reads=[rstdT], writes=[rstdT])


def build_proj(ntok, n_out, N=512):
    nc = bass.Bass("TRN2", target_bir_lowering=False)
    KT = D_MODEL // 128
    hT_d = nc.dram_tensor("hT", [D_MODEL, ntok], F32, kind="ExternalInput").ap()
    w_d = nc.dram_tensor("w", [D_MODEL, n_out], F32, kind="ExternalInput").ap()
    g_d = nc.dram_tensor("g", [128, KT], F32, kind="ExternalInput").ap()
    z_d = nc.dram_tensor("zT", [n_out, ntok], F32, kind="ExternalOutput").ap()
    with ExitStack() as st:
        cx = Ctx(nc, st)
        S = cx.S
        hT, hTT = cx.sb("hT", [128, KT, N], F32)
        xb, xbT = cx.sb("xb", [128, KT, N], BF16)
        g, gT = cx.sb("g", [128, KT], F32)
        ones, onesT = cx.sb("ones", [128, 128], F32)
        rstd, rstdT = cx.sb("rstd", [128, N], F32)
        sq_rot = Rot([cx.sb(f"sq{i}", [128, N], F32) for i in range(2)])
        ob_rot = Rot([cx.sb(f"ob{i}", [128, N], F32) for i in range(3)])
        gm = Gemm(cx, N)
        ps_rot = Rot([cx.ps("pstat", [128, 512])])
        S.dma("sp", lambda E: E.dma_start(out=g[:, :], in_=g_d[:, :]), writes=[gT])
        S.op("dve", lambda E: E.memset(ones[:, :], 1.0), writes=[onesT])
        outs = []
        for t0 in range(0, ntok, N):
            S.dma("sp", lambda E, t0=t0: E.dma_start(out=hT[:, :, :], in_=hT_d[:, t0:t0 + N].rearrange("(kt p) n -> p kt n", p=128)),
                  writes=[hTT])
            rms_stats(cx, hT, hTT, KT, N, ones, onesT, sq_rot, ps_rot, rstd, rstdT, D_MODEL)
            for kt in range(KT):
                S.op("dve", lambda E, kt=kt: E.tensor_scalar(xb[:, kt, :], hT[:, kt, :], g[:, kt:kt + 1], None, ALU.mult),
                     reads=[hTT, gT], writes=[xbT])

            def epi(mt, ps, psT, t0=t0):
                ob, obT = ob_rot.next()
                S.op("dve", lambda E: E.tensor_tensor(ob[:, :], ps, rstd[:, 0:N], ALU.mult), reads=[psT, rstdT], writes=[obT])
                S.dma("sp", lambda E: E.dma_start(out=z_d[mt * 128:(mt + 1) * 128, t0:t0 + N], in_=ob[:, :]), reads=[obT])
                if obT not in outs:
                    outs.append(obT)

            gm.run(w_d, D_MODEL, n_out, xb, xbT, epi)
        S.final_wait("sp", outs)
        S.emit(st)
        nc._stats = S.stats
    return nc


def build_ffn(ntok, N=512, d_ff=D_FF):
    nc = bass.Bass("TRN2", target_bir_lowering=False)
    KT = D_MODEL // 128
    FT = d_ff // 128
    PT = PLE_DIM // 128
    hT_d = nc.dram_tensor("hT", [D_MODEL, ntok], F32, kind="ExternalInput").ap()
    mixT_d = nc.dram_tensor("mixT", [D_MODEL, ntok], F32, kind="ExternalInput").ap()
    pT_d = nc.dram_tensor("pT", [PLE_DIM, ntok], F32, kind="ExternalInput").ap()
    w_out_d = nc.dram_tensor("w_out", [D_MODEL, D_MODEL], F32, kind="ExternalInput").ap()
    w_up_d = nc.dram_tensor("w_up", [D_MODEL, d_ff], F32, kind="ExternalInput").ap()
    w_down_d = nc.dram_tensor("w_down", [d_ff, D_MODEL], F32, kind="ExternalInput").ap()
    w_gate_d = nc.dram_tensor("w_gate", [D_MODEL, D_MODEL], F32, kind="ExternalInput").ap()
    w_ple_d = nc.dram_tensor("w_ple", [PLE_DIM, D_MODEL], F32, kind="ExternalInput").ap()
    g2_d = nc.dram_tensor("g_mlp", [128, KT], F32, kind="ExternalInput").ap()
    g3_d = nc.dram_tensor("g_ple", [128, KT], F32, kind="ExternalInput").ap()
    o_d = nc.dram_tensor("oT", [D_MODEL, ntok], F32, kind="ExternalOutput").ap()
    caches = {nm: (nc.dram_tensor("c_" + nm, shp, BF16).ap(), T("c_" + nm)) for nm, shp in
              (("w_out", [D_MODEL, D_MODEL]), ("w_up", [D_MODEL, d_ff]), ("w_down", [d_ff, D_MODEL]), ("w_gate", [D_MODEL, D_MODEL]))}
    with ExitStack() as st:
        cx = Ctx(nc, st)
        S = cx.S
        hT, hTT = cx.sb("hT", [128, KT, N], F32)
        xb, xbT = cx.sb("xb", [128, KT, N], BF16)
        aT, aTT = cx.sb("aT", [128, FT, N], BF16)
        pb, pbT = cx.sb("pb", [128, PT, N], BF16)
        g2, g2T = cx.sb("g2", [128, KT], F32)
        g3, g3T = cx.sb("g3", [128, KT], F32)
        ones, onesT = cx.sb("ones", [128, 128], F32)
        rstd, rstdT = cx.sb("rstd", [128, N], F32)
        sq_rot = Rot([cx.sb(f"sq{i}", [128, N], F32) for i in range(2)])
        tmp_rot = Rot([cx.sb(f"tmp{i}", [128, N], F32) for i in range(3)])
        gm = Gemm(cx, N, nslots=4)
        ps_rot = Rot([cx.ps("pstat", [128, 512])])
        pple_rot = Rot([cx.ps(f"pple{i}", [128, 512]) for i in range(2)])
        wple_rot = Rot([cx.sb(f"wple{i}", [128, PT, 128], BF16) for i in range(2)])
        S.dma("sp", lambda E: E.dma_start(out=g2[:, :], in_=g2_d[:, :]), writes=[g2T])
        S.dma("sp", lambda E: E.dma_start(out=g3[:, :], in_=g3_d[:, :]), writes=[g3T])
        S.op("dve", lambda E: E.memset(ones[:, :], 1.0), writes=[onesT])
        for t0 in range(0, ntok, N):
            S.dma("sp", lambda E, t0=t0: E.dma_start(out=hT[:, :, :], in_=hT_d[:, t0:t0 + N].rearrange("(kt p) n -> p kt n", p=128)),
                  writes=[hTT])
            S.dma("pool", lambda E, t0=t0: E.dma_start(out=xb[:, :, :], in_=mixT_d[:, t0:t0 + N].rearrange("(kt p) n -> p kt n", p=128)),
                  writes=[xbT])
            S.dma("pool", lambda E, t0=t0: E.dma_start(out=pb[:, :, :], in_=pT_d[:, t0:t0 + N].rearrange("(kt p) n -> p kt n", p=128)),
                  writes=[pbT])

            def epi_add(mt, ps, psT):
                S.op("dve", lambda E: E.tensor_tensor(hT[:, mt, :], hT[:, mt, :], ps, ALU.add), reads=[psT, hTT], writes=[hTT])
            gm.run(w_out_d, D_MODEL, D_MODEL, xb, xbT, epi_add, cache=caches["w_out"], cached=(t0 > 0))

            def norm_to_xb(gv, gvT):
                rms_stats(cx, hT, hTT, KT, N, ones, onesT, sq_rot, ps_rot, rstd, rstdT, D_MODEL)
                for kt in range(KT):
                    S.op("dve", lambda E, kt=kt: E.scalar_tensor_tensor(out=xb[:, kt, :], in0=hT[:, kt, :], scalar=gv[:, kt:kt + 1],
                                                                        in1=rstd[:, 0:N], op0=ALU.mult, op1=ALU.mult),
                         reads=[hTT, gvT, rstdT], writes=[xbT])
            norm_to_xb(g2, g2T)

            def epi_relu2(mt, ps, psT):
                tmp, tmpT = tmp_rot.next()
                S.op("act", lambda E: E.activation(tmp[:, :], ps, AF.Relu), reads=[psT], writes=[tmpT])
                S.op("dve", lambda E: E.tensor_tensor(aT[:, mt, :], tmp[:, :], tmp[:, :], ALU.mult), reads=[tmpT], writes=[aTT])
            gm.run(w_up_d, D_MODEL, d_ff, xb, xbT, epi_relu2, cache=caches["w_up"], cached=(t0 > 0))

            gm.run(w_down_d, d_ff, D_MODEL, aT, aTT, epi_add, cache=caches["w_down"], cached=(t0 > 0))

            norm_to_xb(g3, g3T)

            def epi_gate(mt, ps, psT):
                tmp, tmpT = tmp_rot.next()
                S.op("act", lambda E: E.activation(tmp[:, :], ps, AF.Sigmoid), reads=[psT], writes=[tmpT])
                wbt, wbT = wple_rot.next()
                src = w_ple_d[:, mt * 128:(mt + 1) * 128].rearrange("(kt p) m -> p kt m", p=128)
                S.dma("pool", lambda E: E.dma_start(out=wbt[:, 0:PT, 0:128], in_=src), writes=[wbT])
                pp, ppT = pple_rot.next()
                for kt in range(PT):
                    S.op("pe", lambda E, kt=kt: E.matmul(pp[:, 0:N], wbt[:, kt, 0:128], pb[:, kt, :], start=(kt == 0), stop=(kt == PT - 1)),
                         reads=[wbT, pbT], writes=[ppT])
                S.op("dve", lambda E: E.tensor_tensor(tmp[:, :], tmp[:, :], pp[:, 0:N], ALU.mult), reads=[tmpT, ppT], writes=[tmpT])
                S.op("dve", lambda E: E.tensor_tensor(hT[:, mt, :], hT[:, mt, :], tmp[:, :], ALU.add), reads=[tmpT, hTT], writes=[hTT])
            gm.run(w_gate_d, D_MODEL, D_MODEL, xb, xbT, epi_gate, cache=caches["w_gate"], cached=(t0 > 0))

            S.dma("sp", lambda E, t0=t0: E.dma_start(out=o_d[:, t0:t0 + N].rearrange("(kt p) n -> p kt n", p=128), in_=hT[:, :, :]),
                  reads=[hTT])
        S.final_wait("sp", [hTT])
        S.emit(st)
        nc._stats = S.stats
    return nc


def build_attn(S_len, layer_idx):
    nc = bass.Bass("TRN2", target_bir_lowering=False)
    NQ = 512
    nkt = S_len // 128
    nqb = S_len // NQ
    lam_init = 0.8 - 0.6 * math.exp(-0.3 * layer_idx)
    scale = 64 ** -0.5
    qT_d = nc.dram_tensor("qT", [128, S_len], F32, kind="ExternalInput").ap()
    kT_d = nc.dram_tensor("kT", [128, S_len], F32, kind="ExternalInput").ap()
    v_d = nc.dram_tensor("v", [S_len, 128], F32, kind="ExternalInput").ap()
    qg_d = nc.dram_tensor("qg", [128, 1], F32, kind="ExternalInput").ap()
    kg_d = nc.dram_tensor("kg", [128, 1], F32, kind="ExternalInput").ap()
    lam_d = nc.dram_tensor("lam", [128, 256], F32, kind="ExternalInput").ap()
    sub_d = nc.dram_tensor("subln", [128, 128], F32, kind="ExternalInput").ap()
    bias_d = nc.dram_tensor("biasT", [128, 3, 128], F32, kind="ExternalInput").ap()
    cfar_d = nc.dram_tensor("cfar", [2, 128, 1], F32, kind="ExternalInput").ap()
    o_d = nc.dram_tensor("o", [S_len, 128], F32, kind="ExternalOutput").ap()
    with ExitStack() as st:
        cx = Ctx(nc, st)
        S = cx.S
        qb_, qbT = cx.sb("qhat", [128, S_len], BF16)
        kb_, kbT = cx.sb("khat", [128, S_len], BF16)
        vb, vbT = cx.sb("vb", [128, nkt, 130], BF16)
        qg, qgT = cx.sb("qg", [128, 1], F32)
        kg, kgT = cx.sb("kg", [128, 1], F32)
        lam, lamT = cx.sb("lam", [128, 256], F32)
        sub, subT = cx.sb("sub", [128, 128], F32)
        biasT, biasTT = cx.sb("biasT", [128, 3, 128], F32)
        cfar, cfarT = cx.sb("cfar", [128, 2, 16], F32)
        ones, onesT = cx.sb("ones", [128, 128], F32)
        sm, smT = cx.sb("sm", [128, 8], F32)
        stg_rot = Rot([cx.sb(f"stg{i}", [128, NQ], F32) for i in range(2)])
        sq_rot = Rot([cx.sb(f"sq{i}", [128, NQ], F32) for i in range(2)])
        rs_rot = Rot([cx.sb(f"rs{i}", [128, NQ], F32) for i in range(2)])
        pT_rot = Rot([cx.sb(f"pT{i}", [128, NQ], BF16) for i in range(4)])
        tb_rot = Rot([cx.sb(f"tb{i}", [128, 128], F32) for i in range(3)])
        ep_rot = Rot([cx.sb(f"ep{i}", [128, 3, 128], F32) for i in range(2)])
        eps_rot = Rot([cx.sb(f"eps{i}", [128, 4], F32) for i in range(2)])
        ps_s = Rot([cx.ps(f"ps_s{i}", [128, 512]) for i in range(4)])
        ps_a = [cx.ps(f"ps_a{i}", [128, 512]) for i in range(3)]
        ps_n = Rot([cx.ps("ps_n", [128, 512])])

        for (t, tT, d) in ((qg, qgT, qg_d), (kg, kgT, kg_d), (lam, lamT, lam_d), (sub, subT, sub_d)):
            S.dma("sp", lambda E, t=t, d=d: E.dma_start(out=t[:, :], in_=d[:, :]), writes=[tT])
        S.dma("sp", lambda E: E.dma_start(out=biasT[:, :, :], in_=bias_d[:, :, :]), writes=[biasTT])
        S.dma("sp", lambda E: [E.dma_start(out=cfar[:, 0, 0:1], in_=cfar_d[0, :, :]), E.dma_start(out=cfar[:, 1, 0:1], in_=cfar_d[1, :, :])], writes=[cfarT], parts=2)
        S.op("dve", lambda E: E.memset(ones[:, :], 0.0), writes=[onesT])
        S.op("dve", lambda E: E.memset(ones[0:64, 0:64], 1.0), writes=[onesT])
        S.op("dve", lambda E: E.memset(ones[64:128, 64:128], 1.0), writes=[onesT])
        S.dma("pool", lambda E: E.dma_start(out=vb[:, :, 0:128], in_=v_d.rearrange("(kt p) d -> p kt d", p=128)), writes=[vbT])
        S.op("dve", lambda E: E.memset(vb[:, :, 128:130], 1.0), writes=[vbT])
        S.op("dve", lambda E: E.tensor_tensor(lam[:, 0:64], lam[:, 0:64], lam[:, 64:128], ALU.mult), reads=[lamT], writes=[lamT])
        S.op("dve", lambda E: E.tensor_tensor(lam[:, 128:192], lam[:, 128:192], lam[:, 192:256], ALU.mult), reads=[lamT], writes=[lamT])
        S.op("dve", lambda E: E.reduce_sum(sm[:, 0:1], lam[:, 0:64], AX.X), reads=[lamT], writes=[smT])
        S.op("dve", lambda E: E.reduce_sum(sm[:, 1:2], lam[:, 128:192], AX.X), reads=[lamT], writes=[smT])
        S.op("act", lambda E: E.activation(sm[:, 0:2], sm[:, 0:2], AF.Exp), reads=[smT], writes=[smT])
        S.op("dve", lambda E: E.tensor_tensor(sm[:, 2:3], sm[:, 1:2], sm[:, 0:1], ALU.subtract), reads=[smT], writes=[smT])
        S.op("dve", lambda E: E.tensor_scalar(sm[:, 2:3], sm[:, 2:3], -lam_init, None, ALU.add), reads=[smT], writes=[smT])
        S.op("dve", lambda E: E.tensor_scalar(sub[:, :], sub[:, :], 1.0 - lam_init, None, ALU.mult), reads=[subT], writes=[subT])

        for (src_d, g, gT, dst, dstT) in ((qT_d, qg, qgT, qb_, qbT), (kT_d, kg, kgT, kb_, kbT)):
            for c0 in range(0, S_len, NQ):
                stg, stgT = stg_rot.next()
                sq, sqT = sq_rot.next()
                rs, rsT = rs_rot.next()
                pn, pnT = ps_n.next()
                S.dma("sp", lambda E, stg=stg, c0=c0, src_d=src_d: E.dma_start(out=stg[:, :], in_=src_d[:, c0:c0 + NQ]), writes=[stgT])
                S.op("act", lambda E, sq=sq, stg=stg: E.activation(sq[:, :], stg[:, :], AF.Square), reads=[stgT], writes=[sqT])
                S.op("pe", lambda E, pn=pn, sq=sq: E.matmul(pn[:, :], ones[:, :], sq[:, :], start=True, stop=True), reads=[onesT, sqT], writes=[pnT])
                S.op("dve", lambda E, rs=rs, pn=pn: E.tensor_scalar(rs[:, :], pn[:, :], 1.0 / 64, NORM_EPS, ALU.mult, ALU.add), reads=[pnT], writes=[rsT])
                S.op("act", lambda E, rs=rs: E.activation(rs[:, :], rs[:, :], AF.Sqrt), reads=[rsT], writes=[rsT])
                S.op("dve", lambda E, rs=rs: E.reciprocal(rs[:, :], rs[:, :]), reads=[rsT], writes=[rsT])
                S.op("dve", lambda E, rs=rs, stg=stg, g=g, dst=dst, c0=c0: E.scalar_tensor_tensor(
                    out=dst[:, c0:c0 + NQ], in0=stg[:, :], scalar=g[:, 0:1], in1=rs[:, :], op0=ALU.mult, op1=ALU.mult),
                    reads=[stgT, gT, rsT], writes=[dstT])

        outs = []
        for qb in range(nqb):
            q0 = qb * NQ
            started = [False, False, False]

            def acc_of(m, s):
                if s < 3:
                    return m, s * 130
                return 2, m * 130

            units = [(kt, m) for kt in range(nkt) for m in range(2)]
            qk_out = {}

            def emit_qk(u):
                kt, m = units[u]
                pss, pssT = ps_s.next()
                S.op("pe", lambda E, pss=pss, m=m, kt=kt, q0=q0: E.matmul(pss[:, 0:NQ], kb_[64 * m:64 * m + 64, kt * 128:(kt + 1) * 128],
                                                                         qb_[64 * m:64 * m + 64, q0:q0 + NQ], start=True, stop=True),
                     reads=[kbT, qbT], writes=[pssT])
                qk_out[u] = (pss, pssT)

            LA = 3
            for u in range(min(LA, len(units))):
                emit_qk(u)
            for u, (kt, m) in enumerate(units):
                pss, pssT = qk_out.pop(u)
                pT, pTT = pT_rot.next()
                qs0 = 4 * qb
                if kt < qs0 - 1 or kt > qs0 + 4:
                    col = 0 if kt < qs0 else 1
                    S.op("act", lambda E, pT=pT, pss=pss, col=col: E.activation(pT[:, :], pss[:, 0:NQ], AF.Exp, bias=cfar[:, col, 0:1], scale=scale),
                         reads=[pssT, cfarT], writes=[pTT])
                else:
                    for s in range(4):
                        dlt = kt - (qs0 + s)
                        sl = slice(s * 128, (s + 1) * 128)
                        if abs(dlt) <= 1:
                            tb, tbT = tb_rot.next()
                            S.op("dve", lambda E, tb=tb, pss=pss, sl=sl, dlt=dlt: E.scalar_tensor_tensor(
                                out=tb[:, :], in0=pss[:, sl], scalar=scale, in1=biasT[:, dlt + 1, :], op0=ALU.mult, op1=ALU.add),
                                reads=[pssT, biasTT], writes=[tbT])
                            S.op("act", lambda E, pT=pT, tb=tb, sl=sl: E.activation(pT[:, sl], tb[:, :], AF.Exp), reads=[tbT], writes=[pTT])
                        else:
                            col = 0 if dlt < 0 else 1
                            S.op("act", lambda E, pT=pT, pss=pss, sl=sl, col=col: E.activation(pT[:, sl], pss[:, sl], AF.Exp, bias=cfar[:, col, 0:1], scale=scale),
                                 reads=[pssT, cfarT], writes=[pTT])
                if u + LA < len(units):
                    emit_qk(u + LA)
                for s in range(4):
                    bi, off = acc_of(m, s)
                    acc, accT = ps_a[bi]
                    stt = not started[bi]
                    started[bi] = True
                    S.op("pe", lambda E, acc=acc, off=off, pT=pT, s=s, kt=kt, stt=stt: E.matmul(
                        acc[:, off:off + 130], pT[:, s * 128:(s + 1) * 128], vb[:, kt, :], start=stt, stop=(kt == nkt - 1),
                        skip_group_check=True), reads=[pTT, vbT], writes=[accT])
            for s in range(4):
                ep, epT = ep_rot.next()
                es, esT = eps_rot.next()
                for m in range(2):
                    bi, off = acc_of(m, s)
                    acc, accT = ps_a[bi]
                    S.op("dve", lambda E, es=es, acc=acc, off=off, m=m: E.reciprocal(es[:, m:m + 1], acc[:, off + 128:off + 129]), reads=[accT], writes=[esT])
                    S.op("dve", lambda E, ep=ep, es=es, acc=acc, off=off, m=m: E.tensor_scalar(ep[:, m, :], acc[:, off:off + 128], es[:, m:m + 1], None, ALU.mult),
                         reads=[accT, esT], writes=[epT])
                S.op("dve", lambda E, ep=ep: E.scalar_tensor_tensor(out=ep[:, 2, :], in0=ep[:, 1, :], scalar=sm[:, 2:3], in1=ep[:, 0, :], op0=ALU.mult, op1=ALU.add),
                     reads=[epT, smT], writes=[epT])
                S.op("act", lambda E, ep=ep, es=es: E.activation(ep[:, 0, :], ep[:, 2, :], AF.Square, accum_out=es[:, 2:3]), reads=[epT], writes=[epT, esT])
                S.op("dve", lambda E, es=es: E.tensor_scalar(es[:, 2:3], es[:, 2:3], 1.0 / 128, NORM_EPS, ALU.mult, ALU.add), reads=[esT], writes=[esT])
                S.op("act", lambda E, es=es: E.activation(es[:, 2:3], es[:, 2:3], AF.Sqrt), reads=[esT], writes=[esT])
                S.op("dve", lambda E, es=es: E.reciprocal(es[:, 3:4], es[:, 2:3]), reads=[esT], writes=[esT])
                S.op("dve", lambda E, ep=ep, es=es: E.scalar_tensor_tensor(out=ep[:, 1, :], in0=ep[:, 2, :], scalar=es[:, 3:4], in1=sub[:, :], op0=ALU.mult, op1=ALU.mult),
                     reads=[epT, esT, subT], writes=[epT])
                S.dma("sp", lambda E, ep=ep, s=s, q0=q0: E.dma_start(out=o_d[q0 + s * 128:q0 + (s + 1) * 128, :], in_=ep[:, 1, :]), reads=[epT])
                if epT not in outs:
                    outs.append(epT)
        S.final_wait("sp", outs)
        S.emit(st)
        nc._stats = S.stats
    return nc


def t5_bucket_np(rel):
    half = 16
    max_exact = 8
    n = np.abs(rel)
    nf = np.maximum(n, 1).astype(np.float32)
    large = max_exact + (np.log(nf / max_exact) / np.float32(math.log(128 / max_exact)) * (half - max_exact)).astype(np.int32)
    large = np.minimum(large, half - 1)
    return np.where(rel > 0, half, 0) + np.where(n < max_exact, n, large)


def attn_inputs(zT_b, j, rel_bias, a_qnorm, a_knorm, a_lambda, a_subln):
    qT = np.ascontiguousarray(zT_b[j * 128:(j + 1) * 128])
    kT = np.ascontiguousarray(zT_b[512 + j * 128:512 + (j + 1) * 128])
    v = np.ascontiguousarray(zT_b[1024 + j * 128:1024 + (j + 1) * 128].T)
    kl = np.arange(128)[:, None]
    ql = np.arange(128)[None, :]
    tiles = []
    for o in (-1, 0, 1):
        rel = o * 128 + kl - ql
        tiles.append(rel_bias[t5_bucket_np(rel), j])
    biasT = np.ascontiguousarray(np.stack(tiles, axis=1).astype(np.float32))
    cfar = np.ascontiguousarray(np.broadcast_to(np.array([rel_bias[15, j], rel_bias[31, j]], np.float32)[:, None, None], (2, 128, 1)))
    return dict(qT=qT, kT=kT, v=v,
                qg=np.ascontiguousarray(np.tile(a_qnorm, 2)[:, None]), kg=np.ascontiguousarray(np.tile(a_knorm, 2)[:, None]),
                lam=np.ascontiguousarray(np.broadcast_to(a_lambda.reshape(1, 256), (128, 256))),
                subln=np.ascontiguousarray(np.broadcast_to(a_subln[None, :], (128, 128))),
                biasT=biasT, cfar=cfar)


def hgrn_masks():
    idx = np.arange(128)
    same = (idx[:, None] // 64) == (idx[None, :] // 64)
    s = idx[:, None]
    t = idx[None, :]
    out = {}
    for name, le, mid in (("f", lambda a, b: a <= b, 31), ("b", lambda a, b: a >= b, 32)):
        M = (same & le(s, t)).astype(np.float32)
        r = (idx // 64) * 64 + mid
        Mq = M - M[:, r]
        Mc = (same & ~le(s, t)).astype(np.float32)
        out[name] = np.stack([M, Mq, Mc, M], axis=1)
    return np.ascontiguousarray(np.concatenate([out["f"], out["b"]], axis=1).astype(np.float32))


def build_hgrn(S_len, layer_idx):
    nc = bass.Bass("TRN2", target_bir_lowering=False)
    nt = S_len // 128
    has_lb = layer_idx > 0
    zqT_d = nc.dram_tensor("zqT", [128, S_len], F32, kind="ExternalInput").ap()
    zgT_d = nc.dram_tensor("zgT", [128, S_len], F32, kind="ExternalInput").ap()
    zfT_d = nc.dram_tensor("zfT", [2, 128, S_len], F32, kind="ExternalInput").ap()
    zf_d = nc.dram_tensor("zf", [2, S_len, 128], F32, kind="ExternalInput").ap()
    zi_d = nc.dram_tensor("zi", [S_len, 128], F32, kind="ExternalInput").ap()
    lbr_d = nc.dram_tensor("lbrow", [128, 2, 128], F32, kind="ExternalInput").ap()
    lbc_d = nc.dram_tensor("lbcol", [128, 2], F32, kind="ExternalInput").ap()
    og_d = nc.dram_tensor("ogain", [128, 1], F32, kind="ExternalInput").ap()
    mk_d = nc.dram_tensor("masks", [128, 8, 128], F32, kind="ExternalInput").ap()
    o_d = nc.dram_tensor("oT", [128, S_len], F32, kind="ExternalOutput").ap()
    with ExitStack() as st:
        cx = Ctx(nc, st)
        S = cx.S
        oT, oTT = cx.sb("oacc", [128, S_len], F32)
        vtm, vtmT = cx.sb("vtm", [128, nt, 128], BF16)
        mk, mkT = cx.sb("mk", [128, 8, 128], F32)
        lbr, lbrT = cx.sb("lbr", [128, 2, 128], F32)
        lbc, lbcT = cx.sb("lbc", [128, 2], F32)
        og, ogT = cx.sb("og", [128, 1], F32)
        ones, onesT = cx.sb("ones", [128, 128], F32)
        St, StT = cx.sb("state", [128, 128], F32)
        Sb, SbT = cx.sb("stateb", [128, 128], BF16)
        R = {}
        for nm, dt, n in (("zf", F32, 2), ("zfT", F32, 2), ("zqT", F32, 2), ("sg", F32, 2), ("lf", F32, 2), ("ktm", F32, 2), ("e1", F32, 2),
                          ("kdec", BF16, 2), ("eq", F32, 2), ("ek", F32, 2), ("ed", F32, 3), ("qT", F32, 2), ("kT", F32, 2),
                          ("qq", BF16, 2), ("kk", BF16, 2), ("qd", BF16, 2), ("attm", BF16, 2)):
            R[nm] = Rot([cx.sb(f"{nm}{i}", [128, 128], dt) for i in range(n)])
        P = {nm: Rot([cx.ps(nm, [128, 512])]) for nm in ("p1", "p2", "p3", "p4", "p5", "p6", "pn")}
        S.dma("sp", lambda E: E.dma_start(out=mk[:, :, :], in_=mk_d[:, :, :]), writes=[mkT])
        S.dma("sp", lambda E: E.dma_start(out=lbr[:, :, :], in_=lbr_d[:, :, :]), writes=[lbrT])
        S.dma("sp", lambda E: E.dma_start(out=lbc[:, :], in_=lbc_d[:, :]), writes=[lbcT])
        S.dma("sp", lambda E: E.dma_start(out=og[:, :], in_=og_d[:, :]), writes=[ogT])
        S.dma("pool", lambda E: E.dma_start(out=vtm[:, :, :], in_=zi_d.rearrange("(kt p) d -> p kt d", p=128)), writes=[vtmT])
        S.op("dve", lambda E: E.memset(ones[:, :], 1.0), writes=[onesT])
        if has_lb:
            S.op("dve", lambda E: E.tensor_tensor(lbr[:, 0, :], lbr[:, 1, :], lbr[:, 0, :], ALU.subtract), reads=[lbrT], writes=[lbrT])
            S.op("act", lambda E: E.activation(lbr[:, 0, :], lbr[:, 0, :], AF.Sigmoid), reads=[lbrT], writes=[lbrT])
            S.op("dve", lambda E: E.tensor_scalar(lbr[:, 1, :], lbr[:, 0, :], -1.0, 1.0, ALU.mult, ALU.add), reads=[lbrT], writes=[lbrT])
            S.op("dve", lambda E: E.tensor_tensor(lbc[:, 0:1], lbc[:, 1:2], lbc[:, 0:1], ALU.subtract), reads=[lbcT], writes=[lbcT])
            S.op("act", lambda E: E.activation(lbc[:, 0:1], lbc[:, 0:1], AF.Sigmoid), reads=[lbcT], writes=[lbcT])
            S.op("dve", lambda E: E.tensor_scalar(lbc[:, 1:2], lbc[:, 0:1], -1.0, 1.0, ALU.mult, ALU.add), reads=[lbcT], writes=[lbcT])

        for z in range(2):
            mo = 4 * z
            S.op("dve", lambda E: E.memset(St[:, :], 0.0), writes=[StT])
            S.op("dve", lambda E: E.memset(Sb[:, :], 0.0), writes=[SbT])
            tiles = list(range(nt)) if z == 0 else list(range(nt - 1, -1, -1))

            def stage_a(i, z=z, mo=mo):
                c0 = i * 128
                zf, zfT_ = R["zf"].next()
                zfT, zfTT = R["zfT"].next()
                zqT, zqTT = R["zqT"].next()
                S.dma("sp", lambda E, zf=zf, z=z, c0=c0: E.dma_start(out=zf[:, :], in_=zf_d[z, c0:c0 + 128, :]), writes=[zfT_])
                S.dma("sp", lambda E, zfT=zfT, z=z, c0=c0: E.dma_start(out=zfT[:, :], in_=zfT_d[z, :, c0:c0 + 128]), writes=[zfTT])
                S.dma("sp", lambda E, zqT=zqT, c0=c0: E.dma_start(out=zqT[:, :], in_=zqT_d[:, c0:c0 + 128]), writes=[zqTT])
                sg, sgT = R["sg"].next()
                lf, lfT = R["lf"].next()
                ktm, ktmT = R["ktm"].next()
                S.op("act", lambda E, sg=sg, zf=zf: E.activation(sg[:, :], zf[:, :], AF.Sigmoid), reads=[zfT_], writes=[sgT])
                S.op("act", lambda E, ktm=ktm, zf=zf: E.activation(ktm[:, :], zf[:, :], AF.Sigmoid, scale=-1.0), reads=[zfT_], writes=[ktmT])
                if has_lb:
                    S.op("dve", lambda E, sg=sg: E.tensor_tensor(sg[:, :], sg[:, :], lbr[:, 1, :], ALU.mult), reads=[sgT, lbrT], writes=[sgT])
                    S.op("dve", lambda E, sg=sg: E.tensor_tensor(sg[:, :], sg[:, :], lbr[:, 0, :], ALU.add), reads=[sgT, lbrT], writes=[sgT])
                    S.op("dve", lambda E, ktm=ktm: E.tensor_tensor(ktm[:, :], ktm[:, :], lbr[:, 1, :], ALU.mult), reads=[ktmT, lbrT], writes=[ktmT])
                S.op("act", lambda E, lf=lf, sg=sg: E.activation(lf[:, :], sg[:, :], AF.Ln), reads=[sgT], writes=[lfT])
                p1, p1T = P["p1"].next()
                p2, p2T = P["p2"].next()
                p3, p3T = P["p3"].next()
                S.op("pe", lambda E, p1=p1, lf=lf, mo=mo: E.matmul(p1[:, 0:128], mk[:, mo + 2, :], lf[:, :], start=True, stop=True), reads=[mkT, lfT], writes=[p1T])
                S.op("pe", lambda E, p2=p2, lf=lf, mo=mo: E.matmul(p2[:, 0:128], lf[:, :], mk[:, mo + 1, :], start=True, stop=True), reads=[mkT, lfT], writes=[p2T])
                S.op("pe", lambda E, p3=p3, lf=lf, mo=mo: E.matmul(p3[:, 0:128], lf[:, :], mk[:, mo + 0, :], start=True, stop=True), reads=[mkT, lfT], writes=[p3T])
                e1, e1T = R["e1"].next()
                kdec, kdecT = R["kdec"].next()
                S.op("act", lambda E, e1=e1, p1=p1: E.activation(e1[:, :], p1[:, 0:128], AF.Exp), reads=[p1T], writes=[e1T])
                S.op("dve", lambda E, kdec=kdec, ktm=ktm, e1=e1: E.tensor_tensor(kdec[:, :], ktm[:, :], e1[:, :], ALU.mult), reads=[ktmT, e1T], writes=[kdecT])
                eq, eqT = R["eq"].next()
                ek, ekT = R["ek"].next()
                ed, edT = R["ed"].next()
                S.op("act", lambda E, eq=eq, p2=p2: E.activation(eq[:, :], p2[:, 0:128], AF.Exp), reads=[p2T], writes=[eqT])
                S.op("act", lambda E, ek=ek, p2=p2: E.activation(ek[:, :], p2[:, 0:128], AF.Exp, scale=-1.0), reads=[p2T], writes=[ekT])
                S.op("act", lambda E, ed=ed, p3=p3: E.activation(ed[:, :], p3[:, 0:128], AF.Exp), reads=[p3T], writes=[edT])
                qT, qTT = R["qT"].next()
                kT, kTT = R["kT"].next()
                S.op("act", lambda E, qT=qT, zqT=zqT: E.activation(qT[:, :], zqT[:, :], AF.Silu), reads=[zqTT], writes=[qTT])
                S.op("act", lambda E, kT=kT, zfT=zfT: E.activation(kT[:, :], zfT[:, :], AF.Sigmoid, scale=-1.0), reads=[zfTT], writes=[kTT])
                if has_lb:
                    S.op("dve", lambda E, kT=kT: E.tensor_scalar(kT[:, :], kT[:, :], lbc[:, 1:2], None, ALU.mult), reads=[kTT, lbcT], writes=[kTT])
                qq, qqT = R["qq"].next()
                kk, kkT = R["kk"].next()
                qd, qdT = R["qd"].next()
                S.op("dve", lambda E, qq=qq, qT=qT, eq=eq: E.tensor_tensor(qq[:, :], qT[:, :], eq[:, :], ALU.mult), reads=[qTT, eqT], writes=[qqT])
                S.op("dve", lambda E, kk=kk, kT=kT, ek=ek: E.tensor_tensor(kk[:, :], kT[:, :], ek[:, :], ALU.mult), reads=[kTT, ekT], writes=[kkT])
                S.op("dve", lambda E, qd=qd, qT=qT, ed=ed: E.tensor_tensor(qd[:, :], qT[:, :], ed[:, :], ALU.mult), reads=[qTT, edT], writes=[qdT])
                p4, p4T = P["p4"].next()
                S.op("pe", lambda E, p4=p4, kk=kk, qq=qq: E.matmul(p4[:, 0:128], kk[:, :], qq[:, :], start=True, stop=True), reads=[kkT, qqT], writes=[p4T])
                attm, attmT = R["attm"].next()
                S.op("dve", lambda E, attm=attm, p4=p4, mo=mo: E.tensor_tensor(attm[:, :], p4[:, 0:128], mk[:, mo + 3, :], ALU.mult), reads=[p4T, mkT], writes=[attmT])
                return dict(i=i, c0=c0, kdec=(kdec, kdecT), qd=(qd, qdT), attm=(attm, attmT), ed=(ed, edT))

            def stage_b(ctx_, z=z):
                i = ctx_["i"]; c0 = ctx_["c0"]
                kdec, kdecT = ctx_["kdec"]; qd, qdT = ctx_["qd"]; attm, attmT = ctx_["attm"]; ed, edT = ctx_["ed"]
                for c2 in ((0, 1) if z == 0 else (1, 0)):
                    r0 = 64 * c2
                    p5, p5T = P["p5"].next()
                    S.op("pe", lambda E, p5=p5, i=i, attm=attm, r0=r0: E.matmul(p5[:, 0:64], vtm[:, i, :], attm[:, r0:r0 + 64], start=True, stop=False),
                         reads=[vtmT, attmT], writes=[p5T])
                    S.op("pe", lambda E, p5=p5, qd=qd, r0=r0: E.matmul(p5[:, 0:64], Sb[:, :], qd[:, r0:r0 + 64], start=False, stop=True),
                         reads=[SbT, qdT], writes=[p5T])
                    if z == 0:
                        S.op("act", lambda E, p5=p5, c0=c0, r0=r0: E.copy(oT[:, c0 + r0:c0 + r0 + 64], p5[:, 0:64]), reads=[p5T], writes=[oTT])
                    else:
                        S.op("dve", lambda E, p5=p5, c0=c0, r0=r0: E.tensor_tensor(oT[:, c0 + r0:c0 + r0 + 64], oT[:, c0 + r0:c0 + r0 + 64], p5[:, 0:64], ALU.add),
                             reads=[p5T, oTT], writes=[oTT])
                    p6, p6T = P["p6"].next()
                    S.op("pe", lambda E, p6=p6, kdec=kdec, i=i, r0=r0: E.matmul(p6[:, 0:128], kdec[r0:r0 + 64, :], vtm[r0:r0 + 64, i, :], start=True, stop=True),
                         reads=[kdecT, vtmT], writes=[p6T])
                    dcol = r0 + 63 if z == 0 else r0
                    S.op("dve", lambda E, p6=p6, ed=ed, dcol=dcol: E.scalar_tensor_tensor(out=St[:, :], in0=St[:, :], scalar=ed[:, dcol:dcol + 1], in1=p6[:, 0:128],
                                                                                           op0=ALU.mult, op1=ALU.add), reads=[StT, edT, p6T], writes=[StT])
                    S.op("act", lambda E: E.copy(Sb[:, :], St[:, :]), reads=[StT], writes=[SbT])

            nxt = stage_a(tiles[0])
            for ti_ in range(len(tiles)):
                cur_ = nxt
                if ti_ + 1 < len(tiles):
                    nxt = stage_a(tiles[ti_ + 1])
                stage_b(cur_)
        ob_rot = Rot([cx.sb(f"ob{i}", [128, 512], F32) for i in range(2)])
        zg_rot = Rot([cx.sb(f"zg{i}", [128, 512], F32) for i in range(2)])
        sq_rot = Rot([cx.sb(f"sq{i}", [128, 512], F32) for i in range(2)])
        rs_rot = Rot([cx.sb(f"rs{i}", [128, 512], F32) for i in range(2)])
        outs = []
        NB = min(512, S_len)
        for c0 in range(0, S_len, NB):
            ob, obT = ob_rot.next()
            zg, zgT = zg_rot.next()
            sq, sqT = sq_rot.next()
            rs, rsT = rs_rot.next()
            pn, pnT = P["pn"].next()
            S.dma("sp", lambda E, zg=zg, c0=c0: E.dma_start(out=zg[:, 0:NB], in_=zgT_d[:, c0:c0 + NB]), writes=[zgT])
            S.op("act", lambda E, zg=zg: E.activation(zg[:, 0:NB], zg[:, 0:NB], AF.Silu), reads=[zgT], writes=[zgT])
            S.op("act", lambda E, sq=sq, c0=c0: E.activation(sq[:, 0:NB], oT[:, c0:c0 + NB], AF.Square), reads=[oTT], writes=[sqT])
            S.op("pe", lambda E, pn=pn, sq=sq: E.matmul(pn[:, 0:NB], ones[:, :], sq[:, 0:NB], start=True, stop=True), reads=[onesT, sqT], writes=[pnT])
            S.op("dve", lambda E, rs=rs, pn=pn: E.tensor_scalar(rs[:, 0:NB], pn[:, 0:NB], 1.0 / 128, NORM_EPS, ALU.mult, ALU.add), reads=[pnT], writes=[rsT])
            S.op("act", lambda E, rs=rs: E.activation(rs[:, 0:NB], rs[:, 0:NB], AF.Sqrt), reads=[rsT], writes=[rsT])
            S.op("dve", lambda E, rs=rs: E.reciprocal(rs[:, 0:NB], rs[:, 0:NB]), reads=[rsT], writes=[rsT])
            S.op("dve", lambda E, ob=ob, rs=rs, c0=c0: E.scalar_tensor_tensor(out=ob[:, 0:NB], in0=oT[:, c0:c0 + NB], scalar=og[:, 0:1], in1=rs[:, 0:NB],
                                                                             op0=ALU.mult, op1=ALU.mult), reads=[oTT, ogT, rsT], writes=[obT])
            S.op("dve", lambda E, ob=ob, zg=zg: E.tensor_tensor(ob[:, 0:NB], ob[:, 0:NB], zg[:, 0:NB], ALU.mult), reads=[obT, zgT], writes=[obT])
            S.dma("sp", lambda E, ob=ob, c0=c0: E.dma_start(out=o_d[:, c0:c0 + NB], in_=ob[:, 0:NB]), reads=[obT])
            if obT not in outs:
                outs.append(obT)
        S.final_wait("sp", outs)
        S.emit(st)
        nc._stats = S.stats
    return nc


def hgrn_inputs(zT_b, j, h_lb_logits, h_onorm):
    base = 1536
    def rows(k):
        return zT_b[base + k * 512 + j * 128: base + k * 512 + (j + 1) * 128]
    zqT, zffT, zfbT, ziT, zgT = [rows(k) for k in range(5)]
    lb = h_lb_logits[:, j * 128:(j + 1) * 128]
    return dict(zqT=np.ascontiguousarray(zqT), zgT=np.ascontiguousarray(zgT),
                zfT=np.ascontiguousarray(np.stack([zffT, zfbT])),
                zf=np.ascontiguousarray(np.stack([zffT.T, zfbT.T])),
                zi=np.ascontiguousarray(ziT.T),
                lbrow=np.ascontiguousarray(np.broadcast_to(lb[None, :, :], (128, 2, 128))),
                lbcol=np.ascontiguousarray(lb.T),
                ogain=np.ascontiguousarray(h_onorm[:, None]),
                masks=hgrn_masks())


R_DECAY_SCALE = math.exp(-0.5)
R_GN_EPS = 64e-5


def build_rwkv(S_len, TB=256):
    nc = bass.Bass("TRN2", target_bir_lowering=False)
    nblk = S_len // TB
    n128 = S_len // 128
    n64 = S_len // 64
    zr_d = nc.dram_tensor("zr3T", [3, 768, S_len], F32, kind="ExternalInput").ap()
    vht_d = nc.dram_tensor("v3ht", [128, 3, n64, 64], F32, kind="ExternalInput").ap()
    vtm_d = nc.dram_tensor("v3tm", [128, 3, n128, 128], F32, kind="ExternalInput").ap()
    mucol_d = nc.dram_tensor("mucol", [128, 6, 2], F32, kind="ExternalInput").ap()
    muht_d = nc.dram_tensor("muht", [128, 2, 64], F32, kind="ExternalInput").ap()
    mutm_d = nc.dram_tensor("mutm", [128, 2, 128], F32, kind="ExternalInput").ap()
    w2_d = nc.dram_tensor("w2t", [128, 128], F32, kind="ExternalInput").ap()
    a2_d = nc.dram_tensor("a2t", [128, 128], F32, kind="ExternalInput").ap()
    g2_d = nc.dram_tensor("g2m", [128, 128], F32, kind="ExternalInput").ap()
    cols_d = nc.dram_tensor("cols", [8, 128, 1], F32, kind="ExternalInput").ap()
    gn_d = nc.dram_tensor("gnrow", [128, 2, 128], F32, kind="ExternalInput").ap()
    es_d = nc.dram_tensor("esel", [128, 64, 128], F32, kind="ExternalInput").ap()
    o_d = nc.dram_tensor("o", [S_len, 128], F32, kind="ExternalOutput").ap()
    with ExitStack() as st:
        cx = Ctx(nc, st)
        S = cx.S
        esel, eselT = cx.sb("esel", [128, 64, 128], BF16)
        vht, vhtT = cx.sb("vht", [128, 2, n64, 64], BF16)
        vtmp_rot = Rot([cx.sb(f"vtmp{i}", [128, 512], F32) for i in range(2)])
        vtmp2_rot = Rot([cx.sb(f"vtmpb{i}", [128, 512], F32) for i in range(2)])
        vtm, vtmT = cx.sb("vtm", [128, n128, 128], F32)
        ysb1 = cx.sb("ysb", [128, n128, 128], F32)
        mucol, mucolT = cx.sb("mucol", [128, 6, 3], F32)
        muht, muhtT = cx.sb("muht", [128, 3, 64], F32)
        mutm, mutmT = cx.sb("mutm", [128, 3, 128], F32)
        w2b, w2bT = cx.sb("w2b", [128, 128], BF16)
        a2b, a2bT = cx.sb("a2b", [128, 128], BF16)
        g2b, g2bT = cx.sb("g2b", [128, 128], BF16)
        cols, colsT = cx.sb("cols", [128, 8, 16], F32)
        gn, gnT = cx.sb("gn", [128, 2, 128], F32)
        bones, bonesT = cx.sb("bones", [128, 128], F32)
        hsel, hselT = cx.sb("hsel", [128, 2], F32)
        zwin, zwinT = cx.sb("zwin", [128, 256], BF16)
        Stt = [cx.sb(f"St{d}", [128, 64], F32) for d in range(2)]
        tmp_rot = [Rot([cx.sb(f"tmp{d}_{i}", [128, 64], F32) for i in range(2)]) for d in range(2)]
        t2_rot = [Rot([cx.sb(f"t2m{d}_{i}", [128, 2, 64], BF16) for i in range(2)]) for d in range(2)]
        sa_ps = [cx.ps(f"sa{d}", [128, 512]) for d in range(2)]
        sv_rot = [Rot([cx.ps(f"vbc{d}_{i}", [128, 512]) for i in range(2)]) for d in range(2)]
        yps1 = cx.ps("yps", [128, 512])
        pprep = Rot([cx.ps("pprep0", [128, 512])])

        S.dma("pool", lambda E: E.dma_start(out=esel[:, :, :], in_=es_d[:, :, :]), writes=[eselT])
        S.dma("sp", lambda E: E.dma_start(out=mucol[:, :, 0:2], in_=mucol_d[:, :, :]), writes=[mucolT])
        S.dma("sp", lambda E: E.dma_start(out=muht[:, 0:2, :], in_=muht_d[:, :, :]), writes=[muhtT])
        S.dma("sp", lambda E: E.dma_start(out=mutm[:, 0:2, :], in_=mutm_d[:, :, :]), writes=[mutmT])
        S.dma("sp", lambda E: E.dma_start(out=gn[:, :, :], in_=gn_d[:, :, :]), writes=[gnT])
        S.dma("sp", lambda E: [E.dma_start(out=cols[:, c, 0:1], in_=cols_d[c, :, :]) for c in range(8)], writes=[colsT], parts=8)
        S.dma("pool", lambda E: E.dma_start(out=w2b[:, :], in_=w2_d[:, :]), writes=[w2bT])
        S.dma("pool", lambda E: E.dma_start(out=a2b[:, :], in_=a2_d[:, :]), writes=[a2bT])
        S.dma("pool", lambda E: E.dma_start(out=g2b[:, :], in_=g2_d[:, :]), writes=[g2bT])
        S.op("dve", lambda E: E.memset(bones[:, :], 0.0), writes=[bonesT])
        S.op("dve", lambda E: E.memset(bones[0:64, 0:64], 1.0), writes=[bonesT])
        S.op("dve", lambda E: E.memset(bones[64:128, 64:128], 1.0), writes=[bonesT])
        S.op("dve", lambda E: E.memset(hsel[:, :], 0.0), writes=[hselT])
        S.op("dve", lambda E: E.memset(hsel[0:64, 0:1], 1.0), writes=[hselT])
        S.op("dve", lambda E: E.memset(hsel[64:128, 1:2], 1.0), writes=[hselT])
        S.op("dve", lambda E: E.memset(zwin[:, :], 0.0), writes=[zwinT])
        S.op("dve", lambda E: E.memset(zwin[:, 127:128], 1.0), writes=[zwinT])
        for (m, mT, sl) in ((mucol, mucolT, lambda i: mucol[:, :, i]), (muht, muhtT, lambda i: muht[:, i, :]), (mutm, mutmT, lambda i: mutm[:, i, :])):
            S.op("dve", lambda E, sl=sl: E.tensor_tensor(sl(2), sl(0), sl(1), ALU.add), reads=[mT], writes=[mT])
            S.op("dve", lambda E, sl=sl: E.tensor_scalar(sl(2), sl(2), -1.0, 1.0, ALU.mult, ALU.add), reads=[mT], writes=[mT])
        S.op("dve", lambda E: E.tensor_scalar(cols[:, 6, 0:1], cols[:, 6, 0:1], 0.5, None, ALU.mult), reads=[colsT], writes=[colsT])

        stg_rot = Rot([cx.sb(f"vstg{i}", [128, 3, 512], F32) for i in range(2)])
        for (which, dstT, src, mu_, muT_, nb_, w_) in (("ht", vhtT, vht_d, muht, muhtT, n64, 64), ("tm", vtmT, vtm_d, mutm, mutmT, n128, 128)):
            per = 512 // w_
            for b0 in range(0, nb_, per):
                bn = min(per, nb_ - b0)
                stg, stgT = stg_rot.next()
                S.dma("sp", lambda E, stg=stg, src=src, b0=b0, bn=bn, w_=w_: [E.dma_start(out=stg[:, i, 0:bn * w_].rearrange("p (b w) -> p b w", w=w_),
                                                                                            in_=src[:, i, b0:b0 + bn, :]) for i in range(3)], writes=[stgT], parts=3)
                if which == "tm":
                    acc = lambda b: vtm[:, b0 + b, :]
                    accT = vtmT
                else:
                    vt, vtT = vtmp_rot.next()
                    acc = lambda b, vt=vt, w_=w_: vt[:, b * w_:(b + 1) * w_]
                    accT = vtT
                L_ = bn * w_
                v3 = lambda ap_, w_=w_: ap_.rearrange("p (b w) -> p b w", w=w_)
                mub = lambda i, mu_=mu_, bn=bn, w_=w_: mu_[:, i, :].unsqueeze(1).broadcast_to([128, bn, w_])
                if which == "tm":
                    o3 = vtm[:, b0:b0 + bn, :]
                else:
                    o3 = v3(vt[:, 0:L_])
                S.op("dve", lambda E, o3=o3, stg=stg, L_=L_, mub=mub, v3=v3: E.tensor_tensor(o3, v3(stg[:, 0, 0:L_]), mub(2), ALU.mult), reads=[stgT, muT_], writes=[accT])
                for i in (1, 2):
                    S.op("dve", lambda E, stg=stg, L_=L_, mub=mub, v3=v3, i=i: E.tensor_tensor(v3(stg[:, i, 0:L_]), v3(stg[:, i, 0:L_]), mub(i - 1), ALU.mult),
                         reads=[stgT, muT_], writes=[stgT])
                    S.op("dve", lambda E, o3=o3, stg=stg, L_=L_, v3=v3, i=i: E.tensor_tensor(o3, o3, v3(stg[:, i, 0:L_]), ALU.add), reads=[stgT, accT], writes=[accT])
                if which == "ht":
                    v2, v2T = vtmp2_rot.next()
                    L = bn * w_
                    hi = vht[:, 0, b0:b0 + bn, :]
                    lo = vht[:, 1, b0:b0 + bn, :]
                    S.op("act", lambda E, hi=hi, vt=vt, L=L: E.copy(hi, vt[:, 0:L].rearrange("p (b w) -> p b w", w=64)), reads=[vtT], writes=[vhtT])
                    S.op("dve", lambda E, hi=hi, vt=vt, v2=v2, L=L: E.tensor_tensor(v2[:, 0:L].rearrange("p (b w) -> p b w", w=64), vt[:, 0:L].rearrange("p (b w) -> p b w", w=64), hi, ALU.subtract),
                         reads=[vtT, vhtT], writes=[v2T])
                    S.op("act", lambda E, lo=lo, v2=v2, L=L: E.copy(lo, v2[:, 0:L].rearrange("p (b w) -> p b w", w=64)), reads=[v2T], writes=[vhtT])

        def make_set(tag):
            names = ["raw", "r", "k", "wl", "al", "gl", "kx", "sq", "rn", "kk", "thb", "alb", "sgw", "a0", "a1", "w0", "w1", "nb0", "nb1", "ke0", "ke1", "t1"]
            d = {}
            for nm in names:
                if nm == "raw":
                    d[nm] = cx.sb(f"{tag}_{nm}", [128, 3, TB], F32)
                elif nm in ("thb", "alb"):
                    d[nm] = cx.sb(f"{tag}_{nm}", [128, TB], BF16)
                else:
                    d[nm] = cx.sb(f"{tag}_{nm}", [128, TB], F32)
            return d

        def shift(P_, tile_idx, out_name, c0):
            raw, rawT = P_["raw"]
            o, oT = P_[out_name]
            S.dma("sp", lambda E: [E.dma_start(out=raw[:, i, :], in_=zr_d[i, tile_idx * 128:(tile_idx + 1) * 128, c0:c0 + TB]) for i in range(3)], writes=[rawT], parts=3)
            S.op("act", lambda E: E.activation(o[:, :], raw[:, 0, :], AF.Copy, scale=mucol[:, tile_idx, 2:3]), reads=[rawT, mucolT], writes=[oT])
            for i in (1, 2):
                S.op("dve", lambda E, i=i: E.scalar_tensor_tensor(out=o[:, :], in0=raw[:, i, :], scalar=mucol[:, tile_idx, i - 1:i], in1=o[:, :], op0=ALU.mult, op1=ALU.add),
                     reads=[rawT, mucolT, oT], writes=[oT])

        def prep(P_, blk, dirs, want_gl=False):
            c0 = blk * TB
            shift(P_, 0, "r", c0)
            shift(P_, 1, "k", c0)
            shift(P_, 3, "wl", c0)
            shift(P_, 4, "al", c0)
            if want_gl:
                shift(P_, 5, "gl", c0)
            r, rT = P_["r"]; k, kT = P_["k"]; wl, wlT = P_["wl"]; al, alT = P_["al"]
            kx, kxT = P_["kx"]; sq, sqT = P_["sq"]; rn, rnT = P_["rn"]; kk, kkT = P_["kk"]
            thb, thbT = P_["thb"]; alb, albT = P_["alb"]; sgw, sgwT = P_["sgw"]; t1, t1T = P_["t1"]
            S.op("dve", lambda E: E.tensor_scalar(kx[:, :], k[:, :], cols[:, 4, 0:1], None, ALU.mult), reads=[kT, colsT], writes=[kxT])
            S.op("act", lambda E: E.activation(sq[:, :], kx[:, :], AF.Square), reads=[kxT], writes=[sqT])
            pp, ppT = pprep.next()
            S.op("pe", lambda E: E.matmul(pp[:, 0:TB], bones[:, :], sq[:, :], start=True, stop=True), reads=[bonesT, sqT], writes=[ppT])
            S.op("dve", lambda E: E.tensor_scalar(rn[:, :], pp[:, 0:TB], 1e-12, None, ALU.add), reads=[ppT], writes=[rnT])
            S.op("act", lambda E: E.activation(rn[:, :], rn[:, :], AF.Sqrt), reads=[rnT], writes=[rnT])
            S.op("dve", lambda E: E.reciprocal(rn[:, :], rn[:, :]), reads=[rnT], writes=[rnT])
            S.op("dve", lambda E: E.tensor_tensor(kk[:, :], kx[:, :], rn[:, :], ALU.mult), reads=[kxT, rnT], writes=[kkT])
            S.op("act", lambda E: E.activation(thb[:, :], wl[:, :], AF.Tanh), reads=[wlT], writes=[thbT])
            S.op("act", lambda E: E.copy(alb[:, :], al[:, :]), reads=[alT], writes=[albT])
            for d in dirs:
                a_, aT_ = P_[f"a{d}"]; w_, wT_ = P_[f"w{d}"]; nb_, nbT_ = P_[f"nb{d}"]; ke_, keT_ = P_[f"ke{d}"]
                pw, pwT = pprep.next()
                S.op("pe", lambda E, d=d, pw=pw: E.matmul(pw[:, 0:TB], w2b[64 * d:64 * d + 64, :], thb[64 * d:64 * d + 64, :], start=True, stop=True), reads=[w2bT, thbT], writes=[pwT])
                S.op("act", lambda E, d=d, pw=pw: E.activation(sgw[:, :], pw[:, 0:TB], AF.Sigmoid, bias=cols[:, d, 0:1]), reads=[pwT, colsT], writes=[sgwT])
                S.op("act", lambda E, w_=w_: E.activation(w_[:, :], sgw[:, :], AF.Exp, scale=-R_DECAY_SCALE), reads=[sgwT], writes=[wT_])
                pa, paT = pprep.next()
                S.op("pe", lambda E, d=d, pa=pa: E.matmul(pa[:, 0:TB], a2b[64 * d:64 * d + 64, :], alb[64 * d:64 * d + 64, :], start=True, stop=True), reads=[a2bT, albT], writes=[paT])
                S.op("act", lambda E, d=d, pa=pa, a_=a_: E.activation(a_[:, :], pa[:, 0:TB], AF.Sigmoid, bias=cols[:, 2 + d, 0:1]), reads=[paT, colsT], writes=[aT_])
                S.op("dve", lambda E, a_=a_: E.tensor_scalar(t1[:, :], a_[:, :], cols[:, 5, 0:1], cols[:, 5, 0:1], ALU.mult, ALU.subtract), reads=[aT_, colsT], writes=[t1T])
                S.op("dve", lambda E, ke_=ke_: E.scalar_tensor_tensor(out=ke_[:, :], in0=t1[:, :], scalar=1.0, in1=k[:, :], op0=ALU.add, op1=ALU.mult), reads=[t1T, kT], writes=[keT_])
                S.op("dve", lambda E, nb_=nb_, a_=a_: E.scalar_tensor_tensor(out=nb_[:, :], in0=kk[:, :], scalar=-1.0, in1=a_[:, :], op0=ALU.mult, op1=ALU.mult), reads=[kkT, aT_], writes=[nbT_])

        sets = [make_set("pf"), make_set("pb")]
        for d in range(2):
            S.op("dve", lambda E, d=d: E.memset(Stt[d][0][:, :], 0.0), writes=[Stt[d][1]])
            for (t2, t2T) in t2_rot[d].items:
                S.op("dve", lambda E, t2=t2: E.memset(t2[:, :, :], 0.0), writes=[t2T])

        St2 = [[Stt[d], cx.sb(f"StB{d}", [128, 64], F32)] for d in range(2)]
        kkb_rot = [Rot([cx.sb(f"kkb{d}_{i}", [128, 128], F32) for i in range(5)]) for d in range(2)]
        zwin2, zwin2T = cx.sb("zwin2", [128, 2, 256], BF16)
        S.op("dve", lambda E: E.memset(zwin2[:, :, :], 0.0), writes=[zwin2T])
        S.op("dve", lambda E: E.memset(zwin2[0:64, 0, 127:128], 1.0), writes=[zwin2T])
        S.op("dve", lambda E: E.memset(zwin2[64:128, 1, 127:128], 1.0), writes=[zwin2T])
        for d in range(2):
            S.op("dve", lambda E, d=d: E.memset(St2[d][1][0][:, :], 0.0), writes=[St2[d][1][1]])
        steps = []
        for n in range(nblk):
            for i in range(TB):
                steps.append((n, i))

        def tinfo(gi, d):
            n, i = steps[gi]
            if d == 0:
                return n * TB + i, i
            return (nblk - 1 - n) * TB + TB - 1 - i, TB - 1 - i

        sv_cur = [None, None]

        def emit_vbc(gi):
            for d in range(2):
                t, c = tinfo(gi, d)
                sv, svT = sv_rot[d].next()
                sv_cur[d] = (sv, svT)
                S.op("pe", lambda E, sv=sv, t=t: E.matmul(sv[:, 0:64], esel[:, t % 64, :], vht[:, 0, t // 64, :], start=True, stop=False), reads=[eselT, vhtT], writes=[svT])
                S.op("pe", lambda E, sv=sv, t=t: E.matmul(sv[:, 0:64], esel[:, t % 64, :], vht[:, 1, t // 64, :], start=False, stop=True), reads=[eselT, vhtT], writes=[svT])

        pending = []

        def flush_pending():
            for (d, t, c, new, newT, P_) in pending:
                r, rT = P_["r"]
                t2, t2T = t2_rot[d].next()
                S.op("act", lambda E, t2=t2, new=new, r=r, c=c: E.activation(t2[:, 0, :], new[:, :], AF.Copy, scale=r[:, c:c + 1]), reads=[newT, rT], writes=[t2T])
                tl = t % 128
                first = (tl == 0) if d == 0 else (tl == 127)
                last = (tl == 127) if d == 0 else (tl == 0)
                yp, ypT = yps1
                yo = 128 * d
                for h in range(2):
                    S.op("pe", lambda E, yp=yp, t2=t2, tl=tl, first=first, last=last, h=h, yo=yo, d=d: E.matmul(yp[:, yo + 64 * h:yo + 64 * h + 64], zwin2[:, h, 127 - tl:255 - tl], t2[:, 0, :],
                                                                                                     start=(first and h == 0 and d == 0), stop=last, skip_group_check=True), reads=[zwin2T, t2T], writes=[ypT])
                if last:
                    ys, ysT = ysb1
                    ti = t // 128
                    if (d == 0) == (ti < (n128 + 1) // 2):
                        S.op("act", lambda E, ys=ys, yp=yp, ti=ti, yo=yo: E.copy(ys[:, ti, :], yp[:, yo:yo + 128]), reads=[ypT], writes=[ysT])
                    else:
                        S.op("dve", lambda E, ys=ys, yp=yp, ti=ti, yo=yo: E.tensor_tensor(ys[:, ti, :], ys[:, ti, :], yp[:, yo:yo + 128], ALU.add), reads=[ypT, ysT], writes=[ysT])
            pending.clear()

        kkb_q = [{}, {}]

        def emit_kkb(gi):
            for d in range(2):
                t, c = tinfo(gi, d)
                tmp, tmpT = kkb_rot[d].next()
                kk, kkT = sets[d]["kk"]
                S.op("act", lambda E, tmp=tmp, kk=kk, c=c: E.activation(tmp[:, :], bones[:, :], AF.Copy, scale=kk[:, c:c + 1]), reads=[bonesT, kkT], writes=[tmpT])
                kkb_q[d][gi] = (tmp, tmpT)

        LOOK = 2
        for gi, (n, i) in enumerate(steps):
            if i == 0:
                flush_pending()
                prep(sets[0], n, (0,))
                prep(sets[1], nblk - 1 - n, (1,))
                for la in range(min(LOOK, TB)):
                    emit_kkb(gi + la)
            if gi == 0:
                emit_vbc(0)
            par = gi % 2
            svs = list(sv_cur)
            info = []
            for d in range(2):
                t, c = tinfo(gi, d)
                cur, curT = St2[d][par]
                tmp, tmpT = kkb_q[d].pop(gi)
                info.append((t, c, cur, curT, tmp, tmpT))
            for d in range(2):
                t, c, cur, curT, tmp, tmpT = info[d]
                sa, saT = sa_ps[d]
                S.op("pe", lambda E, sa=sa, tmp=tmp, cur=cur: E.matmul(sa[:, 0:64], tmp[:, :], cur[:, :], start=True, stop=True), reads=[tmpT, curT], writes=[saT])
            flush_pending()
            if i + LOOK < TB:
                emit_kkb(gi + LOOK)
            if gi + 1 < len(steps):
                emit_vbc(gi + 1)
            for phase in range(3):
                for d in range(2):
                    t, c, cur, curT, tmp, tmpT = info[d]
                    new, newT = St2[d][1 - par]
                    sv, svT = svs[d]
                    w_, wT_ = sets[d][f"w{d}"]; ke_, keT_ = sets[d][f"ke{d}"]; nb_, nbT_ = sets[d][f"nb{d}"]
                    if phase == 0:
                        S.op("dve", lambda E, new=new, cur=cur, w_=w_, c=c: E.tensor_scalar(new[:, :], cur[:, :], w_[:, c:c + 1], None, ALU.mult), reads=[curT, wT_], writes=[newT])
                    elif phase == 1:
                        S.op("dve", lambda E, new=new, sv=sv, ke_=ke_, c=c: E.scalar_tensor_tensor(out=new[:, :], in0=sv[:, 0:64], scalar=ke_[:, c:c + 1], in1=new[:, :], op0=ALU.mult, op1=ALU.add),
                             reads=[svT, keT_, newT], writes=[newT])
                    else:
                        sa, saT = sa_ps[d]
                        S.op("dve", lambda E, new=new, sa=sa, nb_=nb_, c=c: E.scalar_tensor_tensor(out=new[:, :], in0=sa[:, 0:64], scalar=nb_[:, c:c + 1], in1=new[:, :], op0=ALU.mult, op1=ALU.add),
                             reads=[saT, nbT_, newT], writes=[newT])
                        pending.append((d, t, c, new, newT, sets[d]))
        flush_pending()

        es = sets[0]
        ept = {nm: Rot([cx.sb(f"ep_{nm}{i}", [128, 128], dt) for i in range(2)]) for nm, dt in
               (("y", F32), ("yc", F32), ("junk", F32), ("g", F32), ("PT", F32), ("sglb", BF16), ("o", F32))}
        sm_rot = Rot([cx.sb(f"ep_sm{i}", [128, 16], F32) for i in range(2)])
        outs = []
        for blk in range(nblk):
            prep(es, blk, (0, 1), want_gl=True)
            r, rT = es["r"]; gl, glT = es["gl"]; ke0, ke0T = es["ke0"]; ke1, ke1T = es["ke1"]; t1, t1T = es["t1"]
            S.op("dve", lambda E: E.tensor_tensor(t1[:, :], ke0[:, :], ke1[:, :], ALU.add), reads=[ke0T, ke1T], writes=[t1T])
            S.op("dve", lambda E: E.scalar_tensor_tensor(out=t1[:, :], in0=t1[:, :], scalar=cols[:, 6, 0:1], in1=r[:, :], op0=ALU.mult, op1=ALU.mult), reads=[t1T, colsT, rT], writes=[t1T])
            for sub in range(TB // 128):
                ti = blk * (TB // 128) + sub
                cs = slice(sub * 128, (sub + 1) * 128)
                y, yT = ept["y"].next(); yc, ycT = ept["yc"].next(); junk, junkT = ept["junk"].next(); g, gT_ = ept["g"].next()
                sglb, sglbT = ept["sglb"].next(); ob, obT = ept["o"].next(); sm, smT = sm_rot.next()
                S.op("act", lambda E, sglb=sglb, cs=cs: E.activation(sglb[:, :], gl[:, cs], AF.Sigmoid), reads=[glT], writes=[sglbT])
                pg, pgT = pprep.next()
                S.op("pe", lambda E, pg=pg, sglb=sglb: E.matmul(pg[:, 0:128], sglb[:, :], g2b[:, :], start=True, stop=True), reads=[sglbT, g2bT], writes=[pgT])
                S.op("act", lambda E, g=g, pg=pg: E.copy(g[:, :], pg[:, 0:128]), reads=[pgT], writes=[gT_])
                pb_, pbT_ = pprep.next()
                S.op("pe", lambda E, pb_=pb_, cs=cs: E.matmul(pb_[:, 0:2], t1[:, cs], hsel[:, :], start=True, stop=True), reads=[t1T, hselT], writes=[pbT_])
                S.op("dve", lambda E, sm=sm, pb_=pb_: E.tensor_copy(sm[:, 8:10], pb_[:, 0:2]), reads=[pbT_], writes=[smT])
                S.op("dve", lambda E, y=y, ti=ti: E.tensor_copy(y[:, :], ysb1[0][:, ti, :]), reads=[ysb1[1]], writes=[yT])
                for h in range(2):
                    hs = slice(64 * h, 64 * h + 64)
                    S.op("dve", lambda E, sm=sm, y=y, hs=hs, h=h: E.reduce_sum(sm[:, h:h + 1], y[:, hs], AX.X), reads=[yT], writes=[smT])
                    S.op("dve", lambda E, sm=sm, h=h: E.tensor_scalar(sm[:, h:h + 1], sm[:, h:h + 1], 1.0 / 64, None, ALU.mult), reads=[smT], writes=[smT])
                    S.op("dve", lambda E, yc=yc, y=y, sm=sm, hs=hs, h=h: E.tensor_scalar(yc[:, hs], y[:, hs], sm[:, h:h + 1], None, ALU.subtract), reads=[yT, smT], writes=[ycT])
                    S.op("act", lambda E, junk=junk, yc=yc, sm=sm, hs=hs, h=h: E.activation(junk[:, hs], yc[:, hs], AF.Square, accum_out=sm[:, 2 + h:3 + h]), reads=[ycT], writes=[junkT, smT])
                    S.op("dve", lambda E, sm=sm, h=h: E.tensor_scalar(sm[:, 2 + h:3 + h], sm[:, 2 + h:3 + h], 1.0 / 64, R_GN_EPS, ALU.mult, ALU.add), reads=[smT], writes=[smT])
                    S.op("act", lambda E, sm=sm, h=h: E.activation(sm[:, 2 + h:3 + h], sm[:, 2 + h:3 + h], AF.Sqrt), reads=[smT], writes=[smT])
                    S.op("dve", lambda E, sm=sm, h=h: E.reciprocal(sm[:, 4 + h:5 + h], sm[:, 2 + h:3 + h]), reads=[smT], writes=[smT])
                    S.op("dve", lambda E, yc=yc, sm=sm, hs=hs, h=h: E.scalar_tensor_tensor(out=yc[:, hs], in0=yc[:, hs], scalar=sm[:, 4 + h:5 + h], in1=gn[:, 0, hs], op0=ALU.mult, op1=ALU.mult),
                         reads=[ycT, smT, gnT], writes=[ycT])
                    S.op("dve", lambda E, yc=yc, hs=hs: E.tensor_tensor(yc[:, hs], yc[:, hs], gn[:, 1, hs], ALU.add), reads=[ycT, gnT], writes=[ycT])
                    S.op("dve", lambda E, yc=yc, sm=sm, hs=hs, h=h, ti=ti: E.scalar_tensor_tensor(out=yc[:, hs], in0=vtm[:, ti, hs], scalar=sm[:, 8 + h:9 + h], in1=yc[:, hs], op0=ALU.mult, op1=ALU.add),
                         reads=[vtmT, smT, ycT], writes=[ycT])
                S.op("dve", lambda E, ob=ob, yc=yc, g=g: E.tensor_tensor(ob[:, :], yc[:, :], g[:, :], ALU.mult), reads=[ycT, gT_], writes=[obT])
                S.dma("sp", lambda E, ob=ob, ti=ti: E.dma_start(out=o_d[ti * 128:(ti + 1) * 128, :], in_=ob[:, :]), reads=[obT])
                if obT not in outs:
                    outs.append(obT)
        S.final_wait("sp", outs)
        S.emit(st)
        nc._stats = S.stats
    return nc


def rwkv_consts():
    es = np.zeros((128, 64, 128), np.float32)
    for h in range(2):
        for t in range(64):
            es[h * 64 + t, t, h * 64:(h + 1) * 64] = 1.0
    return es


def rwkv_inputs(zT_b, j, r_mu, r_w0, r_w2, r_a0, r_a2, r_g2, r_kk, r_ka, r_rk, r_gn_g, r_gn_b):
    S_len = zT_b.shape[1]
    base = 1536 + 2560 + 512
    zr = zT_b[base:base + 1920]
    my = np.arange(j * 128, (j + 1) * 128)
    rows = np.concatenate([my, 512 + my, 1024 + my, np.arange(1536, 1920)])
    cur = zr[rows]
    prev = np.zeros_like(cur); prev[:, 1:] = cur[:, :-1]
    nxt = np.zeros_like(cur); nxt[:, :-1] = cur[:, 1:]
    zr3T = np.ascontiguousarray(np.stack([cur, prev, nxt]))
    v3 = zr3T[:, 256:384, :]
    n64, n128 = S_len // 64, S_len // 128
    v3ht = np.ascontiguousarray(v3.reshape(3, 2, 64, n64, 64).transpose(1, 4, 0, 3, 2).reshape(128, 3, n64, 64))
    v3tm = np.ascontiguousarray(v3.reshape(3, 128, n128, 128).transpose(3, 0, 2, 1))
    mu = r_mu[:, rows]
    mucol = np.ascontiguousarray(mu.reshape(2, 6, 128).transpose(2, 1, 0))
    muv = mu[:, 256:384]
    muht = np.ascontiguousarray(np.repeat(muv.reshape(2, 2, 64).transpose(1, 0, 2), 64, axis=0))
    mutm = np.ascontiguousarray(np.broadcast_to(muv[None], (128, 2, 128)))
    w2t = np.ascontiguousarray(np.concatenate([r_w2[0][:, my], r_w2[1][:, my]], axis=0))
    a2t = np.ascontiguousarray(np.concatenate([r_a2[0][:, my], r_a2[1][:, my]], axis=0))
    g2m = np.ascontiguousarray(r_g2[:, my])
    cols = np.ascontiguousarray(np.stack([r_w0[0][my], r_w0[1][my], r_a0[0][my], r_a0[1][my], r_kk[my], r_ka[my], r_rk[my], np.zeros(128, np.float32)], axis=0)[:, :, None])
    gnrow = np.ascontiguousarray(np.broadcast_to(np.stack([r_gn_g[my], r_gn_b[my]])[None], (128, 2, 128)))
    return dict(zr3T=zr3T, v3ht=v3ht, v3tm=v3tm, mucol=mucol, muht=muht, mutm=mutm, w2t=w2t, a2t=a2t, g2m=g2m,
                cols=cols.astype(np.float32), gnrow=gnrow, esel=rwkv_consts())


POOL_WINDOWS = (2, 4, 8, 16)


def build_pool(ntok, NB=512):
    nc = bass.Bass("TRN2", target_bir_lowering=False)
    z_d = nc.dram_tensor("zc", [4, 128, ntok + 16], F32, kind="ExternalInput").ap()
    ci_d = nc.dram_tensor("cinv", [128, 4, ntok], F32, kind="ExternalInput").ap()
    cw_d = nc.dram_tensor("cw", [128, 4, 128], F32, kind="ExternalInput").ap()
    cc_d = nc.dram_tensor("ccol", [128, 4, 2], F32, kind="ExternalInput").ap()
    o_d = nc.dram_tensor("oT", [4, 128, ntok], F32, kind="ExternalOutput").ap()
    with ExitStack() as st:
        cx = Ctx(nc, st)
        S = cx.S
        cw, cwT = cx.sb("cw", [128, 4, 128], BF16)
        cc, ccT = cx.sb("cc", [128, 4, 2], F32)
        x_rot = Rot([cx.sb(f"x{i}", [128, NB + 16], F32) for i in range(2)])
        s_rot = Rot([cx.sb(f"s{i}", [128, NB + 16], F32) for i in range(3)])
        ci_rot = Rot([cx.sb(f"ci{i}", [128, NB], F32) for i in range(2)])
        d_rot = Rot([cx.sb(f"d{i}", [128, NB], BF16) for i in range(2)])
        ob_rot = Rot([cx.sb(f"ob{i}", [128, NB], F32) for i in range(2)])
        pp = Rot([cx.ps(f"pp{i}", [128, 512]) for i in range(2)])
        S.dma("pool", lambda E: E.dma_start(out=cw[:, :, :], in_=cw_d[:, :, :]), writes=[cwT])
        S.dma("sp", lambda E: E.dma_start(out=cc[:, :, :], in_=cc_d[:, :, :]), writes=[ccT])
        outs = []
        for g, w in enumerate(POOL_WINDOWS):
            for t0 in range(0, ntok, NB):
                x, xT = x_rot.next()
                ci, ciT = ci_rot.next()
                S.dma("sp", lambda E, x=x, g=g, t0=t0: E.dma_start(out=x[:, :], in_=z_d[g, :, t0:t0 + NB + 16]), writes=[xT])
                S.dma("sp", lambda E, ci=ci, g=g, t0=t0: E.dma_start(out=ci[:, :], in_=ci_d[:, g, t0:t0 + NB]), writes=[ciT])
                cur, curT = x, xT
                L = NB + 16
                step = 1
                while step < w:
                    nx, nxT = s_rot.next()
                    L2 = L - step
                    S.op("dve", lambda E, nx=nx, cur=cur, L2=L2, step=step: E.tensor_tensor(nx[:, 0:L2], cur[:, 0:L2], cur[:, step:step + L2], ALU.add),
                         reads=[curT], writes=[nxT])
                    cur, curT, L = nx, nxT, L2
                    step *= 2
                off = 8 - w // 2
                m, mT = s_rot.next()
                S.op("dve", lambda E, m=m, cur=cur, ci=ci, off=off: E.tensor_tensor(m[:, 0:NB], cur[:, off:off + NB], ci[:, :], ALU.mult), reads=[curT, ciT], writes=[mT])
                d, dT = d_rot.next()
                S.op("dve", lambda E, d=d, m=m, x=x: E.tensor_tensor(d[:, :], m[:, 0:NB], x[:, 8:8 + NB], ALU.subtract), reads=[mT, xT], writes=[dT])
                ps, psT = pp.next()
                S.op("pe", lambda E, ps=ps, d=d, g=g: E.matmul(ps[:, 0:NB], cw[:, g, :], d[:, :], start=True, stop=True), reads=[cwT, dT], writes=[psT])
                ob, obT = ob_rot.next()
                S.op("dve", lambda E, ob=ob, ps=ps, g=g: E.tensor_scalar(ob[:, :], ps[:, 0:NB], cc[:, g, 0:1], cc[:, g, 1:2], ALU.add, ALU.mult), reads=[psT, ccT], writes=[obT])
                S.dma("sp", lambda E, ob=ob, g=g, t0=t0: E.dma_start(out=o_d[g, :, t0:t0 + NB], in_=ob[:, :]), reads=[obT])
                if obT not in outs:
                    outs.append(obT)
        S.final_wait("sp", outs)
        S.emit(st)
        nc._stats = S.stats
    return nc


def pool_inputs(zT_b, q, ntok, S_len, c_w, c_b, c_scale):
    base = 1536 + 2560
    zc = zT_b[base:base + 512]
    pad = np.zeros((512, S_len + 16), np.float32)
    pad[:, 8:8 + S_len] = zc
    t0 = q * ntok
    zcp = np.ascontiguousarray(pad[:, t0:t0 + ntok + 16].reshape(4, 128, ntok + 16))
    t = np.arange(t0, t0 + ntok)
    cinv = np.zeros((4, ntok), np.float32)
    for g, w in enumerate(POOL_WINDOWS):
        lo = np.clip(t - w // 2, 0, S_len - 1)
        hi = np.clip(t + (w - w // 2 - 1), 0, S_len - 1)
        cinv[g] = 1.0 / (hi - lo + 1).astype(np.float32)
    cinv = np.ascontiguousarray(np.broadcast_to(cinv[None], (128, 4, ntok)))
    cw = np.ascontiguousarray(c_w.transpose(1, 0, 2))
    ccol = np.ascontiguousarray(np.stack([c_b.reshape(4, 128).T, c_scale.reshape(4, 128).T], axis=2))
    return dict(zc=zcp, cinv=cinv, cw=cw, ccol=ccol)


_PROGS = {}


def _prog(key, fn):
    if key not in _PROGS:
        _PROGS[key] = fn()
    return _PROGS[key]


def _run(nc, in_maps):
    res = run_bass_kernel_spmd(nc, in_maps, core_ids=list(range(NCORES)))
    return res.results


def _gl(g):
    return np.ascontiguousarray(np.asarray(g, np.float32).reshape(16, 128).T)


def kernel(x, p, mix_norm_g, w_in, w_out, rel_bias, a_qnorm, a_knorm, a_lambda, a_subln,
           h_lb_logits, h_onorm, c_w, c_b, c_scale, r_mu, r_w0, r_w2, r_a0, r_a2, r_g2,
           r_kk, r_ka, r_rk, r_gn_g, r_gn_b, mlp_norm_g, w_up, w_down, ple_norm_g, w_ple, w_ple_gate):
    f = lambda a: np.asarray(a, np.float32)
    x = f(x); p = f(p)
    B, S_len, D = x.shape
    depth = w_in.shape[0]
    ntok = B * S_len // NCORES
    qpb = S_len // ntok
    hT = np.ascontiguousarray(x.reshape(B * S_len, D).T)
    for li in range(depth):
        nc1 = _prog(("proj", ntok), lambda: build_proj(ntok, N_IN))
        g1 = _gl(mix_norm_g[li])
        wi = np.ascontiguousarray(f(w_in[li]))
        res = _run(nc1, [dict(hT=np.ascontiguousarray(hT[:, c * ntok:(c + 1) * ntok]), w=wi, g=g1) for c in range(NCORES)])
        zT = np.concatenate([res[c]["zT"] for c in range(NCORES)], axis=1)
        zTb = [zT[:, b * S_len:(b + 1) * S_len] for b in range(B)]
        mixT = np.empty((D, B * S_len), np.float32)
        nca = _prog(("attn", S_len, li), lambda: build_attn(S_len, li))
        res = _run(nca, [attn_inputs(zTb[c // 4], c % 4, f(rel_bias), f(a_qnorm[li]), f(a_knorm[li]), f(a_lambda[li]), f(a_subln[li])) for c in range(NCORES)])
        for c in range(NCORES):
            b, j = c // 4, c % 4
            mixT[j * 128:(j + 1) * 128, b * S_len:(b + 1) * S_len] = res[c]["o"].T
        ncb = _prog(("hgrn", S_len, min(li, 1)), lambda: build_hgrn(S_len, li))
        lbl = f(h_lb_logits)[[0, li]] if li > 0 else f(h_lb_logits)[[0, 0]]
        res = _run(ncb, [hgrn_inputs(zTb[c // 4], c % 4, lbl, f(h_onorm[li])) for c in range(NCORES)])
        for c in range(NCORES):
            b, j = c // 4, c % 4
            mixT[512 + j * 128:512 + (j + 1) * 128, b * S_len:(b + 1) * S_len] = res[c]["oT"]
        ncc = _prog(("pool", ntok), lambda: build_pool(ntok))
        res = _run(ncc, [pool_inputs(zTb[c // qpb], c % qpb, ntok, S_len, f(c_w[li]), f(c_b[li]), f(c_scale[li])) for c in range(NCORES)])
        for c in range(NCORES):
            mixT[1024:1536, c * ntok:(c + 1) * ntok] = res[c]["oT"].reshape(512, ntok)
        ncd = _prog(("rwkv", S_len), lambda: build_rwkv(S_len))
        res = _run(ncd, [rwkv_inputs(zTb[c // 4], c % 4, f(r_mu[li]), f(r_w0[li]), f(r_w2[li]), f(r_a0[li]), f(r_a2[li]), f(r_g2[li]),
                                     f(r_kk[li]), f(r_ka[li]), f(r_rk[li]), f(r_gn_g[li]), f(r_gn_b[li])) for c in range(NCORES)])
        for c in range(NCORES):
            b, j = c // 4, c % 4
            mixT[1536 + j * 128:1536 + (j + 1) * 128, b * S_len:(b + 1) * S_len] = res[c]["o"].T
        nc3 = _prog(("ffn", ntok), lambda: build_ffn(ntok))
        pT = np.ascontiguousarray(p[li].reshape(B * S_len, PLE_DIM).T)
        wts = dict(w_out=np.ascontiguousarray(f(w_out[li])), w_up=np.ascontiguousarray(f(w_up[li])), w_down=np.ascontiguousarray(f(w_down[li])),
                   w_gate=np.ascontiguousarray(f(w_ple_gate[li])), w_ple=np.ascontiguousarray(f(w_ple[li])),
                   g_mlp=_gl(mlp_norm_g[li]), g_ple=_gl(ple_norm_g[li]))
        res = _run(nc3, [dict(hT=np.ascontiguousarray(hT[:, c * ntok:(c + 1) * ntok]), mixT=np.ascontiguousarray(mixT[:, c * ntok:(c + 1) * ntok]),
                              pT=np.ascontiguousarray(pT[:, c * ntok:(c + 1) * ntok]), **wts) for c in range(NCORES)])
        hT = np.concatenate([res[c]["oT"] for c in range(NCORES)], axis=1)
    return np.ascontiguousarray(hT.T).reshape(B, S_len, D).astype(np.float32)
```
